# Optimizing a Trainium2 kernel written in Bass

```python
import jax, jax.numpy as jnp
from jax import lax
import numpy as np

D_MODEL = 1024
BATCH = 8
SEQ = 4096
DEPTH = 1

HEAD_DIM = 64
ROT_DIM = HEAD_DIM // 4
ROPE_THETA = 500000.0
ATTN_SCALE = HEAD_DIM ** -0.5
QBLOCK = 128

A_HEADS = 12
A_WIDTH = A_HEADS * HEAD_DIM
IDX_HEADS = 8
IDX_DIM = 64
IDX_SCALE = (IDX_HEADS ** -0.5) * (IDX_DIM ** -0.5)
TOPK_MAX = 256

B_GROUPS = ((128, 1), (512, 4), (2048, 16))
B_HEADS_PER_GROUP = 4
B_HEADS = B_HEADS_PER_GROUP * len(B_GROUPS)
B_WIDTH = B_HEADS * HEAD_DIM
B_OUT_WIDTH = B_HEADS_PER_GROUP * HEAD_DIM

FFN_HIDDEN = -(-8 * D_MODEL // (3 * 256)) * 256

DN_ALPHA = (2 * DEPTH) ** 0.25
DN_BETA = (8 * DEPTH) ** -0.25
LN_EPS = 1e-5
NEG = -1e30

IN_SPLITS = (A_WIDTH, A_WIDTH, A_WIDTH, B_WIDTH, B_WIDTH, B_WIDTH,
             IDX_HEADS * IDX_DIM, IDX_DIM, IDX_HEADS, 2 * D_MODEL)
N_IN = sum(IN_SPLITS)

kernel_name = 'hybrid_dsa_dilated_gated_deepnorm'


def layer_norm(x, g, b):
    xf = x.astype(jnp.float32)
    mu = xf.mean(-1, keepdims=True)
    var = jnp.square(xf - mu).mean(-1, keepdims=True)
    y = (xf - mu) * lax.rsqrt(var + LN_EPS)
    return (y * g.astype(jnp.float32) + b.astype(jnp.float32)).astype(x.dtype)


def rope_tables(positions, dtype):
    inv_freq = ROPE_THETA ** (-jnp.arange(0, ROT_DIM, 2, dtype=jnp.float32) / ROT_DIM)
    ang = positions.astype(jnp.float32)[..., None] * inv_freq
    return (jnp.cos(ang).astype(dtype)[:, :, None, :],
            jnp.sin(ang).astype(dtype)[:, :, None, :])


def partial_rope(x, cos, sin):
    r = cos.shape[-1]
    x1, x2, rest = x[..., :r], x[..., r:2 * r], x[..., 2 * r:]
    return jnp.concatenate([x1 * cos - x2 * sin, x2 * cos + x1 * sin, rest], axis=-1)


def to_blocks(a):
    b, s = a.shape[:2]
    a = a.reshape((b, s // QBLOCK, QBLOCK) + a.shape[2:])
    return jnp.moveaxis(a, 1, 0)


def from_blocks(a):
    a = jnp.moveaxis(a, 0, 1)
    return a.reshape((a.shape[0], -1) + a.shape[3:])


def dsa_attention(q, k, v, q_idx, k_idx, w_idx):
    seq = q.shape[1]
    k_top = min(TOPK_MAX, seq // 4)
    nblk = seq // QBLOCK
    key_pos = jnp.arange(seq)
    k_idx_f = k_idx.astype(jnp.float32)

    def block(args):
        qb, qib, wb, start = args
        t = start + jnp.arange(QBLOCK)
        s = jnp.einsum('bqhd,bsd->bqhs', qib.astype(jnp.float32), k_idx_f)
        scores = jnp.einsum('bqh,bqhs->bqs', wb.astype(jnp.float32) * IDX_SCALE, jax.nn.relu(s))
        causal = key_pos[None, :] <= t[:, None]
        scores = jnp.where(causal[None], scores, NEG)
        _, sel = lax.top_k(scores, k_top)
        valid = sel <= t[None, :, None]
        gather = jax.vmap(lambda kv, ii: kv[ii])
        k_sel = gather(k, sel)
        v_sel = gather(v, sel)
        logits = jnp.einsum('bqhd,bqkhd->bhqk', qb, k_sel).astype(jnp.float32) * ATTN_SCALE
        logits = jnp.where(valid[:, None], logits, NEG)
        p = jax.nn.softmax(logits, axis=-1).astype(v.dtype)
        return jnp.einsum('bhqk,bqkhd->bqhd', p, v_sel)

    starts = jnp.arange(nblk, dtype=jnp.int32) * QBLOCK
    out = lax.map(block, (to_blocks(q), to_blocks(q_idx), to_blocks(w_idx), starts))
    return from_blocks(out)


def dilated_attention(q, k, v):
    seq = q.shape[1]
    nblk = seq // QBLOCK
    hs = [slice(g * B_HEADS_PER_GROUP, (g + 1) * B_HEADS_PER_GROUP) for g in range(len(B_GROUPS))]
    k_groups = [k[:, :, h] for h in hs]
    v_groups = [v[:, :, h] for h in hs]

    def block(args):
        qb, start = args
        t = start + jnp.arange(QBLOCK)
        outs, lses = [], []
        for g, (window, dil) in enumerate(B_GROUPS):
            offs = dil * jnp.arange(window // dil + 1)
            pos = t[:, None] - offs[None, :]
            valid = pos >= 0
            pos = jnp.maximum(pos, 0)
            k_g = k_groups[g][:, pos]
            v_g = v_groups[g][:, pos]
            logits = jnp.einsum('bqhd,bqjhd->bhqj', qb[:, :, hs[g]], k_g).astype(jnp.float32) * ATTN_SCALE
            logits = jnp.where(valid[None, None], logits, NEG)
            lse = jax.nn.logsumexp(logits, axis=-1)
            p = jnp.exp(logits - lse[..., None]).astype(v.dtype)
            outs.append(jnp.einsum('bhqj,bqjhd->bqhd', p, v_g))
            lses.append(lse)
        alpha = jax.nn.softmax(jnp.stack(lses, axis=0), axis=0)
        alpha = jnp.swapaxes(alpha, 2, 3)[..., None].astype(v.dtype)
        return jnp.sum(alpha * jnp.stack(outs, axis=0), axis=0)

    starts = jnp.arange(nblk, dtype=jnp.int32) * QBLOCK
    out = lax.map(block, (to_blocks(q), starts))
    return from_blocks(out)


def hybrid_layer(x, cos, sin, w_in, b_gate, w_branch_a, w_branch_b, w_out, ln1_g, ln1_b,
                 w_ffn_gate, w_ffn_up, w_ffn_down, ln2_g, ln2_b):
    bsz, seq, _ = x.shape
    points = np.cumsum(IN_SPLITS)[:-1].tolist()
    qa, ka, va, qb, kb, vb, qi, ki, wi, gate_pre = jnp.split(x @ w_in, points, axis=-1)
    heads = lambda a, h: a.reshape(bsz, seq, h, -1)
    qa = partial_rope(heads(qa, A_HEADS), cos, sin)
    ka = partial_rope(heads(ka, A_HEADS), cos, sin)
    qi = partial_rope(heads(qi, IDX_HEADS), cos, sin)
    ki = partial_rope(ki.reshape(bsz, seq, 1, IDX_DIM), cos, sin)[:, :, 0]
    o_a = dsa_attention(qa, ka, heads(va, A_HEADS), qi, ki, wi).reshape(bsz, seq, A_WIDTH)
    qb = partial_rope(heads(qb, B_HEADS), cos, sin)
    kb = partial_rope(heads(kb, B_HEADS), cos, sin)
    o_b = dilated_attention(qb, kb, heads(vb, B_HEADS)).reshape(bsz, seq, B_OUT_WIDTH)
    g_a, g_b = jnp.split(jax.nn.sigmoid(gate_pre + b_gate), 2, axis=-1)
    mixed = (g_a * (o_a @ w_branch_a) + g_b * (o_b @ w_branch_b)) @ w_out
    x = layer_norm(DN_ALPHA * x + mixed, ln1_g, ln1_b)
    h = jax.nn.silu(x @ w_ffn_gate) * (x @ w_ffn_up)
    return layer_norm(DN_ALPHA * x + h @ w_ffn_down, ln2_g, ln2_b)


def setup_inputs(seed: int = 0) -> dict:
    key = jax.random.key(seed)
    ks = jax.random.split(key, 16)
    nrm = lambda k, shape, scale: jax.random.normal(k, shape, jnp.float32) * scale
    return {
        'x': nrm(ks[0], (BATCH, SEQ, D_MODEL), 1.0),
        'positions': jnp.broadcast_to(jnp.arange(SEQ, dtype=jnp.int32), (BATCH, SEQ)),
        'w_in': nrm(ks[1], (DEPTH, D_MODEL, N_IN), D_MODEL ** -0.5),
        'b_gate': nrm(ks[2], (DEPTH, 2 * D_MODEL), 0.02),
        'w_branch_a': nrm(ks[3], (DEPTH, A_WIDTH, D_MODEL), A_WIDTH ** -0.5),
        'w_branch_b': nrm(ks[4], (DEPTH, B_OUT_WIDTH, D_MODEL), B_OUT_WIDTH ** -0.5),
        'w_out': nrm(ks[5], (DEPTH, D_MODEL, D_MODEL), DN_BETA * D_MODEL ** -0.5),
        'ln1_g': 1.0 + nrm(ks[6], (DEPTH, D_MODEL), 0.02),
        'ln1_b': nrm(ks[7], (DEPTH, D_MODEL), 0.02),
        'w_ffn_gate': nrm(ks[8], (DEPTH, D_MODEL, FFN_HIDDEN), D_MODEL ** -0.5),
        'w_ffn_up': nrm(ks[9], (DEPTH, D_MODEL, FFN_HIDDEN), D_MODEL ** -0.5),
        'w_ffn_down': nrm(ks[10], (DEPTH, FFN_HIDDEN, D_MODEL), DN_BETA * FFN_HIDDEN ** -0.5),
        'ln2_g': 1.0 + nrm(ks[11], (DEPTH, D_MODEL), 0.02),
        'ln2_b': nrm(ks[12], (DEPTH, D_MODEL), 0.02),
    }


def reference(x, positions, w_in, b_gate, w_branch_a, w_branch_b, w_out, ln1_g, ln1_b,
              w_ffn_gate, w_ffn_up, w_ffn_down, ln2_g, ln2_b):
    cos, sin = rope_tables(positions, x.dtype)
    for layer in range(DEPTH):
        x = hybrid_layer(x, cos, sin, w_in[layer], b_gate[layer], w_branch_a[layer],
                         w_branch_b[layer], w_out[layer], ln1_g[layer], ln1_b[layer],
                         w_ffn_gate[layer], w_ffn_up[layer], w_ffn_down[layer],
                         ln2_g[layer], ln2_b[layer])
    return x
```

```python
import math
from contextlib import ExitStack

import numpy as np
import concourse.bass as bass
import concourse.mybir as mybir
from concourse.bass_utils import run_bass_kernel_spmd

F32 = mybir.dt.float32
BF16 = mybir.dt.bfloat16
I32 = mybir.dt.int32
AF = mybir.ActivationFunctionType
ALU = mybir.AluOpType
AX = mybir.AxisListType

D = 1024
HD = 64
NQA = 768
FFN = 2816
NHC = FFN // 128
N_IN = 7240
ROPE_THETA = 500000.0
ALPHA = 2.0 ** 0.25
LN_EPS = 1e-5
NIT = 16
BIG = 1.0e30
MASK_NEG = 30000.0
SAME_ENGINE_SYNC = True
import os
KSTOP = int(os.environ.get('KSTOP', '99'))
KMAX = int(os.environ.get('KMAX', '100000000'))

C_ID, C_PERM, C_TRIM, C_TRIP, C_MD, C_MP, C_ROPE, C_POW = 0, 128, 256, 384, 512, 640, 768, 772
NCONST = C_POW + NIT + 1


class Buf:
    __slots__ = ("name", "W", "Wf", "R", "Rprev", "excl", "last")

    def __init__(self, name, excl=False):
        self.name = name
        self.excl = excl
        self.last = {}
        self.W = []
        self.Wf = []
        self.R = []
        self.Rprev = []


class Op:
    __slots__ = ("eng", "fn", "deps", "dma", "ms", "msidx", "dsem", "dval", "prev_dval", "seq", "cdeps")

    def __init__(self, eng, fn, dma):
        self.eng = eng
        self.fn = fn
        self.deps = []
        self.dma = dma
        self.ms = False
        self.msidx = 0
        self.dsem = None
        self.dval = 0
        self.prev_dval = 0


class Prog:
    ENGS = ("pe", "act", "dve", "pool", "sp")

    def __init__(self, n_dma_sems=24):
        self.ops = {e: [] for e in self.ENGS}
        self.n_dma_sems = n_dma_sems
        self.n_dma = 0
        self.final_dmas = []
        self.phase = Buf("phase")
        self.stopped = False

    def op(self, eng, fn, reads=(), writes=(), pw=(), dma=False, final=False, _barrier=False):
        if self.stopped:
            return None
        self.nops = getattr(self, "nops", 0) + 1
        if self.nops > KMAX:
            self.stopped = True
            return None
        o = Op(eng, fn, dma)
        deps = []
        for b in list(reads) + list(writes) + list(pw):
            if b.excl:
                deps.extend(op_ for e2, op_ in b.last.items() if e2 != eng)
                b.last[eng] = o
        reads = [b for b in reads if not b.excl]
        writes = [b for b in writes if not b.excl]
        pw = [b for b in pw if not b.excl]
        wset = set(id(b) for b in writes) | set(id(b) for b in pw)
        rlist = [b for b in reads if id(b) not in wset]
        if not _barrier:
            rlist = rlist + [self.phase]
        for b in rlist:
            deps.extend(b.W)
        for b in list(writes) + list(pw):
            if b.R:
                b.Rprev = b.R
                b.R = []
                oldW = b.W
                b.W = []
                b.Wf = []
                if any(id(b) == id(x) for x in reads):
                    deps.extend(oldW)
            deps.extend(b.Rprev)
            if any(id(b) == id(x) for x in writes):
                deps.extend(b.W)
            else:
                deps.extend(b.Wf)
        seen = set()
        for d in deps:
            if id(d) in seen or d is o:
                continue
            seen.add(id(d))
            if (not d.dma) and (not dma) and d.eng == eng and (eng == "pe" or not SAME_ENGINE_SYNC):
                continue
            o.deps.append(d)
        for b in rlist:
            b.R.append(o)
        for b in list(writes) + list(pw):
            b.W.append(o)
        for b in writes:
            b.Wf.append(o)
        if dma:
            k = self.n_dma % self.n_dma_sems
            o.dsem = k
            o.dval = 16 * (self.n_dma // self.n_dma_sems + 1)
            o.prev_dval = o.dval - 16
            self.n_dma += 1
            if final:
                self.final_dmas.append(o)
        o.seq = len(self.ops[eng])
        self.ops[eng].append(o)
        return o

    def barrier(self, fn):
        return self.op("pool", fn, writes=[self.phase], _barrier=True)

    def emit(self, nc, es, block):
        for e in self.ENGS:
            for o in self.ops[e]:
                best = {}
                for d in o.deps:
                    if not d.dma:
                        if d.eng not in best or best[d.eng].seq < d.seq:
                            best[d.eng] = d
                for d in best.values():
                    d.ms = True
                o.cdeps = list(best.values()) + [d for d in o.deps if d.dma]
        for e in self.ENGS:
            c = 0
            for o in self.ops[e]:
                if o.ms and not o.dma:
                    c += 1
                    o.msidx = c
        esem = {e: es.enter_context(nc.semaphore("sem_" + e)) for e in self.ENGS}
        dsems = [es.enter_context(nc.semaphore("dsem%d" % i)) for i in range(self.n_dma_sems)]
        prog = self

        def run(eng_name):
            def body(e):
                waited = {}

                def wait(key, sem, val):
                    if val <= 0:
                        return
                    if waited.get(key, 0) >= val:
                        return
                    waited[key] = val
                    e.wait_ge(sem, val)

                for o in prog.ops[eng_name]:
                    need = {}
                    for d in o.cdeps:
                        if d.dma:
                            k_, s_, v_ = ("d", d.dsem), dsems[d.dsem], d.dval
                        else:
                            k_, s_, v_ = ("e", d.eng), esem[d.eng], d.msidx
                        if k_ not in need or need[k_][1] < v_:
                            need[k_] = (s_, v_)
                    if o.dma:
                        k_ = ("d", o.dsem)
                        if k_ not in need or need[k_][1] < o.prev_dval:
                            need[k_] = (dsems[o.dsem], o.prev_dval)
                    for k_, (s_, v_) in need.items():
                        wait(k_, s_, v_)
                    inst = o.fn(e)
                    if o.dma:
                        inst.then_inc(dsems[o.dsem], 16)
                    elif o.ms:
                        inst.then_inc(esem[eng_name], 1)
                if eng_name == "sp":
                    for o in prog.final_dmas:
                        wait(("d", o.dsem), dsems[o.dsem], o.dval)
            return body

        block.tensor(run("pe"))
        block.scalar(run("act"))
        block.vector(run("dve"))
        block.gpsimd(run("pool"))
        block.sync(run("sp"))


def build(S, KTOP):
    NT = S // 128
    NCH = S // 512
    nc = bass.Bass("TRN2", target_bir_lowering=False)
    P = Prog()

    def din(name, shape, dt=F32):
        return nc.dram_tensor(name, list(shape), dt, kind="ExternalInput").ap()

    def dscr(name, shape, dt):
        return nc.dram_tensor(name, list(shape), dt, kind="Internal").ap()

    x_d = din("x", [S, D])
    pos_d = din("pos", [1, S], I32)
    wfm_d = din("w_fm", [45, 128, 1024])
    wtm_d = din("w_tm", [128, 8 * 1544])
    bg_d = din("bg", [128, 16])
    wa_d = din("wa", [128, 6 * 1024])
    wb_d = din("wb", [128, 2 * 1024])
    wo_d = din("wo", [128, 8 * 1024])
    wg_d = din("wg", [128, 8 * FFN])
    wu_d = din("wu", [128, 8 * FFN])
    wd_d = din("wd", [128, NHC * 1024])
    ln_d = din("ln", [4, D])
    cst_d = din("cst", [128, NCONST])
    out_d = nc.dram_tensor("out", [S, D], F32, kind="ExternalOutput").ap()

    QA_T = dscr("s_qat", [6, 128, S], BF16)
    KA_T = dscr("s_kat", [6, 128, S], BF16)
    QB_T = dscr("s_qbt", [6, 128, S], BF16)
    KB_T = dscr("s_kbt", [6, 128, S], BF16)
    QI_T = dscr("s_qit", [4, 128, S], BF16)
    KI_T = dscr("s_kit", [128, S], BF16)
    G_T = dscr("s_gt", [16, 128, S], BF16)
    VA_S = dscr("s_va", [S, 6 * 192], BF16)
    VB_S = dscr("s_vb", [S, 6 * 192], BF16)
    OA_T = dscr("s_oat", [6, 128, S], BF16)
    OB_T = dscr("s_obt", [2, 128, S], BF16)
    H1 = dscr("s_h1", [S, D], F32)
    H1T = dscr("s_h1t", [8, 128, S], BF16)
    b_QA, b_KA, b_QB, b_KB, b_QI, b_KI, b_G = (Buf(n) for n in ("QA", "KA", "QB", "KB", "QI", "KI", "G"))
    b_VA, b_VB, b_OA, b_OB, b_H1, b_H1T = (Buf(n) for n in ("VA", "VB", "OA", "OB", "H1", "H1T"))

    with ExitStack() as es:
        def sb(name, shape, dt, st=None):
            return (st or es).enter_context(nc.sbuf_tensor("sb_" + name, list(shape), dt))

        banks = [es.enter_context(nc.psum_tensor("bank%d" % i, [128, 512], F32)) for i in range(8)]
        bbuf = [Buf("bank%d" % i, excl=True) for i in range(8)]

        cst = sb("cst", [128, NCONST], F32)
        b_cst = Buf("cst")
        identb = sb("identb", [128, 128], BF16)
        permb = sb("permb", [128, 128], BF16)
        mb4 = sb("mb4", [128, 512], BF16)
        b_cb = Buf("constb")
        P.op("sp", lambda e: e.dma_start(out=cst[:], in_=cst_d[:, :]), writes=[b_cst], dma=True)
        P.op("dve", lambda e: e.tensor_copy(out=identb[:], in_=cst[:, C_ID:C_ID + 128]), reads=[b_cst], writes=[b_cb])
        P.op("dve", lambda e: e.tensor_copy(out=permb[:], in_=cst[:, C_PERM:C_PERM + 128]), reads=[b_cst], writes=[b_cb])
        for j in range(4):
            src = C_MD if j % 2 == 0 else C_MP
            P.op("dve", lambda e, j=j, src=src: e.tensor_copy(out=mb4[:, j * 128:(j + 1) * 128], in_=cst[:, src:src + 128]),
                 reads=[b_cst], writes=[b_cb])
        ident_f = cst[:, C_ID:C_ID + 128]
        negb = sb("negb", [128, 1], F32)
        P.op("dve", lambda e: e.memset(negb[:], -MASK_NEG), writes=[b_cb])

        WI = sb("WI", [128, NT, 8], F32)
        b_WI = Buf("WI")
        bar = sb("bar", [128, 8], F32)

        def barrier(k=0):
            if os.environ.get("KDBG"):
                print("barrier", k, getattr(P, "nops", 0))
            if k > KSTOP:
                P.stopped = True
            P.barrier(lambda e: e.memset(bar[:], 0.0))

        with ExitStack() as s2:
            XT = sb("XT", [128, 8, S], BF16, s2)
            b_XT = [Buf("XT%d" % t) for t in range(NT)]
            xs = [sb("xs%d" % i, [128, D], F32, s2) for i in range(2)]
            b_xs = [Buf("xs%d" % i) for i in range(2)]
            for tt in range(NT):
                sl = tt % 2
                P.op("sp", lambda e, tt=tt, sl=sl: e.dma_start(out=xs[sl][:], in_=x_d[tt * 128:(tt + 1) * 128, :]),
                     writes=[b_xs[sl]], dma=True)
                for half in range(2):
                    bk = (tt * 2 + half) % 2
                    for j in range(4):
                        kc = half * 4 + j
                        P.op("pe", lambda e, bk=bk, j=j, kc=kc, sl=sl: e.transpose(
                            out=banks[bk][:, j * 128:(j + 1) * 128], in_=xs[sl][:, kc * 128:(kc + 1) * 128], identity=ident_f),
                            reads=[b_xs[sl], b_cst], writes=[bbuf[bk]])
                    eng = "act" if half == 0 else "dve"
                    if eng == "act":
                        P.op("act", lambda e, bk=bk, half=half, tt=tt: e.activation(
                            out=XT[:, half * 4:half * 4 + 4, tt * 128:(tt + 1) * 128],
                            in_=banks[bk][:].rearrange("p (a b) -> p a b", a=4), func=AF.Copy),
                            reads=[bbuf[bk]], pw=[b_XT[tt]])
                    else:
                        P.op("dve", lambda e, bk=bk, half=half, tt=tt: e.tensor_copy(
                            out=XT[:, half * 4:half * 4 + 4, tt * 128:(tt + 1) * 128],
                            in_=banks[bk][:].rearrange("p (a b) -> p a b", a=4)),
                            reads=[bbuf[bk]], pw=[b_XT[tt]])

            CT = sb("CT", [128, S], F32, s2)
            ST = sb("ST", [128, S], F32, s2)
            b_CT, b_ST = Buf("CT"), Buf("ST")
            with ExitStack() as sr:
                posi = sb("posi", [128, S], I32, sr)
                posf = sb("posf", [128, S], F32, sr)
                tk = sb("ropek", [128, S], F32, sr)
                b_posi, b_posf, b_tk = Buf("posi"), Buf("posf"), Buf("tk")
                P.op("sp", lambda e: e.dma_start(out=posi[:], in_=pos_d[0:1, :].partition_broadcast(128)),
                     writes=[b_posi], dma=True)
                P.op("dve", lambda e: e.tensor_copy(out=posf[:], in_=posi[:]), reads=[b_posi], writes=[b_posf])
                MAGIC = 12582912.0
                for (T, bT, ca, cb) in ((ST, b_ST, C_ROPE + 0, None), (CT, b_CT, C_ROPE + 1, C_ROPE + 2)):
                    if cb is None:
                        P.op("dve", lambda e, T=T, ca=ca: e.tensor_scalar(
                            out=T[:], in0=posf[:], scalar1=cst[:, ca:ca + 1], scalar2=None, op0=ALU.mult),
                            reads=[b_posf, b_cst], writes=[bT])
                    else:
                        P.op("dve", lambda e, T=T, ca=ca, cb=cb: e.tensor_scalar(
                            out=T[:], in0=posf[:], scalar1=cst[:, ca:ca + 1], scalar2=cst[:, cb:cb + 1],
                            op0=ALU.mult, op1=ALU.add), reads=[b_posf, b_cst], writes=[bT])
                    P.op("dve", lambda e, T=T: e.tensor_scalar(out=tk[:], in0=T[:], scalar1=MAGIC, scalar2=None, op0=ALU.add),
                         reads=[bT], writes=[b_tk])
                    P.op("dve", lambda e, T=T: e.tensor_scalar(out=tk[:], in0=tk[:], scalar1=MAGIC, scalar2=None, op0=ALU.subtract),
                         reads=[b_tk], writes=[b_tk])
                    P.op("dve", lambda e, T=T: e.tensor_tensor(out=T[:], in0=T[:], in1=tk[:], op=ALU.subtract),
                         reads=[bT, b_tk], writes=[bT])
                    P.op("dve", lambda e, T=T: e.tensor_scalar(out=tk[:], in0=T[:], scalar1=0.5, scalar2=None, op0=ALU.is_gt),
                         reads=[bT], writes=[b_tk])
                    P.op("dve", lambda e, T=T: e.tensor_tensor(out=T[:], in0=T[:], in1=tk[:], op=ALU.subtract),
                         reads=[bT, b_tk], writes=[bT])
                    P.op("dve", lambda e, T=T: e.tensor_scalar(out=tk[:], in0=T[:], scalar1=-0.5, scalar2=None, op0=ALU.is_lt),
                         reads=[bT], writes=[b_tk])
                    P.op("dve", lambda e, T=T: e.tensor_tensor(out=T[:], in0=T[:], in1=tk[:], op=ALU.add),
                         reads=[bT, b_tk], writes=[bT])
                    P.op("act", lambda e, T=T: e.activation(out=T[:], in_=T[:], func=AF.Sin, scale=2.0 * math.pi * (1.0 - 1e-6)),
                         reads=[bT], writes=[bT])

            barrier(2)
            wst = [sb("wst%d" % i, [128, 1024], F32, s2) for i in range(2)]
            wbf = [sb("wbf%d" % i, [128, 8, 128], BF16, s2) for i in range(2)]
            b_wst = [Buf("wst%d" % i) for i in range(2)]
            b_wbf = [Buf("wbf%d" % i) for i in range(2)]
            bgt = sb("bgt", [128, 16], F32, s2)
            b_bgt = Buf("bgt")
            P.op("sp", lambda e: e.dma_start(out=bgt[:], in_=bg_d[:, :]), writes=[b_bgt], dma=True)
            qtmp = [sb("qtmp%d" % i, [128, 512], BF16, s2) for i in range(2)]
            b_qtmp = [Buf("qtmp%d" % i) for i in range(2)]
            rt1 = [sb("rt1_%d" % i, [128, 512], F32, s2) for i in range(2)]
            rt2 = [sb("rt2_%d" % i, [128, 512], F32, s2) for i in range(2)]
            b_rt1 = [Buf("rt1_%d" % i) for i in range(2)]
            b_rt2 = [Buf("rt2_%d" % i) for i in range(2)]
            ost = [sb("ost%d" % i, [128, 512], BF16, s2) for i in range(3)]
            b_ost = [Buf("ost%d" % i) for i in range(3)]
            tiles = []
            for c in range(6):
                tiles.append((QA_T[c], b_QA, "rope"))
            for c in range(6):
                tiles.append((KA_T[c], b_KA, "rope"))
            for c in range(6):
                tiles.append((QB_T[c], b_QB, "rope"))
            for c in range(6):
                tiles.append((KB_T[c], b_KB, "rope"))
            for c in range(4):
                tiles.append((QI_T[c], b_QI, "rope"))
            tiles.append((KI_T, b_KI, "rope"))
            for c in range(16):
                tiles.append((G_T[c], b_G, "gate"))
            def fm_main(idx):
                ci, n = idx // NCH, idx % NCH
                dst, bdst, kind = tiles[ci]
                ws = ci % 2
                if n == 0:
                    for cj in ([0, 1] if ci == 0 else [ci + 1]):
                        if cj < len(tiles):
                            wj = cj % 2
                            P.op("act", lambda e, cj=cj, wj=wj: e.dma_start(out=wst[wj][:], in_=wfm_d[cj]), writes=[b_wst[wj]], dma=True)
                            P.op("pool", lambda e, wj=wj: e.tensor_copy(out=wbf[wj][:].rearrange("p a b -> p (a b)"), in_=wst[wj][:]),
                                 reads=[b_wst[wj]], writes=[b_wbf[wj]])
                bk = idx % 2
                tsl = slice(n * 512, (n + 1) * 512)
                xdeps = [b_XT[t] for t in range(n * 4, n * 4 + 4)]
                for kc in range(8):
                    P.op("pe", lambda e, bk=bk, kc=kc, ws=ws, tsl=tsl: e.matmul(
                        banks[bk][:], lhsT=wbf[ws][:, kc, :], rhs=XT[:, kc, tsl], start=(kc == 0), stop=(kc == 7)),
                        reads=[b_wbf[ws]] + xdeps, writes=[bbuf[bk]])
                o3 = idx % 3
                if kind == "gate":
                    gc = ci - 29
                    P.op("act", lambda e, bk=bk, o3=o3, gc=gc: e.activation(
                        out=ost[o3][:], in_=banks[bk][:], func=AF.Sigmoid, bias=bgt[:, gc:gc + 1]),
                        reads=[bbuf[bk], b_bgt], writes=[b_ost[o3]])
                    P.op("sp", lambda e, o3=o3, dst=dst, tsl=tsl: e.dma_start(out=dst[:, tsl], in_=ost[o3][:]),
                         reads=[b_ost[o3]], pw=[bdst], dma=True)
                else:
                    P.op("act", lambda e, bk=bk: e.activation(out=qtmp[bk][:], in_=banks[bk][:], func=AF.Copy),
                         reads=[bbuf[bk]], writes=[b_qtmp[bk]])

            def fm_rope(idx):
                ci, n = idx // NCH, idx % NCH
                dst, bdst, kind = tiles[ci]
                if kind == "gate":
                    return
                bk = idx % 2
                q2 = bk
                o3 = idx % 3
                tsl = slice(n * 512, (n + 1) * 512)
                P.op("pe", lambda e, bk=bk, q2=q2: e.matmul(banks[2 + bk][:], lhsT=permb[:], rhs=qtmp[q2][:], start=True, stop=True),
                     reads=[b_qtmp[q2], b_cb], writes=[bbuf[2 + bk]])
                P.op("pool", lambda e, q2=q2, tsl=tsl: e.tensor_tensor(out=rt1[q2][:], in0=qtmp[q2][:], in1=CT[:, tsl], op=ALU.mult),
                     reads=[b_qtmp[q2], b_CT], writes=[b_rt1[q2]])
                P.op("dve", lambda e, bk=bk, q2=q2, tsl=tsl: e.tensor_tensor(out=rt2[q2][:], in0=banks[2 + bk][:], in1=ST[:, tsl], op=ALU.mult),
                     reads=[bbuf[2 + bk], b_ST], writes=[b_rt2[q2]])
                P.op("dve", lambda e, q2=q2, o3=o3: e.tensor_tensor(out=ost[o3][:], in0=rt1[q2][:], in1=rt2[q2][:], op=ALU.add),
                     reads=[b_rt1[q2], b_rt2[q2]], writes=[b_ost[o3]])
                P.op("sp", lambda e, o3=o3, dst=dst, tsl=tsl: e.dma_start(out=dst[:, tsl], in_=ost[o3][:]),
                     reads=[b_ost[o3]], pw=[bdst], dma=True)

            nitems = len(tiles) * NCH
            for idx in range(nitems + 1):
                if idx < nitems:
                    fm_main(idx)
                if idx >= 1:
                    fm_rope(idx - 1)

            if os.environ.get("KDBG"):
                print("tm start", getattr(P, "nops", 0))
            wtm = sb("wtm", [128, 8, 1544], BF16, s2)
            b_wtm = Buf("wtm")
            for kc in range(8):
                ws = kc % 2
                P.op("sp", lambda e, kc=kc, ws=ws: e.dma_start(out=wst[ws][:, 0:1544 - 1024], in_=wtm_d[:, kc * 1544 + 1024:(kc + 1) * 1544]),
                     writes=[b_wst[ws]], dma=True)
                P.op("pool", lambda e, kc=kc, ws=ws: e.tensor_copy(out=wtm[:, kc, 1024:1544], in_=wst[ws][:, 0:520]),
                     reads=[b_wst[ws]], pw=[b_wtm])
                ws = (kc + 1) % 2
                P.op("sp", lambda e, kc=kc, ws=ws: e.dma_start(out=wst[ws][:], in_=wtm_d[:, kc * 1544:kc * 1544 + 1024]),
                     writes=[b_wst[ws]], dma=True)
                P.op("pool", lambda e, kc=kc, ws=ws: e.tensor_copy(out=wtm[:, kc, 0:1024], in_=wst[ws][:]),
                     reads=[b_wst[ws]], pw=[b_wtm])
            vst = [sb("vst%d" % i, [128, 2, 6, 192], BF16, s2) for i in range(2)]
            b_vst = [Buf("vst%d" % i) for i in range(2)]
            for i in range(2):
                P.op("pool", lambda e, i=i: e.memset(vst[i][:].rearrange("p a b c -> p (a b c)"), 1.0), writes=[b_vst[i]])
            colofs = [0, 384, 776, 1160]
            colw = [384, 392, 384, 384]
            for tt in range(NT):
                vs = tt % 2
                for part in range(4):
                    bk = 4 + (tt * 4 + part) % 4
                    for kc in range(8):
                        P.op("pe", lambda e, bk=bk, kc=kc, tt=tt, part=part: e.matmul(
                            banks[bk][:, 0:colw[part]], lhsT=XT[:, kc, tt * 128:(tt + 1) * 128],
                            rhs=wtm[:, kc, colofs[part]:colofs[part] + colw[part]], start=(kc == 0), stop=(kc == 7)),
                            reads=[b_wtm, b_XT[tt]], writes=[bbuf[bk]])
                    ab = part // 2
                    pr0 = (part % 2) * 3
                    src = banks[bk][:, 0:384].rearrange("p (a b) -> p a b", a=3)
                    P.op("act", lambda e, vs=vs, ab=ab, pr0=pr0, src=src: e.activation(
                        out=vst[vs][:, ab, pr0:pr0 + 3, 0:64], in_=src[:, :, 0:64], func=AF.Copy),
                        reads=[bbuf[bk]], pw=[b_vst[vs]])
                    P.op("dve", lambda e, vs=vs, ab=ab, pr0=pr0, src=src: e.tensor_copy(
                        out=vst[vs][:, ab, pr0:pr0 + 3, 128:192], in_=src[:, :, 64:128]),
                        reads=[bbuf[bk]], pw=[b_vst[vs]])
                    if part == 1:
                        P.op("dve", lambda e, bk=bk, tt=tt: e.tensor_copy(out=WI[:, tt, :], in_=banks[bk][:, 384:392]),
                             reads=[bbuf[bk]], writes=[b_WI])
                P.op("sp", lambda e, vs=vs, tt=tt: e.dma_start(out=VA_S[tt * 128:(tt + 1) * 128, :],
                                                              in_=vst[vs][:, 0].rearrange("p b c -> p (b c)")),
                     reads=[b_vst[vs]], pw=[b_VA], dma=True)
                P.op("sp", lambda e, vs=vs, tt=tt: e.dma_start(out=VB_S[tt * 128:(tt + 1) * 128, :],
                                                              in_=vst[vs][:, 1].rearrange("p b c -> p (b c)")),
                     reads=[b_vst[vs]], pw=[b_VB], dma=True)

        barrier(3)
        with ExitStack() as s3:
            KA = sb("KA", [128, 6, S], BF16, s3)
            VA = sb("VA", [128, NT, 6 * 192], BF16, s3)
            KI = sb("KI", [128, S], BF16, s3)
            b_KAs, b_VAs, b_KIs = Buf("KAs"), Buf("VAs"), Buf("KIs")
            for c in range(6):
                P.op("sp", lambda e, c=c: e.dma_start(out=KA[:, c, :], in_=KA_T[c]), reads=[b_KA], pw=[b_KAs], dma=True)
            for t0 in range(0, NT, 8):
                t1 = min(NT, t0 + 8)
                P.op("sp", lambda e, t0=t0, t1=t1: e.dma_start(
                    out=VA[:, t0:t1, :], in_=VA_S[t0 * 128:t1 * 128, :].rearrange("(t p) c -> p t c", p=128)),
                    reads=[b_VA], pw=[b_VAs], dma=True)
            P.op("sp", lambda e: e.dma_start(out=KI[:], in_=KI_T), reads=[b_KI], writes=[b_KIs], dma=True)

            qch = sb("qch", [128, 6, 2, 256], BF16, s3)
            qich = sb("qich", [128, 4, 2, 256], BF16, s3)
            b_qch, b_qich = Buf("qch"), Buf("qich")
            P.op("pool", lambda e: e.memset(qch[:].rearrange("p a b c -> p (a b c)"), 0.0), writes=[b_qch])
            P.op("pool", lambda e: e.memset(qich[:].rearrange("p a b c -> p (a b c)"), 0.0), writes=[b_qich])
            scs = [sb("sc%d" % i, [128, S], F32, s3) for i in range(2)]
            b_scs = [Buf("sc%d" % i) for i in range(2)]
            junk = sb("junk", [128, S], BF16, s3)
            b_junk = Buf("junk")
            MT = [sb("MT%d" % i, [128, NT, 128], BF16, s3) for i in range(1)]
            b_MT = [Buf("MT%d" % i) for i in range(1)]
            rsb = [sb("rsb%d" % i, [128, 512], BF16, s3) for i in range(2)]
            b_rsb = [Buf("rsb%d" % i) for i in range(2)]
            esb = [sb("esb%d" % i, [128, 512], BF16, s3) for i in range(3)]
            b_esb = [Buf("esb%d" % i) for i in range(3)]
            otile = [sb("otile%d" % i, [128, 6, 128], BF16, s3) for i in range(2)]
            b_otile = [Buf("otile%d" % i) for i in range(2)]
            dsg = [sb("dsg%d" % i, [128, 8, 128], BF16, s3) for i in range(2)]
            b_dsg = [Buf("dsg%d" % i) for i in range(2)]
            wab = [sb("wab%d" % i, [128, 8], F32, s3) for i in range(2)]
            wsg = [sb("wsg%d" % i, [128, 8], F32, s3) for i in range(2)]
            b_wab = [Buf("wab%d" % i) for i in range(2)]
            b_wsg = [Buf("wsg%d" % i) for i in range(2)]
            dmin = [sb("dmin%d" % i, [128, 1], F32, s3) for i in range(2)]
            b_dmin = [Buf("dmin%d" % i) for i in range(2)]
            small = sb("small", [128, 16], F32, s3)
            b_small = Buf("small")
            hw = sb("hw", [128, NIT + 1], F32, s3)
            b_hw = Buf("hw")
            dtmp = sb("dtmp", [128, 128], F32, s3)
            b_dtmp = Buf("dtmp")
            rden = [sb("rden%d" % i, [128, 128], F32, s3) for i in range(2)]
            b_rden = [Buf("rden%d" % i) for i in range(2)]
            cnts = {"z": 0, "e": 0, "o": 0}

            def stageB1(qt):
                qc, qo = qt // 2, (qt % 2) * 128
                if qt % 2 == 0:
                    for hh_ in range(2):
                        P.op("sp", lambda e, qc=qc, hh_=hh_: e.dma_start(
                            out=qich[hh_ * 64:(hh_ + 1) * 64, :, hh_, :],
                            in_=QI_T[:, hh_ * 64:(hh_ + 1) * 64, qc * 256:(qc + 1) * 256].rearrange("c p t -> p c t")),
                            reads=[b_QI], writes=[b_qich], dma=True)
                n_s = qt + 1
                sc, b_sc = scs[qt % 2], b_scs[qt % 2]
                n = 128 * n_s
                nchk = (n_s + 3) // 4
                ds = qt % 2
                P.op("dve", lambda e, ds=ds, qt=qt: e.tensor_scalar(out=wsg[ds][:], in0=WI[:, qt, :], scalar1=0.0, scalar2=-0.5,
                                                                   op0=ALU.is_gt, op1=ALU.add),
                     reads=[b_WI], writes=[b_wsg[ds]])
                P.op("dve", lambda e, ds=ds, qt=qt: e.scalar_tensor_tensor(out=wsg[ds][:], in0=WI[:, qt, :], scalar=0.0, in1=wsg[ds][:],
                                                                          op0=ALU.is_lt, op1=ALU.subtract),
                     reads=[b_WI, b_wsg[ds]], writes=[b_wsg[ds]])
                P.op("dve", lambda e, ds=ds: e.tensor_scalar(out=wsg[ds][:], in0=wsg[ds][:], scalar1=-1.0, scalar2=0.5,
                                                            op0=ALU.mult, op1=ALU.add),
                     reads=[b_wsg[ds]], writes=[b_wsg[ds]])
                P.op("dve", lambda e, ds=ds, qt=qt: e.tensor_tensor(out=wab[ds][:], in0=WI[:, qt, :], in1=wsg[ds][:], op=ALU.mult),
                     reads=[b_WI, b_wsg[ds]], writes=[b_wab[ds]])
                for h in range(8):
                    P.op("dve", lambda e, ds=ds, h=h: e.tensor_scalar(out=dsg[ds][:, h, :], in0=cst[:, C_ID:C_ID + 128],
                                                                     scalar1=wsg[ds][:, h:h + 1], scalar2=None, op0=ALU.mult),
                         reads=[b_cst, b_wsg[ds]], writes=[b_dsg[ds]])
                yield
                items = [(c, h) for c in range(nchk) for h in range(8)]

                def z_part(c, h):
                    ncols = min(512, n - 512 * c)
                    hp, hh = h // 2, h % 2
                    zb = (c * 8 + h) % 2
                    rows = slice(hh * 64, hh * 64 + 64)
                    P.op("pe", lambda e, zb=zb, hh=hh, hp=hp, c=c, ncols=ncols: e.matmul(
                        banks[zb][:, 0:ncols], lhsT=qich[:, hp, hh, qo:qo + 128], rhs=KI[:, c * 512:c * 512 + ncols],
                        start=True, stop=True), reads=[b_qich, b_KIs], writes=[bbuf[zb]])
                    P.op("act", lambda e, zb=zb, ncols=ncols, h=h: e.activation(
                        out=rsb[zb][:, 0:ncols], in_=banks[zb][:, 0:ncols], func=AF.Relu, scale=wab[ds][:, h:h + 1]),
                        reads=[bbuf[zb], b_wab[ds]], writes=[b_rsb[zb]])

                def d_part(c, h):
                    ncols = min(512, n - 512 * c)
                    zb = (c * 8 + h) % 2
                    ab = 2 if c % 2 == 0 else 7
                    P.op("pe", lambda e, ab=ab, zb=zb, ncols=ncols, h=h: e.matmul(
                        banks[ab][:, 0:ncols], lhsT=dsg[ds][:, h, :], rhs=rsb[zb][:, 0:ncols], start=(h == 0), stop=(h == 7)),
                        reads=[b_dsg[ds], b_rsb[zb]], writes=[bbuf[ab]])
                    if h != 7:
                        return
                    last = (c == nchk - 1)
                    nfull = ncols - 128 if last else ncols
                    if nfull > 0:
                        P.op("dve", lambda e, ab=ab, c=c, nfull=nfull: e.tensor_copy(out=sc[:, c * 512:c * 512 + nfull], in_=banks[ab][:, 0:nfull]),
                             reads=[bbuf[ab]], writes=[b_sc])
                    if last:
                        dsl = slice(n - 128, n)
                        P.op("dve", lambda e, ab=ab, nfull=nfull: e.tensor_tensor(out=dtmp[:], in0=banks[ab][:, nfull:nfull + 128],
                                                                                 in1=cst[:, C_TRIP:C_TRIP + 128], op=ALU.add),
                             reads=[bbuf[ab], b_cst], writes=[b_dtmp])
                        P.op("dve", lambda e: e.tensor_reduce(out=dmin[qt % 2][:, 0:1], in_=dtmp[:], axis=AX.X, op=ALU.min),
                             reads=[b_dtmp], writes=[b_dmin[qt % 2]])
                        P.op("dve", lambda e, ab=ab, nfull=nfull, dsl=dsl: e.tensor_tensor(out=sc[:, dsl], in0=banks[ab][:, nfull:nfull + 128],
                                                                                          in1=cst[:, C_TRIM:C_TRIM + 128], op=ALU.add),
                             reads=[bbuf[ab], b_cst], writes=[b_sc])

                for i_ in range(len(items) + 1):
                    if i_ < len(items):
                        z_part(*items[i_])
                    if i_ >= 1:
                        d_part(*items[i_ - 1])
                    if i_ % 2 == 1:
                        yield

            def stageB2(qt):
                n_s = qt + 1
                n = 128 * n_s
                nchk = (n_s + 3) // 4
                sc, b_sc = scs[qt % 2], b_scs[qt % 2]
                if n > KTOP:
                    P.op("dve", lambda e, n=n: e.tensor_reduce(out=small[:, 0:1], in_=sc[:, 0:n], axis=AX.X, op=ALU.max),
                         reads=[b_sc], writes=[b_small])
                    P.op("dve", lambda e, n=n: e.tensor_reduce(out=small[:, 1:2], in_=sc[:, 0:n - 128], axis=AX.X, op=ALU.min),
                         reads=[b_sc], writes=[b_small])
                    P.op("dve", lambda e: e.tensor_tensor(out=small[:, 1:2], in0=small[:, 1:2], in1=dmin[qt % 2][:, 0:1], op=ALU.min),
                         reads=[b_small, b_dmin[qt % 2]], writes=[b_small])
                    P.op("dve", lambda e: e.tensor_tensor(out=small[:, 3:4], in0=small[:, 0:1], in1=small[:, 1:2], op=ALU.subtract),
                         reads=[b_small], writes=[b_small])
                    P.op("dve", lambda e: e.tensor_scalar(out=small[:, 3:4], in0=small[:, 3:4], scalar1=1.0001, scalar2=1e-6,
                                                          op0=ALU.mult, op1=ALU.add), reads=[b_small], writes=[b_small])
                    P.op("dve", lambda e: e.tensor_scalar(out=hw[:], in0=cst[:, C_POW:C_POW + NIT + 1], scalar1=small[:, 3:4], scalar2=None,
                                                          op0=ALU.mult), reads=[b_small, b_cst], writes=[b_hw])
                    P.op("dve", lambda e: e.tensor_tensor(out=small[:, 4:5], in0=small[:, 1:2], in1=hw[:, 0:1], op=ALU.add),
                         reads=[b_small, b_hw], writes=[b_small])
                    yield
                    for it in range(NIT):
                        P.op("dve", lambda e, n=n: e.tensor_scalar(out=junk[:, 0:n], in0=sc[:, 0:n], scalar1=small[:, 4:5], scalar2=None,
                                                                   op0=ALU.is_ge, op1=ALU.add, accum_out=small[:, 5:6]),
                             reads=[b_sc, b_small], writes=[b_junk, b_small])
                        P.op("dve", lambda e, it=it: e.tensor_scalar(out=small[:, 6:7], in0=small[:, 5:6], scalar1=float(KTOP) - 0.5,
                                                                     scalar2=hw[:, it:it + 1], op0=ALU.is_ge, op1=ALU.mult),
                             reads=[b_small, b_hw], writes=[b_small])
                        P.op("dve", lambda e, it=it: e.scalar_tensor_tensor(out=small[:, 4:5], in0=small[:, 6:7], scalar=hw[:, it + 1:it + 2],
                                                                            in1=small[:, 4:5], op0=ALU.subtract, op1=ALU.add),
                             reads=[b_small, b_hw], writes=[b_small])
                        yield
                    P.op("dve", lambda e: e.scalar_tensor_tensor(out=small[:, 7:8], in0=hw[:, NIT:NIT + 1], scalar=-2.0, in1=small[:, 4:5],
                                                                 op0=ALU.mult, op1=ALU.add),
                         reads=[b_small, b_hw], writes=[b_small])
                else:
                    P.op("dve", lambda e: e.memset(small[:, 7:8], -1.0e29), writes=[b_small])
                P.op("dve", lambda e, n=n: e.tensor_scalar(out=junk[:, 0:n], in0=sc[:, 0:n], scalar1=small[:, 7:8], scalar2=None, op0=ALU.is_ge),
                     reads=[b_sc, b_small], writes=[b_junk])
                yield

            def stageB2b(qt):
                n_s = qt + 1
                nchk = (n_s + 3) // 4
                ms = 0
                for c in range(nchk):
                    g = min(4, n_s - 4 * c)
                    tb = 7 if c % 2 == 0 else 2
                    tbank = banks[tb][:, 0:256].bitcast(BF16)
                    for j in range(g):
                        st = 4 * c + j
                        P.op("pe", lambda e, tbank=tbank, j=j, st=st: e.transpose(
                            out=tbank[:, j * 128:(j + 1) * 128], in_=junk[:, st * 128:(st + 1) * 128], identity=identb[:]),
                            reads=[b_junk, b_cb], writes=[bbuf[tb]])
                    P.op("act", lambda e, tbank=tbank, ms=ms, c=c, g=g: e.activation(
                        out=MT[ms][:, 4 * c:4 * c + g, :].rearrange("p a b -> p (a b)"), in_=tbank[:, 0:g * 128], func=AF.Identity,
                        scale=MASK_NEG, bias=negb[:, 0:1]),
                        reads=[bbuf[tb], b_cb], pw=[b_MT[ms]])
                    yield

            def stageC(qt):
                qc, qo = qt // 2, (qt % 2) * 128
                if qt % 2 == 0:
                    for hh_ in range(2):
                        P.op("sp", lambda e, qc=qc, hh_=hh_: e.dma_start(
                            out=qch[hh_ * 64:(hh_ + 1) * 64, :, hh_, :],
                            in_=QA_T[:, hh_ * 64:(hh_ + 1) * 64, qc * 256:(qc + 1) * 256].rearrange("c p t -> p c t")),
                            reads=[b_QA], writes=[b_qch], dma=True)
                n_s = qt + 1
                nchk = (n_s + 3) // 4
                ms = 0
                ot = qt % 2
                SK = 2
                items = [(h, c) for h in range(12) for c in range(nchk)]
                e0 = cnts["e"]
                cnts["e"] += len(items)
                o0 = cnts["o"]
                cnts["o"] += 12

                def qk_part(idx):
                    h, c = items[idx]
                    hp, hh = h // 2, h % 2
                    rows = slice(hh * 64, hh * 64 + 64)
                    g = min(4, n_s - 4 * c)
                    sbk = 3 + (e0 + idx) % 2
                    es_ = (e0 + idx) % 3
                    for j in range(g):
                        st = 4 * c + j
                        P.op("pe", lambda e, sbk=sbk, j=j, st=st, hh=hh, hp=hp: e.matmul(
                            banks[sbk][:, j * 128:(j + 1) * 128], lhsT=KA[:, hp, st * 128:(st + 1) * 128],
                            rhs=qch[:, hp, hh, qo:qo + 128], start=True, stop=False),
                            reads=[b_KAs, b_qch], writes=[bbuf[sbk]])
                        P.op("pe", lambda e, sbk=sbk, j=j, st=st: e.matmul(
                            banks[sbk][:, j * 128:(j + 1) * 128], lhsT=identb[:], rhs=MT[ms][:, st, :], start=False, stop=True),
                            reads=[b_cb, b_MT[ms]], writes=[bbuf[sbk]])
                    P.op("act", lambda e, sbk=sbk, es_=es_, g=g: e.activation(
                        out=esb[es_][:, 0:g * 128], in_=banks[sbk][:, 0:g * 128], func=AF.Exp, scale=0.125),
                        reads=[bbuf[sbk]], writes=[b_esb[es_]])

                def pv_part(idx):
                    h, c = items[idx]
                    hp, hh = h // 2, h % 2
                    g = min(4, n_s - 4 * c)
                    es_ = (e0 + idx) % 3
                    ob = 5 + (o0 + h) % 2
                    vofs = hp * 192 + hh * 64
                    for j in range(g):
                        st = 4 * c + j
                        P.op("pe", lambda e, ob=ob, es_=es_, j=j, st=st, vofs=vofs: e.matmul(
                            banks[ob][:, 0:128], lhsT=VA[:, st, vofs:vofs + 128], rhs=esb[es_][:, j * 128:(j + 1) * 128],
                            start=(st == 0), stop=(st == n_s - 1)),
                            reads=[b_VAs, b_esb[es_]], writes=[bbuf[ob]])
                    if c != nchk - 1:
                        return
                    num = slice(0, 64) if hh == 0 else slice(64, 128)
                    den = slice(64, 128) if hh == 0 else slice(0, 64)
                    rd = h % 2
                    P.op("act", lambda e, ob=ob, den=den, rd=rd: e.activation(out=rden[rd][den, :], in_=banks[ob][den, 0:128], func=AF.Ln),
                         reads=[bbuf[ob]], writes=[b_rden[rd]])
                    P.op("act", lambda e, den=den, rd=rd: e.activation(out=rden[rd][den, :], in_=rden[rd][den, :], func=AF.Exp, scale=-1.0),
                         reads=[b_rden[rd]], writes=[b_rden[rd]])
                    P.op("dve", lambda e, ob=ob, num=num, den=den, rd=rd, hp=hp: e.tensor_tensor(
                        out=otile[ot][num, hp, :], in0=banks[ob][num, 0:128], in1=rden[rd][den, :], op=ALU.mult),
                        reads=[bbuf[ob], b_rden[rd]], pw=[b_otile[ot]])

                for i_ in range(len(items) + SK):
                    if i_ < len(items):
                        qk_part(i_)
                    if i_ >= SK:
                        pv_part(i_ - SK)
                    yield
                P.op("sp", lambda e, ot=ot, qt=qt: e.dma_start(
                    out=OA_T[:, :, qt * 128:(qt + 1) * 128].rearrange("c p t -> p c t"), in_=otile[ot][:]),
                    reads=[b_otile[ot]], pw=[b_OA], dma=True)
                yield

            def interleave(*gens):
                live = [g for g in gens if g is not None]
                while live:
                    nxt = []
                    for g in live:
                        try:
                            next(g)
                            nxt.append(g)
                        except StopIteration:
                            pass
                    live = nxt

            interleave(stageB1(0))
            interleave(stageB2(0), stageB1(1) if NT > 1 else None)
            interleave(stageB2b(0))
            for qt in range(NT):
                interleave(stageC(qt),
                           stageB2(qt + 1) if qt + 1 < NT else None,
                           stageB1(qt + 2) if qt + 2 < NT else None)
                if qt + 1 < NT:
                    interleave(stageB2b(qt + 1))

        barrier(5)
        with ExitStack() as s5:
            acc = [sb("acc%d" % i, [128, S], F32, s5) for i in range(4)]
            b_acc = [Buf("acc%d" % i) for i in range(4)]
            QB = sb("QBg", [128, 2, S], BF16, s5)
            KB = sb("KBg", [128, 2, S], BF16, s5)
            VB = sb("VBg", [128, NT, 2 * 192], BF16, s5)
            b_QBs, b_KBs, b_VBs = Buf("QBs"), Buf("KBs"), Buf("VBs")
            esb5 = [sb("e5_%d" % i, [128, 512], BF16, s5) for i in range(3)]
            psb5 = [sb("p5_%d" % i, [128, 512], BF16, s5) for i in range(3)]
            b_e5 = [Buf("e5_%d" % i) for i in range(3)]
            b_p5 = [Buf("p5_%d" % i) for i in range(3)]
            obt = sb("obt", [128, 2, S], BF16, s5)
            b_obt = Buf("obt")
            rd5 = sb("rd5", [128, S], F32, s5)
            b_rd5 = Buf("rd5")
            ec = 0
            oc = 0
            for g, dil in enumerate((1, 4, 16)):
                L = S // dil
                nsub = L // 128
                for c in range(2):
                    P.op("sp", lambda e, c=c, g=g: e.dma_start(out=QB[:, c, :], in_=QB_T[2 * g + c]), reads=[b_QB], pw=[b_QBs], dma=True)
                    P.op("sp", lambda e, c=c, g=g: e.dma_start(out=KB[:, c, :], in_=KB_T[2 * g + c]), reads=[b_KB], pw=[b_KBs], dma=True)
                for r in range(dil):
                    for iu0 in range(0, nsub, 8):
                        iu1 = min(nsub, iu0 + 8)
                        base = r + dil * 128 * iu0
                        cnt_rows = (iu1 - iu0) * 128
                        srcv = VB_S[base:base + dil * (cnt_rows - 1) + 1:dil, g * 384:(g + 1) * 384]
                        P.op("sp", lambda e, r=r, iu0=iu0, iu1=iu1, srcv=srcv, nsub=nsub: e.dma_start(
                            out=VB[:, r * nsub + iu0:r * nsub + iu1, :], in_=srcv.rearrange("(t p) c -> p t c", p=128)),
                            reads=[b_VB], pw=[b_VBs], dma=True)
                for hl in range(4):
                    hp, hh = hl // 2, hl % 2
                    rows = slice(hh * 64, hh * 64 + 64)
                    vofs = hp * 192 + hh * 64
                    for r in range(dil):
                        def tok(iu, r=r, dil=dil):
                            b0 = r + dil * 128 * iu
                            return slice(b0, b0 + dil * 127 + 1, dil)
                        blocks = [(0, 0, "D")]
                        for iu in range(1, nsub):
                            blocks.append((iu - 1, iu, "P"))
                            blocks.append((iu, iu, "D"))
                        cur_ob = None
                        for b0 in range(0, len(blocks), 4):
                            chunk = blocks[b0:b0 + 4]
                            gsz = len(chunk)
                            sbk = ec % 3
                            es_ = ec % 3
                            ec += 1
                            for j, (st, iu, kind) in enumerate(chunk):
                                P.op("pe", lambda e, sbk=sbk, j=j, st=st, iu=iu, rows=rows, hp=hp, tok=tok: e.matmul(
                                    banks[sbk][:, j * 128:(j + 1) * 128], lhsT=KB[rows, hp, tok(st)], rhs=QB[rows, hp, tok(iu)],
                                    start=True, stop=True), reads=[b_KBs, b_QBs], writes=[bbuf[sbk]])
                            P.op("act", lambda e, sbk=sbk, es_=es_, gsz=gsz: e.activation(
                                out=esb5[es_][:, 0:gsz * 128], in_=banks[sbk][:, 0:gsz * 128], func=AF.Exp, scale=0.125),
                                reads=[bbuf[sbk]], writes=[b_e5[es_]])
                            P.op("dve", lambda e, es_=es_, gsz=gsz: e.tensor_tensor(
                                out=psb5[es_][:, 0:gsz * 128], in0=esb5[es_][:, 0:gsz * 128], in1=mb4[:, 0:gsz * 128], op=ALU.mult),
                                reads=[b_e5[es_], b_cb], writes=[b_p5[es_]])
                            for j, (st, iu, kind) in enumerate(chunk):
                                if iu % 4 == 0 and kind == ("D" if iu == 0 else "P"):
                                    cur_ob = 4 + oc % 4
                                    oc += 1
                                ob = cur_ob
                                jo = iu % 4
                                P.op("pe", lambda e, ob=ob, jo=jo, es_=es_, j=j, st=st, r=r, nsub=nsub, vofs=vofs, kind=kind, iu=iu: e.matmul(
                                    banks[ob][:, jo * 128:(jo + 1) * 128], lhsT=VB[:, r * nsub + st, vofs:vofs + 128],
                                    rhs=psb5[es_][:, j * 128:(j + 1) * 128],
                                    start=(kind == "P" or iu == 0), stop=(kind == "D")),
                                    reads=[b_VBs, b_p5[es_]], writes=[bbuf[ob]])
                                if kind == "D" and (iu % 4 == 3 or iu == nsub - 1):
                                    iu_lo = iu - (iu % 4)
                                    nt_ = iu - iu_lo + 1
                                    b00 = r + dil * 128 * iu_lo
                                    dst = acc[hl][:, b00:b00 + dil * (nt_ * 128 - 1) + 1:dil]
                                    if g == 0:
                                        P.op("dve", lambda e, ob=ob, nt_=nt_, dst=dst: e.tensor_copy(out=dst, in_=banks[ob][:, 0:nt_ * 128]),
                                             reads=[bbuf[ob]], writes=[b_acc[hl]])
                                    else:
                                        P.op("dve", lambda e, ob=ob, nt_=nt_, dst=dst: e.tensor_tensor(
                                            out=dst, in0=banks[ob][:, 0:nt_ * 128], in1=dst, op=ALU.add),
                                            reads=[bbuf[ob], b_acc[hl]], writes=[b_acc[hl]])
            for hl in range(4):
                hp, hh = hl // 2, hl % 2
                num = slice(0, 64) if hh == 0 else slice(64, 128)
                den = slice(64, 128) if hh == 0 else slice(0, 64)
                P.op("dve", lambda e, hl=hl, den=den, num=num: e.reciprocal(out=rd5[num, :], in_=acc[hl][den, :]),
                     reads=[b_acc[hl]], writes=[b_rd5])
                P.op("dve", lambda e, hl=hl, num=num, den=den, hp=hp: e.tensor_tensor(
                    out=obt[num, hp, :], in0=acc[hl][num, :], in1=rd5[num, :], op=ALU.mult),
                    reads=[b_acc[hl], b_rd5], writes=[b_obt])
            for c in range(2):
                P.op("sp", lambda e, c=c: e.dma_start(out=OB_T[c], in_=obt[:, c, :]), reads=[b_obt], pw=[b_OB], dma=True)

        def load_w(dst2d, src2d, ncols, stg, b_stg, b_dst, cw=2048, engs=("pool", "dve", "act")):
            for i, c0 in enumerate(range(0, ncols, cw)):
                c1 = min(ncols, c0 + cw)
                sl = i % len(stg)
                P.op("sp", lambda e, c0=c0, c1=c1, sl=sl: e.dma_start(out=stg[sl][:, 0:c1 - c0], in_=src2d[:, c0:c1]),
                     writes=[b_stg[sl]], dma=True)
                ceng = engs[i % len(engs)]
                if ceng == "act":
                    P.op("act", lambda e, c0=c0, c1=c1, sl=sl: e.activation(out=dst2d[:, c0:c1], in_=stg[sl][:, 0:c1 - c0], func=AF.Copy),
                         reads=[b_stg[sl]], pw=[b_dst])
                else:
                    P.op(ceng, lambda e, c0=c0, c1=c1, sl=sl: e.tensor_copy(out=dst2d[:, c0:c1], in_=stg[sl][:, 0:c1 - c0]),
                         reads=[b_stg[sl]], pw=[b_dst])

        def layer_norm(src, b_src, dst, b_dst, gbc, bbc, b_ln, stats, mv, b_st):
            for hf in range(2):
                P.op("dve", lambda e, hf=hf: e.bn_stats(out=stats[:, hf * 6:(hf + 1) * 6], in_=src[:, hf * 512:(hf + 1) * 512]),
                     reads=[b_src], writes=[b_st])
            P.op("dve", lambda e: e.bn_aggr(out=mv[:, 0:2], in_=stats[:, 0:12]), reads=[b_st], writes=[b_st])
            P.op("dve", lambda e: e.tensor_scalar(out=mv[:, 2:3], in0=mv[:, 1:2], scalar1=LN_EPS, scalar2=None, op0=ALU.add),
                 reads=[b_st], writes=[b_st])
            P.op("act", lambda e: e.activation(out=mv[:, 2:3], in_=mv[:, 2:3], func=AF.Sqrt), reads=[b_st], writes=[b_st])
            P.op("dve", lambda e: e.reciprocal(out=mv[:, 3:4], in_=mv[:, 2:3]), reads=[b_st], writes=[b_st])
            P.op("dve", lambda e: e.tensor_scalar(out=dst[:], in0=src[:], scalar1=mv[:, 0:1], scalar2=mv[:, 3:4],
                                                  op0=ALU.subtract, op1=ALU.mult), reads=[b_src, b_st], writes=[b_dst])
            P.op("pool", lambda e: e.tensor_tensor(out=dst[:], in0=dst[:], in1=gbc[:], op=ALU.mult), reads=[b_dst, b_ln], writes=[b_dst])
            P.op("pool", lambda e: e.tensor_tensor(out=dst[:], in0=dst[:], in1=bbc[:], op=ALU.add), reads=[b_dst, b_ln], writes=[b_dst])

        barrier(6)
        with ExitStack() as s6:
            stg = [sb("stg%d" % i, [128, 2048], F32, s6) for i in range(3)]
            b_stg = [Buf("stg%d" % i) for i in range(3)]
            WA = sb("WA", [128, 6 * 1024], BF16, s6)
            WB = sb("WB", [128, 2 * 1024], BF16, s6)
            WO = sb("WO", [128, 8 * 1024], BF16, s6)
            b_WA, b_WB, b_WO = Buf("WA"), Buf("WB"), Buf("WO")
            load_w(WA, wa_d, 6 * 1024, stg, b_stg, b_WA)
            load_w(WB, wb_d, 2 * 1024, stg, b_stg, b_WB)
            load_w(WO, wo_d, 8 * 1024, stg, b_stg, b_WO)
            gbc = sb("gbc1", [128, D], F32, s6)
            bbc = sb("bbc1", [128, D], F32, s6)
            b_ln1 = Buf("ln1")
            P.op("sp", lambda e, gbc=gbc: e.dma_start(out=gbc[:], in_=ln_d[0:1, :].partition_broadcast(128)), writes=[b_ln1], dma=True)
            P.op("sp", lambda e, bbc=bbc: e.dma_start(out=bbc[:], in_=ln_d[1:2, :].partition_broadcast(128)), writes=[b_ln1], dma=True)
            oac = [sb("oac%d" % i, [128, 6, 512], BF16, s6) for i in range(2)]
            obc = [sb("obc%d" % i, [128, 2, 512], BF16, s6) for i in range(2)]
            gch = [sb("gch%d" % i, [128, 16, 512], BF16, s6) for i in range(2)]
            b_oac = [Buf("oac%d" % i) for i in range(2)]
            b_obc = [Buf("obc%d" % i) for i in range(2)]
            b_gch = [Buf("gch%d" % i) for i in range(2)]
            mg = [sb("mg%d" % i, [128, 8, 512], BF16, s6) for i in range(2)]
            b_mg = [Buf("mg%d" % i) for i in range(2)]
            mt1 = [sb("mt1_%d" % i, [128, 512], F32, s6) for i in range(2)]
            b_mt1 = [Buf("mt1_%d" % i) for i in range(2)]
            xt6 = [sb("xt6_%d" % i, [128, D], F32, s6) for i in range(2)]
            b_xt6 = [Buf("xt6_%d" % i) for i in range(2)]
            hp6 = [sb("hp6_%d" % i, [128, D], F32, s6) for i in range(2)]
            b_hp6 = [Buf("hp6_%d" % i) for i in range(2)]
            h1s = [sb("h1s_%d" % i, [128, D], F32, s6) for i in range(2)]
            b_h1s = [Buf("h1s_%d" % i) for i in range(2)]
            h1t = [sb("h1t_%d" % i, [128, 8, 128], BF16, s6) for i in range(2)]
            b_h1t = [Buf("h1t_%d" % i) for i in range(2)]
            stats = sb("stats6", [128, 12], F32, s6)
            mv = sb("mv6", [128, 4], F32, s6)
            b_st = Buf("st6")
            mc = 0
            tcnt = 0
            pend6 = None

            def p6a_back(tt, ts_):
                for hf in range(2):
                    bk = 6 + hf
                    for j in range(4):
                        kc = hf * 4 + j
                        P.op("pe", lambda e, bk=bk, j=j, kc=kc: e.transpose(
                            out=banks[bk][:, j * 128:(j + 1) * 128], in_=h1s[ts_][:, kc * 128:(kc + 1) * 128], identity=ident_f),
                            reads=[b_h1s[ts_], b_cst], writes=[bbuf[bk]])
                    P.op("act", lambda e, bk=bk, hf=hf: e.activation(
                        out=h1t[ts_][:, hf * 4:hf * 4 + 4, :], in_=banks[bk][:].rearrange("p (a b) -> p a b", a=4), func=AF.Copy),
                        reads=[bbuf[bk]], writes=[b_h1t[ts_]])
                P.op("sp", lambda e: e.dma_start(
                    out=H1T[:, :, tt * 128:(tt + 1) * 128].rearrange("c p t -> p c t"), in_=h1t[ts_][:]),
                    reads=[b_h1t[ts_]], pw=[b_H1T], dma=True)

            for n in range(NCH):
                cs = n % 2
                tsl = slice(n * 512, (n + 1) * 512)
                P.op("sp", lambda e, cs=cs, tsl=tsl: e.dma_start(out=oac[cs][:], in_=OA_T[:, :, tsl].rearrange("c p t -> p c t")),
                     reads=[b_OA], writes=[b_oac[cs]], dma=True)
                P.op("sp", lambda e, cs=cs, tsl=tsl: e.dma_start(out=obc[cs][:], in_=OB_T[:, :, tsl].rearrange("c p t -> p c t")),
                     reads=[b_OB], writes=[b_obc[cs]], dma=True)
                P.op("sp", lambda e, cs=cs, tsl=tsl: e.dma_start(out=gch[cs][:], in_=G_T[:, :, tsl].rearrange("c p t -> p c t")),
                     reads=[b_G], writes=[b_gch[cs]], dma=True)
                for oc_ in range(8):
                    ba = (mc % 2) * 2
                    bb = ba + 1
                    m1 = mc % 2
                    mc += 1
                    for kc in range(6):
                        P.op("pe", lambda e, ba=ba, kc=kc, oc_=oc_, cs=cs: e.matmul(
                            banks[ba][:], lhsT=WA[:, kc * 1024 + oc_ * 128:kc * 1024 + oc_ * 128 + 128], rhs=oac[cs][:, kc, :],
                            start=(kc == 0), stop=(kc == 5)), reads=[b_WA, b_oac[cs]], writes=[bbuf[ba]])
                    for kc in range(2):
                        P.op("pe", lambda e, bb=bb, kc=kc, oc_=oc_, cs=cs: e.matmul(
                            banks[bb][:], lhsT=WB[:, kc * 1024 + oc_ * 128:kc * 1024 + oc_ * 128 + 128], rhs=obc[cs][:, kc, :],
                            start=(kc == 0), stop=(kc == 1)), reads=[b_WB, b_obc[cs]], writes=[bbuf[bb]])
                    P.op("dve", lambda e, ba=ba, m1=m1, oc_=oc_, cs=cs: e.tensor_tensor(
                        out=mt1[m1][:], in0=banks[ba][:], in1=gch[cs][:, oc_, :], op=ALU.mult),
                        reads=[bbuf[ba], b_gch[cs]], writes=[b_mt1[m1]])
                    P.op("dve", lambda e, bb=bb, m1=m1, oc_=oc_, cs=cs: e.tensor_tensor(
                        out=mg[cs][:, oc_, :], in0=banks[bb][:], in1=gch[cs][:, 8 + oc_, :], op=ALU.mult),
                        reads=[bbuf[bb], b_gch[cs]], writes=[b_mg[cs]])
                    P.op("pool", lambda e, m1=m1, oc_=oc_, cs=cs: e.tensor_tensor(
                        out=mg[cs][:, oc_, :], in0=mg[cs][:, oc_, :], in1=mt1[m1][:], op=ALU.add),
                        reads=[b_mt1[m1], b_mg[cs]], writes=[b_mg[cs]])
                for t4 in range(4):
                    tt = n * 4 + t4
                    ts_ = tcnt % 2
                    tcnt += 1
                    P.op("sp", lambda e, ts_=ts_, tt=tt: e.dma_start(out=xt6[ts_][:], in_=x_d[tt * 128:(tt + 1) * 128, :]),
                         writes=[b_xt6[ts_]], dma=True)
                    for hf in range(2):
                        bk = (4 if ts_ == 0 else 2) + hf
                        for kc in range(8):
                            P.op("pe", lambda e, bk=bk, kc=kc, t4=t4, hf=hf, cs=cs: e.matmul(
                                banks[bk][:], lhsT=mg[cs][:, kc, t4 * 128:(t4 + 1) * 128],
                                rhs=WO[:, kc * 1024 + hf * 512:kc * 1024 + hf * 512 + 512], start=(kc == 0), stop=(kc == 7)),
                                reads=[b_mg[cs], b_WO], writes=[bbuf[bk]])
                        P.op("dve", lambda e, bk=bk, hf=hf, ts_=ts_: e.scalar_tensor_tensor(
                            out=hp6[ts_][:, hf * 512:(hf + 1) * 512], in0=xt6[ts_][:, hf * 512:(hf + 1) * 512], scalar=ALPHA,
                            in1=banks[bk][:], op0=ALU.mult, op1=ALU.add),
                            reads=[b_xt6[ts_], bbuf[bk]], writes=[b_hp6[ts_]])
                    if pend6 is not None:
                        p6a_back(*pend6)
                    layer_norm(hp6[ts_], b_hp6[ts_], h1s[ts_], b_h1s[ts_], gbc, bbc, b_ln1, stats, mv, b_st)
                    P.op("sp", lambda e, ts_=ts_, tt=tt: e.dma_start(out=H1[tt * 128:(tt + 1) * 128, :], in_=h1s[ts_][:]),
                         reads=[b_h1s[ts_]], pw=[b_H1], dma=True)
                    pend6 = (tt, ts_)
            if pend6 is not None:
                p6a_back(*pend6)

        barrier(7)
        with ExitStack() as s7:
            stg = [sb("stgb%d" % i, [128, 1024], F32, s7) for i in range(2)]
            b_stg = [Buf("stgb%d" % i) for i in range(2)]
            WG = sb("WG", [128, 8 * FFN], BF16, s7)
            WU = sb("WU", [128, 8 * FFN], BF16, s7)
            WD = sb("WD", [128, NHC * 1024], BF16, s7)
            b_WG, b_WU, b_WD = Buf("WG"), Buf("WU"), Buf("WD")
            load_w(WG, wg_d, 8 * FFN, stg, b_stg, b_WG, 1024)
            load_w(WU, wu_d, 8 * FFN, stg, b_stg, b_WU, 1024)
            load_w(WD, wd_d, NHC * 1024, stg, b_stg, b_WD, 1024, engs=("pool",))
            gbc = sb("gbc2", [128, D], F32, s7)
            bbc = sb("bbc2", [128, D], F32, s7)
            b_ln2 = Buf("ln2")
            P.op("sp", lambda e, gbc=gbc: e.dma_start(out=gbc[:], in_=ln_d[2:3, :].partition_broadcast(128)), writes=[b_ln2], dma=True)
            P.op("sp", lambda e, bbc=bbc: e.dma_start(out=bbc[:], in_=ln_d[3:4, :].partition_broadcast(128)), writes=[b_ln2], dma=True)
            hch = [sb("hch%d" % i, [128, 8, 512], BF16, s7) for i in range(1)]
            b_hch = [Buf("hch%d" % i) for i in range(1)]
            hid = [sb("hid%d" % i, [128, NHC, 512], BF16, s7) for i in range(1)]
            b_hid = [Buf("hid%d" % i) for i in range(1)]
            sg = [sb("sg%d" % i, [128, 512], F32, s7) for i in range(2)]
            b_sg = [Buf("sg%d" % i) for i in range(2)]
            h1r = [sb("h1r%d" % i, [128, D], F32, s7) for i in range(2)]
            b_h1r = [Buf("h1r%d" % i) for i in range(2)]
            hp7 = [sb("hp7_%d" % i, [128, D], F32, s7) for i in range(2)]
            b_hp7 = [Buf("hp7_%d" % i) for i in range(2)]
            o7, b_o7 = hp7, b_hp7
            stats = sb("stats7", [128, 12], F32, s7)
            mv = sb("mv7", [128, 4], F32, s7)
            b_st = Buf("st7")
            gc_ = 0
            tcnt = 0
            for n in range(NCH):
                cs = 0
                tsl = slice(n * 512, (n + 1) * 512)
                P.op("sp", lambda e, cs=cs, tsl=tsl: e.dma_start(out=hch[cs][:], in_=H1T[:, :, tsl].rearrange("c p t -> p c t")),
                     reads=[b_H1T], writes=[b_hch[cs]], dma=True)
                for hc in range(NHC):
                    bg_ = (gc_ % 2) * 2
                    bu_ = bg_ + 1
                    s1 = gc_ % 2
                    gc_ += 1
                    for kc in range(8):
                        P.op("pe", lambda e, bg_=bg_, kc=kc, hc=hc, cs=cs: e.matmul(
                            banks[bg_][:], lhsT=WG[:, kc * FFN + hc * 128:kc * FFN + hc * 128 + 128], rhs=hch[cs][:, kc, :],
                            start=(kc == 0), stop=(kc == 7)), reads=[b_WG, b_hch[cs]], writes=[bbuf[bg_]])
                    for kc in range(8):
                        P.op("pe", lambda e, bu_=bu_, kc=kc, hc=hc, cs=cs: e.matmul(
                            banks[bu_][:], lhsT=WU[:, kc * FFN + hc * 128:kc * FFN + hc * 128 + 128], rhs=hch[cs][:, kc, :],
                            start=(kc == 0), stop=(kc == 7)), reads=[b_WU, b_hch[cs]], writes=[bbuf[bu_]])
                    P.op("act", lambda e, bg_=bg_, s1=s1: e.activation(out=sg[s1][:], in_=banks[bg_][:], func=AF.Silu),
                         reads=[bbuf[bg_]], writes=[b_sg[s1]])
                    P.op("dve", lambda e, bu_=bu_, s1=s1, hc=hc, cs=cs: e.tensor_tensor(
                        out=hid[cs][:, hc, :], in0=banks[bu_][:], in1=sg[s1][:], op=ALU.mult),
                        reads=[bbuf[bu_], b_sg[s1]], writes=[b_hid[cs]])
                for t4 in range(4):
                    tt = n * 4 + t4
                    ts_ = tcnt % 2
                    tcnt += 1
                    P.op("sp", lambda e, ts_=ts_, tt=tt: e.dma_start(out=h1r[ts_][:], in_=H1[tt * 128:(tt + 1) * 128, :]),
                         reads=[b_H1], writes=[b_h1r[ts_]], dma=True)
                    for hf in range(2):
                        bk = 4 + (tcnt % 2) * 2 + hf
                        for hc in range(NHC):
                            P.op("pe", lambda e, bk=bk, hc=hc, t4=t4, hf=hf, cs=cs: e.matmul(
                                banks[bk][:], lhsT=hid[cs][:, hc, t4 * 128:(t4 + 1) * 128],
                                rhs=WD[:, hc * 1024 + hf * 512:hc * 1024 + hf * 512 + 512], start=(hc == 0), stop=(hc == NHC - 1)),
                                reads=[b_hid[cs], b_WD], writes=[bbuf[bk]])
                        P.op("dve", lambda e, bk=bk, hf=hf, ts_=ts_: e.scalar_tensor_tensor(
                            out=hp7[ts_][:, hf * 512:(hf + 1) * 512], in0=h1r[ts_][:, hf * 512:(hf + 1) * 512], scalar=ALPHA,
                            in1=banks[bk][:], op0=ALU.mult, op1=ALU.add),
                            reads=[b_h1r[ts_], bbuf[bk]], writes=[b_hp7[ts_]])
                    layer_norm(hp7[ts_], b_hp7[ts_], o7[ts_], b_o7[ts_], gbc, bbc, b_ln2, stats, mv, b_st)
                    P.op("sp", lambda e, ts_=ts_, tt=tt: e.dma_start(out=out_d[tt * 128:(tt + 1) * 128, :], in_=o7[ts_][:]),
                         reads=[b_o7[ts_]], writes=[Buf("outd")], dma=True, final=True)

        with ExitStack() as ee:
            block = ee.enter_context(nc.Block())
            P.emit(nc, ee, block)
    return nc


def _consts():
    c = np.zeros((128, NCONST), np.float32)
    idx = np.arange(128)
    c[:, C_ID:C_ID + 128] = np.eye(128, dtype=np.float32)
    perm = np.zeros((128, 128), np.float32)
    for m in range(128):
        mm = m % 64
        if mm < 8:
            perm[m + 8, m] = 1.0
        elif mm < 16:
            perm[m - 8, m] = 1.0
    c[:, C_PERM:C_PERM + 128] = perm
    qq, ss = idx[:, None], idx[None, :]
    c[:, C_TRIM:C_TRIM + 128] = np.where(ss <= qq, 0.0, -BIG)
    c[:, C_TRIP:C_TRIP + 128] = np.where(ss <= qq, 0.0, BIG)
    s_, u_ = idx[:, None], idx[None, :]
    c[:, C_MD:C_MD + 128] = (s_ <= u_).astype(np.float32)
    c[:, C_MP:C_MP + 128] = (s_ >= u_).astype(np.float32)
    inv_freq = (ROPE_THETA ** (-np.arange(0, 16, 2, dtype=np.float32) / 16.0)).astype(np.float32)
    for m in range(128):
        mm = m % 64
        if mm < 8:
            c[m, C_ROPE + 0] = -inv_freq[mm] / (2 * np.pi)
            c[m, C_ROPE + 1] = inv_freq[mm] / (2 * np.pi)
        elif mm < 16:
            c[m, C_ROPE + 0] = inv_freq[mm - 8] / (2 * np.pi)
            c[m, C_ROPE + 1] = inv_freq[mm - 8] / (2 * np.pi)
    c[:, C_ROPE + 2] = 0.25
    for i in range(NIT + 1):
        c[:, C_POW + i] = 2.0 ** (-(i + 1))
    return c


def _prep_weights(w_in, b_gate, w_branch_a, w_branch_b, w_out, ln1_g, ln1_b, w_ffn_gate, w_ffn_up, w_ffn_down, ln2_g, ln2_b):
    w_in = np.asarray(w_in, np.float32)[0]
    cols = []
    for base in (0, 768, 2304, 3072):
        for c in range(6):
            cols.append(np.arange(base + c * 128, base + (c + 1) * 128))
    for c in range(4):
        cols.append(np.arange(4608 + c * 128, 4608 + (c + 1) * 128))
    ki = np.arange(5120, 5184)
    cols.append(np.concatenate([ki, ki]))
    for c in range(16):
        cols.append(np.arange(5192 + c * 128, 5192 + (c + 1) * 128))
    w_fm = np.stack([w_in[:, cc].reshape(8, 128, 128).transpose(1, 0, 2).reshape(128, 1024) for cc in cols], 0)
    tmc = np.concatenate([np.arange(1536, 1920), np.arange(1920, 2304), np.arange(5184, 5192),
                          np.arange(3840, 4224), np.arange(4224, 4608)])
    w_tm = w_in[:, tmc].reshape(8, 128, 1544).transpose(1, 0, 2).reshape(128, 8 * 1544)

    def kmaj(w, nk):
        w = np.asarray(w, np.float32)[0]
        return np.ascontiguousarray(w.reshape(nk, 128, w.shape[1]).transpose(1, 0, 2).reshape(128, nk * w.shape[1]))

    d = {
        "w_fm": np.ascontiguousarray(w_fm),
        "w_tm": np.ascontiguousarray(w_tm),
        "bg": np.ascontiguousarray(np.asarray(b_gate, np.float32)[0].reshape(16, 128).T),
        "wa": kmaj(w_branch_a, 6),
        "wb": kmaj(w_branch_b, 2),
        "wo": kmaj(w_out, 8),
        "wg": kmaj(w_ffn_gate, 8),
        "wu": kmaj(w_ffn_up, 8),
        "wd": kmaj(w_ffn_down, NHC),
        "ln": np.ascontiguousarray(np.stack([np.asarray(a, np.float32)[0] for a in (ln1_g, ln1_b, ln2_g, ln2_b)], 0)),
        "cst": _consts(),
    }
    return d


_NC_CACHE = {}


def run(x, positions, weights, S, ktop):
    B = x.shape[0]
    key = (S, ktop)
    if key not in _NC_CACHE:
        _NC_CACHE[key] = build(S, ktop)
    nc = _NC_CACHE[key]
    wd = _prep_weights(**weights)
    in_maps = []
    for b in range(B):
        m = dict(wd)
        m["x"] = np.ascontiguousarray(np.asarray(x[b], np.float32))
        m["pos"] = np.ascontiguousarray(np.asarray(positions[b], np.int32).reshape(1, S))
        in_maps.append(m)
    res = run_bass_kernel_spmd(nc, in_maps, core_ids=list(range(B)))
    return np.stack([np.asarray(r["out"], np.float32) for r in res.results], 0)


def kernel(x, positions, w_in, b_gate, w_branch_a, w_branch_b, w_out, ln1_g, ln1_b,
           w_ffn_gate, w_ffn_up, w_ffn_down, ln2_g, ln2_b):
    x = np.asarray(x)
    S = x.shape[1]
    weights = dict(w_in=w_in, b_gate=b_gate, w_branch_a=w_branch_a, w_branch_b=w_branch_b, w_out=w_out,
                   ln1_g=ln1_g, ln1_b=ln1_b, w_ffn_gate=w_ffn_gate, w_ffn_up=w_ffn_up, w_ffn_down=w_ffn_down,
                   ln2_g=ln2_g, ln2_b=ln2_b)
    return run(x, np.asarray(positions), weights, S, min(256, S // 4))
```

```python
import math
from contextlib import ExitStack

import numpy as np
import concourse.bass as bass
import concourse.mybir as mybir
from concourse.bass_utils import run_bass_kernel_spmd

F32 = mybir.dt.float32
BF16 = mybir.dt.bfloat16
I32 = mybir.dt.int32
AF = mybir.ActivationFunctionType
ALU = mybir.AluOpType
AX = mybir.AxisListType

D = 1024
HD = 64
NQA = 768
FFN = 2816
NHC = FFN // 128
N_IN = 7240
ROPE_THETA = 500000.0
ALPHA = 2.0 ** 0.25
LN_EPS = 1e-5
NIT = 16
BIG = 1.0e30
MASK_NEG = 30000.0
SAME_ENGINE_SYNC = True
import os
KSTOP = int(os.environ.get('KSTOP', '99'))
KMAX = int(os.environ.get('KMAX', '100000000'))

C_ID, C_PERM, C_TRIM, C_TRIP, C_MD, C_MP, C_ROPE, C_POW = 0, 128, 256, 384, 512, 640, 768, 772
NCONST = C_POW + NIT + 1


class Buf:
    __slots__ = ("name", "W", "Wf", "R", "Rprev", "excl", "last")

    def __init__(self, name, excl=False):
        self.name = name
        self.excl = excl
        self.last = {}
        self.W = []
        self.Wf = []
        self.R = []
        self.Rprev = []


class Op:
    __slots__ = ("eng", "fn", "deps", "dma", "ms", "msidx", "dsem", "dval", "prev_dval", "seq", "cdeps")

    def __init__(self, eng, fn, dma):
        self.eng = eng
        self.fn = fn
        self.deps = []
        self.dma = dma
        self.ms = False
        self.msidx = 0
        self.dsem = None
        self.dval = 0
        self.prev_dval = 0


class Prog:
    ENGS = ("pe", "act", "dve", "pool", "sp")

    def __init__(self, n_dma_sems=24):
        self.ops = {e: [] for e in self.ENGS}
        self.n_dma_sems = n_dma_sems
        self.n_dma = 0
        self.final_dmas = []
        self.phase = Buf("phase")
        self.stopped = False

    def op(self, eng, fn, reads=(), writes=(), pw=(), dma=False, final=False, _barrier=False):
        if self.stopped:
            return None
        self.nops = getattr(self, "nops", 0) + 1
        if self.nops > KMAX:
            self.stopped = True
            return None
        o = Op(eng, fn, dma)
        deps = []
        for b in list(reads) + list(writes) + list(pw):
            if b.excl:
                deps.extend(op_ for e2, op_ in b.last.items() if e2 != eng)
                b.last[eng] = o
        reads = [b for b in reads if not b.excl]
        writes = [b for b in writes if not b.excl]
        pw = [b for b in pw if not b.excl]
        wset = set(id(b) for b in writes) | set(id(b) for b in pw)
        rlist = [b for b in reads if id(b) not in wset]
        if not _barrier:
            rlist = rlist + [self.phase]
        for b in rlist:
            deps.extend(b.W)
        for b in list(writes) + list(pw):
            if b.R:
                b.Rprev = b.R
                b.R = []
                oldW = b.W
                b.W = []
                b.Wf = []
                if any(id(b) == id(x) for x in reads):
                    deps.extend(oldW)
            deps.extend(b.Rprev)
            if any(id(b) == id(x) for x in writes):
                deps.extend(b.W)
            else:
                deps.extend(b.Wf)
        seen = set()
        for d in deps:
            if id(d) in seen or d is o:
                continue
            seen.add(id(d))
            if (not d.dma) and (not dma) and d.eng == eng and (eng == "pe" or not SAME_ENGINE_SYNC):
                continue
            o.deps.append(d)
        for b in rlist:
            b.R.append(o)
        for b in list(writes) + list(pw):
            b.W.append(o)
        for b in writes:
            b.Wf.append(o)
        if dma:
            k = self.n_dma % self.n_dma_sems
            o.dsem = k
            o.dval = 16 * (self.n_dma // self.n_dma_sems + 1)
            o.prev_dval = o.dval - 16
            self.n_dma += 1
            if final:
                self.final_dmas.append(o)
        o.seq = len(self.ops[eng])
        self.ops[eng].append(o)
        return o

    def barrier(self, fn):
        return self.op("pool", fn, writes=[self.phase], _barrier=True)

    def emit(self, nc, es, block):
        for e in self.ENGS:
            for o in self.ops[e]:
                best = {}
                for d in o.deps:
                    if not d.dma:
                        if d.eng not in best or best[d.eng].seq < d.seq:
                            best[d.eng] = d
                for d in best.values():
                    d.ms = True
                o.cdeps = list(best.values()) + [d for d in o.deps if d.dma]
        for e in self.ENGS:
            c = 0
            for o in self.ops[e]:
                if o.ms and not o.dma:
                    c += 1
                    o.msidx = c
        esem = {e: es.enter_context(nc.semaphore("sem_" + e)) for e in self.ENGS}
        dsems = [es.enter_context(nc.semaphore("dsem%d" % i)) for i in range(self.n_dma_sems)]
        prog = self

        def run(eng_name):
            def body(e):
                waited = {}

                def wait(key, sem, val):
                    if val <= 0:
                        return
                    if waited.get(key, 0) >= val:
                        return
                    waited[key] = val
                    e.wait_ge(sem, val)

                for o in prog.ops[eng_name]:
                    need = {}
                    for d in o.cdeps:
                        if d.dma:
                            k_, s_, v_ = ("d", d.dsem), dsems[d.dsem], d.dval
                        else:
                            k_, s_, v_ = ("e", d.eng), esem[d.eng], d.msidx
                        if k_ not in need or need[k_][1] < v_:
                            need[k_] = (s_, v_)
                    if o.dma:
                        k_ = ("d", o.dsem)
                        if k_ not in need or need[k_][1] < o.prev_dval:
                            need[k_] = (dsems[o.dsem], o.prev_dval)
                    for k_, (s_, v_) in need.items():
                        wait(k_, s_, v_)
                    inst = o.fn(e)
                    if o.dma:
                        inst.then_inc(dsems[o.dsem], 16)
                    elif o.ms:
                        inst.then_inc(esem[eng_name], 1)
                if eng_name == "sp":
                    for o in prog.final_dmas:
                        wait(("d", o.dsem), dsems[o.dsem], o.dval)
            return body

        block.tensor(run("pe"))
        block.scalar(run("act"))
        block.vector(run("dve"))
        block.gpsimd(run("pool"))
        block.sync(run("sp"))


def build(S, KTOP):
    NT = S // 128
    NCH = S // 512
    nc = bass.Bass("TRN2", target_bir_lowering=False)
    P = Prog()

    def din(name, shape, dt=F32):
        return nc.dram_tensor(name, list(shape), dt, kind="ExternalInput").ap()

    def dscr(name, shape, dt):
        return nc.dram_tensor(name, list(shape), dt, kind="Internal").ap()

    x_d = din("x", [S, D])
    pos_d = din("pos", [1, S], I32)
    wfm_d = din("w_fm", [45, 128, 1024])
    wtm_d = din("w_tm", [128, 8 * 1544])
    bg_d = din("bg", [128, 16])
    wa_d = din("wa", [128, 6 * 1024])
    wb_d = din("wb", [128, 2 * 1024])
    wo_d = din("wo", [128, 8 * 1024])
    wg_d = din("wg", [128, 8 * FFN])
    wu_d = din("wu", [128, 8 * FFN])
    wd_d = din("wd", [128, NHC * 1024])
    ln_d = din("ln", [4, D])
    cst_d = din("cst", [128, NCONST])
    out_d = nc.dram_tensor("out", [S, D], F32, kind="ExternalOutput").ap()

    QA_T = dscr("s_qat", [6, 128, S], BF16)
    KA_T = dscr("s_kat", [6, 128, S], BF16)
    QB_T = dscr("s_qbt", [6, 128, S], BF16)
    KB_T = dscr("s_kbt", [6, 128, S], BF16)
    QI_T = dscr("s_qit", [4, 128, S], BF16)
    KI_T = dscr("s_kit", [128, S], BF16)
    G_T = dscr("s_gt", [16, 128, S], BF16)
    VA_S = dscr("s_va", [S, 6 * 192], BF16)
    VB_S = dscr("s_vb", [S, 6 * 192], BF16)
    OA_T = dscr("s_oat", [6, 128, S], BF16)
    OB_T = dscr("s_obt", [2, 128, S], BF16)
    H1 = dscr("s_h1", [S, D], F32)
    H1T = dscr("s_h1t", [8, 128, S], BF16)
    b_QA, b_KA, b_QB, b_KB, b_QI, b_KI, b_G = (Buf(n) for n in ("QA", "KA", "QB", "KB", "QI", "KI", "G"))
    b_VA, b_VB, b_OA, b_OB, b_H1, b_H1T = (Buf(n) for n in ("VA", "VB", "OA", "OB", "H1", "H1T"))

    with ExitStack() as es:
        def sb(name, shape, dt, st=None):
            return (st or es).enter_context(nc.sbuf_tensor("sb_" + name, list(shape), dt))

        banks = [es.enter_context(nc.psum_tensor("bank%d" % i, [128, 512], F32)) for i in range(8)]
        bbuf = [Buf("bank%d" % i, excl=True) for i in range(8)]

        cst = sb("cst", [128, NCONST], F32)
        b_cst = Buf("cst")
        identb = sb("identb", [128, 128], BF16)
        permb = sb("permb", [128, 128], BF16)
        mb4 = sb("mb4", [128, 512], BF16)
        b_cb = Buf("constb")
        P.op("sp", lambda e: e.dma_start(out=cst[:], in_=cst_d[:, :]), writes=[b_cst], dma=True)
        P.op("dve", lambda e: e.tensor_copy(out=identb[:], in_=cst[:, C_ID:C_ID + 128]), reads=[b_cst], writes=[b_cb])
        P.op("dve", lambda e: e.tensor_copy(out=permb[:], in_=cst[:, C_PERM:C_PERM + 128]), reads=[b_cst], writes=[b_cb])
        for j in range(4):
            src = C_MD if j % 2 == 0 else C_MP
            P.op("dve", lambda e, j=j, src=src: e.tensor_copy(out=mb4[:, j * 128:(j + 1) * 128], in_=cst[:, src:src + 128]),
                 reads=[b_cst], writes=[b_cb])
        ident_f = cst[:, C_ID:C_ID + 128]
        negb = sb("negb", [128, 1], F32)
        P.op("dve", lambda e: e.memset(negb[:], -MASK_NEG), writes=[b_cb])

        WI = sb("WI", [128, NT, 8], F32)
        b_WI = Buf("WI")
        bar = sb("bar", [128, 8], F32)

        def barrier(k=0):
            if os.environ.get("KDBG"):
                print("barrier", k, getattr(P, "nops", 0))
            if k > KSTOP:
                P.stopped = True
            P.barrier(lambda e: e.memset(bar[:], 0.0))

        with ExitStack() as s2:
            XT = sb("XT", [128, 8, S], BF16, s2)
            b_XT = [Buf("XT%d" % t) for t in range(NT)]
            xs = [sb("xs%d" % i, [128, D], F32, s2) for i in range(2)]
            b_xs = [Buf("xs%d" % i) for i in range(2)]
            for tt in range(NT):
                sl = tt % 2
                P.op("sp", lambda e, tt=tt, sl=sl: e.dma_start(out=xs[sl][:], in_=x_d[tt * 128:(tt + 1) * 128, :]),
                     writes=[b_xs[sl]], dma=True)
                for half in range(2):
                    bk = (tt * 2 + half) % 2
                    for j in range(4):
                        kc = half * 4 + j
                        P.op("pe", lambda e, bk=bk, j=j, kc=kc, sl=sl: e.transpose(
                            out=banks[bk][:, j * 128:(j + 1) * 128], in_=xs[sl][:, kc * 128:(kc + 1) * 128], identity=ident_f),
                            reads=[b_xs[sl], b_cst], writes=[bbuf[bk]])
                    eng = "act" if half == 0 else "dve"
                    if eng == "act":
                        P.op("act", lambda e, bk=bk, half=half, tt=tt: e.activation(
                            out=XT[:, half * 4:half * 4 + 4, tt * 128:(tt + 1) * 128],
                            in_=banks[bk][:].rearrange("p (a b) -> p a b", a=4), func=AF.Copy),
                            reads=[bbuf[bk]], pw=[b_XT[tt]])
                    else:
                        P.op("dve", lambda e, bk=bk, half=half, tt=tt: e.tensor_copy(
                            out=XT[:, half * 4:half * 4 + 4, tt * 128:(tt + 1) * 128],
                            in_=banks[bk][:].rearrange("p (a b) -> p a b", a=4)),
                            reads=[bbuf[bk]], pw=[b_XT[tt]])

            CT = sb("CT", [128, S], F32, s2)
            ST = sb("ST", [128, S], F32, s2)
            b_CT, b_ST = Buf("CT"), Buf("ST")
            with ExitStack() as sr:
                posi = sb("posi", [128, S], I32, sr)
                posf = sb("posf", [128, S], F32, sr)
                tk = sb("ropek", [128, S], F32, sr)
                b_posi, b_posf, b_tk = Buf("posi"), Buf("posf"), Buf("tk")
                P.op("sp", lambda e: e.dma_start(out=posi[:], in_=pos_d[0:1, :].partition_broadcast(128)),
                     writes=[b_posi], dma=True)
                P.op("dve", lambda e: e.tensor_copy(out=posf[:], in_=posi[:]), reads=[b_posi], writes=[b_posf])
                MAGIC = 12582912.0
                for (T, bT, ca, cb) in ((ST, b_ST, C_ROPE + 0, None), (CT, b_CT, C_ROPE + 1, C_ROPE + 2)):
                    if cb is None:
                        P.op("dve", lambda e, T=T, ca=ca: e.tensor_scalar(
                            out=T[:], in0=posf[:], scalar1=cst[:, ca:ca + 1], scalar2=None, op0=ALU.mult),
                            reads=[b_posf, b_cst], writes=[bT])
                    else:
                        P.op("dve", lambda e, T=T, ca=ca, cb=cb: e.tensor_scalar(
                            out=T[:], in0=posf[:], scalar1=cst[:, ca:ca + 1], scalar2=cst[:, cb:cb + 1],
                            op0=ALU.mult, op1=ALU.add), reads=[b_posf, b_cst], writes=[bT])
                    P.op("dve", lambda e, T=T: e.tensor_scalar(out=tk[:], in0=T[:], scalar1=MAGIC, scalar2=None, op0=ALU.add),
                         reads=[bT], writes=[b_tk])
                    P.op("dve", lambda e, T=T: e.tensor_scalar(out=tk[:], in0=tk[:], scalar1=MAGIC, scalar2=None, op0=ALU.subtract),
                         reads=[b_tk], writes=[b_tk])
                    P.op("dve", lambda e, T=T: e.tensor_tensor(out=T[:], in0=T[:], in1=tk[:], op=ALU.subtract),
                         reads=[bT, b_tk], writes=[bT])
                    P.op("dve", lambda e, T=T: e.tensor_scalar(out=tk[:], in0=T[:], scalar1=0.5, scalar2=None, op0=ALU.is_gt),
                         reads=[bT], writes=[b_tk])
                    P.op("dve", lambda e, T=T: e.tensor_tensor(out=T[:], in0=T[:], in1=tk[:], op=ALU.subtract),
                         reads=[bT, b_tk], writes=[bT])
                    P.op("dve", lambda e, T=T: e.tensor_scalar(out=tk[:], in0=T[:], scalar1=-0.5, scalar2=None, op0=ALU.is_lt),
                         reads=[bT], writes=[b_tk])
                    P.op("dve", lambda e, T=T: e.tensor_tensor(out=T[:], in0=T[:], in1=tk[:], op=ALU.add),
                         reads=[bT, b_tk], writes=[bT])
                    P.op("act", lambda e, T=T: e.activation(out=T[:], in_=T[:], func=AF.Sin, scale=2.0 * math.pi * (1.0 - 1e-6)),
                         reads=[bT], writes=[bT])

            barrier(2)
            wst = [sb("wst%d" % i, [128, 1024], F32, s2) for i in range(2)]
            wbf = [sb("wbf%d" % i, [128, 8, 128], BF16, s2) for i in range(2)]
            b_wst = [Buf("wst%d" % i) for i in range(2)]
            b_wbf = [Buf("wbf%d" % i) for i in range(2)]
            bgt = sb("bgt", [128, 16], F32, s2)
            b_bgt = Buf("bgt")
            P.op("sp", lambda e: e.dma_start(out=bgt[:], in_=bg_d[:, :]), writes=[b_bgt], dma=True)
            qtmp = [sb("qtmp%d" % i, [128, 512], BF16, s2) for i in range(2)]
            b_qtmp = [Buf("qtmp%d" % i) for i in range(2)]
            rt1 = [sb("rt1_%d" % i, [128, 512], F32, s2) for i in range(2)]
            rt2 = [sb("rt2_%d" % i, [128, 512], F32, s2) for i in range(2)]
            b_rt1 = [Buf("rt1_%d" % i) for i in range(2)]
            b_rt2 = [Buf("rt2_%d" % i) for i in range(2)]
            ost = [sb("ost%d" % i, [128, 512], BF16, s2) for i in range(3)]
            b_ost = [Buf("ost%d" % i) for i in range(3)]
            tiles = []
            for c in range(6):
                tiles.append((QA_T[c], b_QA, "rope"))
            for c in range(6):
                tiles.append((KA_T[c], b_KA, "rope"))
            for c in range(6):
                tiles.append((QB_T[c], b_QB, "rope"))
            for c in range(6):
                tiles.append((KB_T[c], b_KB, "rope"))
            for c in range(4):
                tiles.append((QI_T[c], b_QI, "rope"))
            tiles.append((KI_T, b_KI, "rope"))
            for c in range(16):
                tiles.append((G_T[c], b_G, "gate"))
            def fm_main(idx):
                ci, n = idx // NCH, idx % NCH
                dst, bdst, kind = tiles[ci]
                ws = ci % 2
                if n == 0:
                    for cj in ([0, 1] if ci == 0 else [ci + 1]):
                        if cj < len(tiles):
                            wj = cj % 2
                            P.op("act", lambda e, cj=cj, wj=wj: e.dma_start(out=wst[wj][:], in_=wfm_d[cj]), writes=[b_wst[wj]], dma=True)
                            P.op("pool", lambda e, wj=wj: e.tensor_copy(out=wbf[wj][:].rearrange("p a b -> p (a b)"), in_=wst[wj][:]),
                                 reads=[b_wst[wj]], writes=[b_wbf[wj]])
                bk = idx % 2
                tsl = slice(n * 512, (n + 1) * 512)
                xdeps = [b_XT[t] for t in range(n * 4, n * 4 + 4)]
                for kc in range(8):
                    P.op("pe", lambda e, bk=bk, kc=kc, ws=ws, tsl=tsl: e.matmul(
                        banks[bk][:], lhsT=wbf[ws][:, kc, :], rhs=XT[:, kc, tsl], start=(kc == 0), stop=(kc == 7)),
                        reads=[b_wbf[ws]] + xdeps, writes=[bbuf[bk]])
                o3 = idx % 3
                if kind == "gate":
                    gc = ci - 29
                    P.op("act", lambda e, bk=bk, o3=o3, gc=gc: e.activation(
                        out=ost[o3][:], in_=banks[bk][:], func=AF.Sigmoid, bias=bgt[:, gc:gc + 1]),
                        reads=[bbuf[bk], b_bgt], writes=[b_ost[o3]])
                    P.op("sp", lambda e, o3=o3, dst=dst, tsl=tsl: e.dma_start(out=dst[:, tsl], in_=ost[o3][:]),
                         reads=[b_ost[o3]], pw=[bdst], dma=True)
                else:
                    P.op("act", lambda e, bk=bk: e.activation(out=qtmp[bk][:], in_=banks[bk][:], func=AF.Copy),
                         reads=[bbuf[bk]], writes=[b_qtmp[bk]])

            def fm_rope(idx):
                ci, n = idx // NCH, idx % NCH
                dst, bdst, kind = tiles[ci]
                if kind == "gate":
                    return
                bk = idx % 2
                q2 = bk
                o3 = idx % 3
                tsl = slice(n * 512, (n + 1) * 512)
                P.op("pe", lambda e, bk=bk, q2=q2: e.matmul(banks[2 + bk][:], lhsT=permb[:], rhs=qtmp[q2][:], start=True, stop=True),
                     reads=[b_qtmp[q2], b_cb], writes=[bbuf[2 + bk]])
                P.op("pool", lambda e, q2=q2, tsl=tsl: e.tensor_tensor(out=rt1[q2][:], in0=qtmp[q2][:], in1=CT[:, tsl], op=ALU.mult),
                     reads=[b_qtmp[q2], b_CT], writes=[b_rt1[q2]])
                P.op("dve", lambda e, bk=bk, q2=q2, tsl=tsl: e.tensor_tensor(out=rt2[q2][:], in0=banks[2 + bk][:], in1=ST[:, tsl], op=ALU.mult),
                     reads=[bbuf[2 + bk], b_ST], writes=[b_rt2[q2]])
                P.op("dve", lambda e, q2=q2, o3=o3: e.tensor_tensor(out=ost[o3][:], in0=rt1[q2][:], in1=rt2[q2][:], op=ALU.add),
                     reads=[b_rt1[q2], b_rt2[q2]], writes=[b_ost[o3]])
                P.op("sp", lambda e, o3=o3, dst=dst, tsl=tsl: e.dma_start(out=dst[:, tsl], in_=ost[o3][:]),
                     reads=[b_ost[o3]], pw=[bdst], dma=True)

            nitems = len(tiles) * NCH
            for idx in range(nitems + 1):
                if idx < nitems:
                    fm_main(idx)
                if idx >= 1:
                    fm_rope(idx - 1)

            if os.environ.get("KDBG"):
                print("tm start", getattr(P, "nops", 0))
            wtm = sb("wtm", [128, 8, 1544], BF16, s2)
            b_wtm = Buf("wtm")
            for kc in range(8):
                ws = kc % 2
                P.op("sp", lambda e, kc=kc, ws=ws: e.dma_start(out=wst[ws][:, 0:1544 - 1024], in_=wtm_d[:, kc * 1544 + 1024:(kc + 1) * 1544]),
                     writes=[b_wst[ws]], dma=True)
                P.op("pool", lambda e, kc=kc, ws=ws: e.tensor_copy(out=wtm[:, kc, 1024:1544], in_=wst[ws][:, 0:520]),
                     reads=[b_wst[ws]], pw=[b_wtm])
                ws = (kc + 1) % 2
                P.op("sp", lambda e, kc=kc, ws=ws: e.dma_start(out=wst[ws][:], in_=wtm_d[:, kc * 1544:kc * 1544 + 1024]),
                     writes=[b_wst[ws]], dma=True)
                P.op("pool", lambda e, kc=kc, ws=ws: e.tensor_copy(out=wtm[:, kc, 0:1024], in_=wst[ws][:]),
                     reads=[b_wst[ws]], pw=[b_wtm])
            vst = [sb("vst%d" % i, [128, 2, 6, 192], BF16, s2) for i in range(2)]
            b_vst = [Buf("vst%d" % i) for i in range(2)]
            for i in range(2):
                P.op("pool", lambda e, i=i: e.memset(vst[i][:].rearrange("p a b c -> p (a b c)"), 1.0), writes=[b_vst[i]])
            colofs = [0, 384, 776, 1160]
            colw = [384, 392, 384, 384]
            for tt in range(NT):
                vs = tt % 2
                for part in range(4):
                    bk = 4 + (tt * 4 + part) % 4
                    for kc in range(8):
                        P.op("pe", lambda e, bk=bk, kc=kc, tt=tt, part=part: e.matmul(
                            banks[bk][:, 0:colw[part]], lhsT=XT[:, kc, tt * 128:(tt + 1) * 128],
                            rhs=wtm[:, kc, colofs[part]:colofs[part] + colw[part]], start=(kc == 0), stop=(kc == 7)),
                            reads=[b_wtm, b_XT[tt]], writes=[bbuf[bk]])
                    ab = part // 2
                    pr0 = (part % 2) * 3
                    src = banks[bk][:, 0:384].rearrange("p (a b) -> p a b", a=3)
                    P.op("act", lambda e, vs=vs, ab=ab, pr0=pr0, src=src: e.activation(
                        out=vst[vs][:, ab, pr0:pr0 + 3, 0:64], in_=src[:, :, 0:64], func=AF.Copy),
                        reads=[bbuf[bk]], pw=[b_vst[vs]])
                    P.op("dve", lambda e, vs=vs, ab=ab, pr0=pr0, src=src: e.tensor_copy(
                        out=vst[vs][:, ab, pr0:pr0 + 3, 128:192], in_=src[:, :, 64:128]),
                        reads=[bbuf[bk]], pw=[b_vst[vs]])
                    if part == 1:
                        P.op("dve", lambda e, bk=bk, tt=tt: e.tensor_copy(out=WI[:, tt, :], in_=banks[bk][:, 384:392]),
                             reads=[bbuf[bk]], writes=[b_WI])
                P.op("sp", lambda e, vs=vs, tt=tt: e.dma_start(out=VA_S[tt * 128:(tt + 1) * 128, :],
                                                              in_=vst[vs][:, 0].rearrange("p b c -> p (b c)")),
                     reads=[b_vst[vs]], pw=[b_VA], dma=True)
                P.op("sp", lambda e, vs=vs, tt=tt: e.dma_start(out=VB_S[tt * 128:(tt + 1) * 128, :],
                                                              in_=vst[vs][:, 1].rearrange("p b c -> p (b c)")),
                     reads=[b_vst[vs]], pw=[b_VB], dma=True)

        barrier(3)
        with ExitStack() as s3:
            KA = sb("KA", [128, 6, S], BF16, s3)
            VA = sb("VA", [128, NT, 6 * 192], BF16, s3)
            KI = sb("KI", [128, S], BF16, s3)
            b_KAs, b_VAs, b_KIs = Buf("KAs"), Buf("VAs"), Buf("KIs")
            for c in range(6):
                P.op("sp", lambda e, c=c: e.dma_start(out=KA[:, c, :], in_=KA_T[c]), reads=[b_KA], pw=[b_KAs], dma=True)
            for t0 in range(0, NT, 8):
                t1 = min(NT, t0 + 8)
                P.op("sp", lambda e, t0=t0, t1=t1: e.dma_start(
                    out=VA[:, t0:t1, :], in_=VA_S[t0 * 128:t1 * 128, :].rearrange("(t p) c -> p t c", p=128)),
                    reads=[b_VA], pw=[b_VAs], dma=True)
            P.op("sp", lambda e: e.dma_start(out=KI[:], in_=KI_T), reads=[b_KI], writes=[b_KIs], dma=True)

            qch = sb("qch", [128, 6, 2, 256], BF16, s3)
            qich = sb("qich", [128, 4, 2, 256], BF16, s3)
            b_qch, b_qich = Buf("qch"), Buf("qich")
            P.op("pool", lambda e: e.memset(qch[:].rearrange("p a b c -> p (a b c)"), 0.0), writes=[b_qch])
            P.op("pool", lambda e: e.memset(qich[:].rearrange("p a b c -> p (a b c)"), 0.0), writes=[b_qich])
            scs = [sb("sc%d" % i, [128, S], F32, s3) for i in range(2)]
            b_scs = [Buf("sc%d" % i) for i in range(2)]
            junk = sb("junk", [128, S], BF16, s3)
            b_junk = Buf("junk")
            MT = [sb("MT%d" % i, [128, NT, 128], BF16, s3) for i in range(1)]
            b_MT = [Buf("MT%d" % i) for i in range(1)]
            rsb = [sb("rsb%d" % i, [128, 512], BF16, s3) for i in range(2)]
            b_rsb = [Buf("rsb%d" % i) for i in range(2)]
            esb = [sb("esb%d" % i, [128, 512], BF16, s3) for i in range(3)]
            b_esb = [Buf("esb%d" % i) for i in range(3)]
            otile = [sb("otile%d" % i, [128, 6, 128], BF16, s3) for i in range(2)]
            b_otile = [Buf("otile%d" % i) for i in range(2)]
            dsg = [sb("dsg%d" % i, [128, 8, 128], BF16, s3) for i in range(2)]
            b_dsg = [Buf("dsg%d" % i) for i in range(2)]
            wab = [sb("wab%d" % i, [128, 8], F32, s3) for i in range(2)]
            wsg = [sb("wsg%d" % i, [128, 8], F32, s3) for i in range(2)]
            b_wab = [Buf("wab%d" % i) for i in range(2)]
            b_wsg = [Buf("wsg%d" % i) for i in range(2)]
            dmin = [sb("dmin%d" % i, [128, 1], F32, s3) for i in range(2)]
            b_dmin = [Buf("dmin%d" % i) for i in range(2)]
            small = sb("small", [128, 16], F32, s3)
            b_small = Buf("small")
            hw = sb("hw", [128, NIT + 1], F32, s3)
            b_hw = Buf("hw")
            dtmp = sb("dtmp", [128, 128], F32, s3)
            b_dtmp = Buf("dtmp")
            rden = [sb("rden%d" % i, [128, 128], F32, s3) for i in range(2)]
            b_rden = [Buf("rden%d" % i) for i in range(2)]
            cnts = {"z": 0, "e": 0, "o": 0}

            def stageB1(qt):
                qc, qo = qt // 2, (qt % 2) * 128
                if qt % 2 == 0:
                    for hh_ in range(2):
                        P.op("sp", lambda e, qc=qc, hh_=hh_: e.dma_start(
                            out=qich[hh_ * 64:(hh_ + 1) * 64, :, hh_, :],
                            in_=QI_T[:, hh_ * 64:(hh_ + 1) * 64, qc * 256:(qc + 1) * 256].rearrange("c p t -> p c t")),
                            reads=[b_QI], writes=[b_qich], dma=True)
                n_s = qt + 1
                sc, b_sc = scs[qt % 2], b_scs[qt % 2]
                n = 128 * n_s
                nchk = (n_s + 3) // 4
                ds = qt % 2
                P.op("dve", lambda e, ds=ds, qt=qt: e.tensor_scalar(out=wsg[ds][:], in0=WI[:, qt, :], scalar1=0.0, scalar2=-0.5,
                                                                   op0=ALU.is_gt, op1=ALU.add),
                     reads=[b_WI], writes=[b_wsg[ds]])
                P.op("dve", lambda e, ds=ds, qt=qt: e.scalar_tensor_tensor(out=wsg[ds][:], in0=WI[:, qt, :], scalar=0.0, in1=wsg[ds][:],
                                                                          op0=ALU.is_lt, op1=ALU.subtract),
                     reads=[b_WI, b_wsg[ds]], writes=[b_wsg[ds]])
                P.op("dve", lambda e, ds=ds: e.tensor_scalar(out=wsg[ds][:], in0=wsg[ds][:], scalar1=-1.0, scalar2=0.5,
                                                            op0=ALU.mult, op1=ALU.add),
                     reads=[b_wsg[ds]], writes=[b_wsg[ds]])
                P.op("dve", lambda e, ds=ds, qt=qt: e.tensor_tensor(out=wab[ds][:], in0=WI[:, qt, :], in1=wsg[ds][:], op=ALU.mult),
                     reads=[b_WI, b_wsg[ds]], writes=[b_wab[ds]])
                for h in range(8):
                    P.op("dve", lambda e, ds=ds, h=h: e.tensor_scalar(out=dsg[ds][:, h, :], in0=cst[:, C_ID:C_ID + 128],
                                                                     scalar1=wsg[ds][:, h:h + 1], scalar2=None, op0=ALU.mult),
                         reads=[b_cst, b_wsg[ds]], writes=[b_dsg[ds]])
                yield
                items = [(c, h) for c in range(nchk) for h in range(8)]

                def z_part(c, h):
                    ncols = min(512, n - 512 * c)
                    hp, hh = h // 2, h % 2
                    zb = (c * 8 + h) % 2
                    rows = slice(hh * 64, hh * 64 + 64)
                    P.op("pe", lambda e, zb=zb, hh=hh, hp=hp, c=c, ncols=ncols: e.matmul(
                        banks[zb][:, 0:ncols], lhsT=qich[:, hp, hh, qo:qo + 128], rhs=KI[:, c * 512:c * 512 + ncols],
                        start=True, stop=True), reads=[b_qich, b_KIs], writes=[bbuf[zb]])
                    P.op("act", lambda e, zb=zb, ncols=ncols, h=h: e.activation(
                        out=rsb[zb][:, 0:ncols], in_=banks[zb][:, 0:ncols], func=AF.Relu, scale=wab[ds][:, h:h + 1]),
                        reads=[bbuf[zb], b_wab[ds]], writes=[b_rsb[zb]])

                def d_part(c, h):
                    ncols = min(512, n - 512 * c)
                    zb = (c * 8 + h) % 2
                    ab = 2 if c % 2 == 0 else 7
                    P.op("pe", lambda e, ab=ab, zb=zb, ncols=ncols, h=h: e.matmul(
                        banks[ab][:, 0:ncols], lhsT=dsg[ds][:, h, :], rhs=rsb[zb][:, 0:ncols], start=(h == 0), stop=(h == 7)),
                        reads=[b_dsg[ds], b_rsb[zb]], writes=[bbuf[ab]])
                    if h != 7:
                        return
                    last = (c == nchk - 1)
                    nfull = ncols - 128 if last else ncols
                    if nfull > 0:
                        P.op("dve", lambda e, ab=ab, c=c, nfull=nfull: e.tensor_copy(out=sc[:, c * 512:c * 512 + nfull], in_=banks[ab][:, 0:nfull]),
                             reads=[bbuf[ab]], writes=[b_sc])
                    if last:
                        dsl = slice(n - 128, n)
                        P.op("dve", lambda e, ab=ab, nfull=nfull: e.tensor_tensor(out=dtmp[:], in0=banks[ab][:, nfull:nfull + 128],
                                                                                 in1=cst[:, C_TRIP:C_TRIP + 128], op=ALU.add),
                             reads=[bbuf[ab], b_cst], writes=[b_dtmp])
                        P.op("dve", lambda e: e.tensor_reduce(out=dmin[qt % 2][:, 0:1], in_=dtmp[:], axis=AX.X, op=ALU.min),
                             reads=[b_dtmp], writes=[b_dmin[qt % 2]])
                        P.op("dve", lambda e, ab=ab, nfull=nfull, dsl=dsl: e.tensor_tensor(out=sc[:, dsl], in0=banks[ab][:, nfull:nfull + 128],
                                                                                          in1=cst[:, C_TRIM:C_TRIM + 128], op=ALU.add),
                             reads=[bbuf[ab], b_cst], writes=[b_sc])

                for i_ in range(len(items) + 1):
                    if i_ < len(items):
                        z_part(*items[i_])
                    if i_ >= 1:
                        d_part(*items[i_ - 1])
                    if i_ % 2 == 1:
                        yield

            def stageB2(qt):
                n_s = qt + 1
                n = 128 * n_s
                nchk = (n_s + 3) // 4
                sc, b_sc = scs[qt % 2], b_scs[qt % 2]
                if n > KTOP:
                    P.op("dve", lambda e, n=n: e.tensor_reduce(out=small[:, 0:1], in_=sc[:, 0:n], axis=AX.X, op=ALU.max),
                         reads=[b_sc], writes=[b_small])
                    P.op("dve", lambda e, n=n: e.tensor_reduce(out=small[:, 1:2], in_=sc[:, 0:n - 128], axis=AX.X, op=ALU.min),
                         reads=[b_sc], writes=[b_small])
                    P.op("dve", lambda e: e.tensor_tensor(out=small[:, 1:2], in0=small[:, 1:2], in1=dmin[qt % 2][:, 0:1], op=ALU.min),
                         reads=[b_small, b_dmin[qt % 2]], writes=[b_small])
                    P.op("dve", lambda e: e.tensor_tensor(out=small[:, 3:4], in0=small[:, 0:1], in1=small[:, 1:2], op=ALU.subtract),
                         reads=[b_small], writes=[b_small])
                    P.op("dve", lambda e: e.tensor_scalar(out=small[:, 3:4], in0=small[:, 3:4], scalar1=1.0001, scalar2=1e-6,
                                                          op0=ALU.mult, op1=ALU.add), reads=[b_small], writes=[b_small])
                    P.op("dve", lambda e: e.tensor_scalar(out=hw[:], in0=cst[:, C_POW:C_POW + NIT + 1], scalar1=small[:, 3:4], scalar2=None,
                                                          op0=ALU.mult), reads=[b_small, b_cst], writes=[b_hw])
                    P.op("dve", lambda e: e.tensor_tensor(out=small[:, 4:5], in0=small[:, 1:2], in1=hw[:, 0:1], op=ALU.add),
                         reads=[b_small, b_hw], writes=[b_small])
                    yield
                    for it in range(NIT):
                        P.op("dve", lambda e, n=n: e.tensor_scalar(out=junk[:, 0:n], in0=sc[:, 0:n], scalar1=small[:, 4:5], scalar2=None,
                                                                   op0=ALU.is_ge, op1=ALU.add, accum_out=small[:, 5:6]),
                             reads=[b_sc, b_small], writes=[b_junk, b_small])
                        P.op("dve", lambda e, it=it: e.tensor_scalar(out=small[:, 6:7], in0=small[:, 5:6], scalar1=float(KTOP) - 0.5,
                                                                     scalar2=hw[:, it:it + 1], op0=ALU.is_ge, op1=ALU.mult),
                             reads=[b_small, b_hw], writes=[b_small])
                        P.op("dve", lambda e, it=it: e.scalar_tensor_tensor(out=small[:, 4:5], in0=small[:, 6:7], scalar=hw[:, it + 1:it + 2],
                                                                            in1=small[:, 4:5], op0=ALU.subtract, op1=ALU.add),
                             reads=[b_small, b_hw], writes=[b_small])
                        yield
                    P.op("dve", lambda e: e.scalar_tensor_tensor(out=small[:, 7:8], in0=hw[:, NIT:NIT + 1], scalar=-2.0, in1=small[:, 4:5],
                                                                 op0=ALU.mult, op1=ALU.add),
                         reads=[b_small, b_hw], writes=[b_small])
                else:
                    P.op("dve", lambda e: e.memset(small[:, 7:8], -1.0e29), writes=[b_small])
                P.op("dve", lambda e, n=n: e.tensor_scalar(out=junk[:, 0:n], in0=sc[:, 0:n], scalar1=small[:, 7:8], scalar2=None, op0=ALU.is_ge),
                     reads=[b_sc, b_small], writes=[b_junk])
                yield

            def stageB2b(qt):
                n_s = qt + 1
                nchk = (n_s + 3) // 4
                ms = 0
                for c in range(nchk):
                    g = min(4, n_s - 4 * c)
                    tb = 7 if c % 2 == 0 else 2
                    tbank = banks[tb][:, 0:256].bitcast(BF16)
                    for j in range(g):
                        st = 4 * c + j
                        P.op("pe", lambda e, tbank=tbank, j=j, st=st: e.transpose(
                            out=tbank[:, j * 128:(j + 1) * 128], in_=junk[:, st * 128:(st + 1) * 128], identity=identb[:]),
                            reads=[b_junk, b_cb], writes=[bbuf[tb]])
                    P.op("act", lambda e, tbank=tbank, ms=ms, c=c, g=g: e.activation(
                        out=MT[ms][:, 4 * c:4 * c + g, :].rearrange("p a b -> p (a b)"), in_=tbank[:, 0:g * 128], func=AF.Identity,
                        scale=MASK_NEG, bias=negb[:, 0:1]),
                        reads=[bbuf[tb], b_cb], pw=[b_MT[ms]])
                    yield

            def stageC(qt):
                qc, qo = qt // 2, (qt % 2) * 128
                if qt % 2 == 0:
                    for hh_ in range(2):
                        P.op("sp", lambda e, qc=qc, hh_=hh_: e.dma_start(
                            out=qch[hh_ * 64:(hh_ + 1) * 64, :, hh_, :],
                            in_=QA_T[:, hh_ * 64:(hh_ + 1) * 64, qc * 256:(qc + 1) * 256].rearrange("c p t -> p c t")),
                            reads=[b_QA], writes=[b_qch], dma=True)
                n_s = qt + 1
                nchk = (n_s + 3) // 4
                ms = 0
                ot = qt % 2
                SK = 2
                items = [(h, c) for h in range(12) for c in range(nchk)]
                e0 = cnts["e"]
                cnts["e"] += len(items)
                o0 = cnts["o"]
                cnts["o"] += 12

                def qk_part(idx):
                    h, c = items[idx]
                    hp, hh = h // 2, h % 2
                    rows = slice(hh * 64, hh * 64 + 64)
                    g = min(4, n_s - 4 * c)
                    sbk = 3 + (e0 + idx) % 2
                    es_ = (e0 + idx) % 3
                    for j in range(g):
                        st = 4 * c + j
                        P.op("pe", lambda e, sbk=sbk, j=j, st=st, hh=hh, hp=hp: e.matmul(
                            banks[sbk][:, j * 128:(j + 1) * 128], lhsT=KA[:, hp, st * 128:(st + 1) * 128],
                            rhs=qch[:, hp, hh, qo:qo + 128], start=True, stop=False),
                            reads=[b_KAs, b_qch], writes=[bbuf[sbk]])
                        P.op("pe", lambda e, sbk=sbk, j=j, st=st: e.matmul(
                            banks[sbk][:, j * 128:(j + 1) * 128], lhsT=identb[:], rhs=MT[ms][:, st, :], start=False, stop=True),
                            reads=[b_cb, b_MT[ms]], writes=[bbuf[sbk]])
                    P.op("act", lambda e, sbk=sbk, es_=es_, g=g: e.activation(
                        out=esb[es_][:, 0:g * 128], in_=banks[sbk][:, 0:g * 128], func=AF.Exp, scale=0.125),
                        reads=[bbuf[sbk]], writes=[b_esb[es_]])

                def pv_part(idx):
                    h, c = items[idx]
                    hp, hh = h // 2, h % 2
                    g = min(4, n_s - 4 * c)
                    es_ = (e0 + idx) % 3
                    ob = 5 + (o0 + h) % 2
                    vofs = hp * 192 + hh * 64
                    for j in range(g):
                        st = 4 * c + j
                        P.op("pe", lambda e, ob=ob, es_=es_, j=j, st=st, vofs=vofs: e.matmul(
                            banks[ob][:, 0:128], lhsT=VA[:, st, vofs:vofs + 128], rhs=esb[es_][:, j * 128:(j + 1) * 128],
                            start=(st == 0), stop=(st == n_s - 1)),
                            reads=[b_VAs, b_esb[es_]], writes=[bbuf[ob]])
                    if c != nchk - 1:
                        return
                    num = slice(0, 64) if hh == 0 else slice(64, 128)
                    den = slice(64, 128) if hh == 0 else slice(0, 64)
                    rd = h % 2
                    P.op("act", lambda e, ob=ob, den=den, rd=rd: e.activation(out=rden[rd][den, :], in_=banks[ob][den, 0:128], func=AF.Ln),
                         reads=[bbuf[ob]], writes=[b_rden[rd]])
                    P.op("act", lambda e, den=den, rd=rd: e.activation(out=rden[rd][den, :], in_=rden[rd][den, :], func=AF.Exp, scale=-1.0),
                         reads=[b_rden[rd]], writes=[b_rden[rd]])
                    P.op("dve", lambda e, ob=ob, num=num, den=den, rd=rd, hp=hp: e.tensor_tensor(
                        out=otile[ot][num, hp, :], in0=banks[ob][num, 0:128], in1=rden[rd][den, :], op=ALU.mult),
                        reads=[bbuf[ob], b_rden[rd]], pw=[b_otile[ot]])

                for i_ in range(len(items) + SK):
                    if i_ < len(items):
                        qk_part(i_)
                    if i_ >= SK:
                        pv_part(i_ - SK)
                    yield
                P.op("sp", lambda e, ot=ot, qt=qt: e.dma_start(
                    out=OA_T[:, :, qt * 128:(qt + 1) * 128].rearrange("c p t -> p c t"), in_=otile[ot][:]),
                    reads=[b_otile[ot]], pw=[b_OA], dma=True)
                yield

            def interleave(*gens):
                live = [g for g in gens if g is not None]
                while live:
                    nxt = []
                    for g in live:
                        try:
                            next(g)
                            nxt.append(g)
                        except StopIteration:
                            pass
                    live = nxt

            interleave(stageB1(0))
            interleave(stageB2(0), stageB1(1) if NT > 1 else None)
            interleave(stageB2b(0))
            for qt in range(NT):
                interleave(stageC(qt),
                           stageB2(qt + 1) if qt + 1 < NT else None,
                           stageB1(qt + 2) if qt + 2 < NT else None)
                if qt + 1 < NT:
                    interleave(stageB2b(qt + 1))

        barrier(5)
        with ExitStack() as s5:
            acc = [sb("acc%d" % i, [128, S], F32, s5) for i in range(4)]
            b_acc = [Buf("acc%d" % i) for i in range(4)]
            QB = sb("QBg", [128, 2, 2, S], BF16, s5)
            KB = sb("KBg", [128, 2, S], BF16, s5)
            VB = sb("VBg", [128, NT, 2 * 192], BF16, s5)
            b_QBs, b_KBs, b_VBs = Buf("QBs"), Buf("KBs"), Buf("VBs")
            psb5 = [sb("p5_%d" % i, [128, 512], BF16, s5) for i in range(3)]
            b_p5 = [Buf("p5_%d" % i) for i in range(3)]
            obt = sb("obt", [128, 2, S], BF16, s5)
            b_obt = Buf("obt")
            rd5 = sb("rd5", [128, S], F32, s5)
            b_rd5 = Buf("rd5")
            mb4n = sb("mb4n", [128, 512], BF16, s5)
            P.op("dve", lambda e: e.tensor_scalar(out=mb4n[:], in0=mb4[:], scalar1=MASK_NEG, scalar2=-MASK_NEG, op0=ALU.mult, op1=ALU.add),
                 reads=[b_cb], writes=[b_cb])
            P.op("pool", lambda e: e.memset(QB[:].rearrange("p a b c -> p (a b c)"), 0.0), writes=[b_QBs])
            oc = 0
            SK5 = 2
            for g, dil in enumerate((1, 4, 16)):
                L = S // dil
                nsub = L // 128
                for c in range(2):
                    for hh_ in range(2):
                        P.op("sp", lambda e, c=c, g=g, hh_=hh_: e.dma_start(
                            out=QB[hh_ * 64:(hh_ + 1) * 64, c, hh_, :], in_=QB_T[2 * g + c][hh_ * 64:(hh_ + 1) * 64, :]),
                            reads=[b_QB], pw=[b_QBs], dma=True)
                    P.op("sp", lambda e, c=c, g=g: e.dma_start(out=KB[:, c, :], in_=KB_T[2 * g + c]), reads=[b_KB], pw=[b_KBs], dma=True)
                for r in range(dil):
                    for iu0 in range(0, nsub, 8):
                        iu1 = min(nsub, iu0 + 8)
                        base = r + dil * 128 * iu0
                        cnt_rows = (iu1 - iu0) * 128
                        srcv = VB_S[base:base + dil * (cnt_rows - 1) + 1:dil, g * 384:(g + 1) * 384]
                        P.op("sp", lambda e, r=r, iu0=iu0, iu1=iu1, srcv=srcv, nsub=nsub: e.dma_start(
                            out=VB[:, r * nsub + iu0:r * nsub + iu1, :], in_=srcv.rearrange("(t p) c -> p t c", p=128)),
                            reads=[b_VB], pw=[b_VBs], dma=True)
                items5 = []
                for hl in range(4):
                    for r in range(dil):
                        blocks = [(0, 0, "D")]
                        for iu in range(1, nsub):
                            blocks.append((iu - 1, iu, "P"))
                            blocks.append((iu, iu, "D"))
                        cur_ob = None
                        for b0 in range(0, len(blocks), 4):
                            chunk = blocks[b0:b0 + 4]
                            obs = []
                            for (st, iu, kind) in chunk:
                                if iu % 4 == 0 and kind == ("D" if iu == 0 else "P"):
                                    cur_ob = 4 + oc % 4
                                    oc += 1
                                obs.append(cur_ob)
                            items5.append((hl, r, chunk, obs))

                def tok5(r, iu, dil=dil):
                    b0 = r + dil * 128 * iu
                    return slice(b0, b0 + dil * 127 + 1, dil)

                def qk5(idx, g=g, nsub=nsub):
                    hl, r, chunk, obs = items5[idx]
                    hp, hh = hl // 2, hl % 2
                    gsz = len(chunk)
                    sbk = idx % 3
                    for j, (st, iu, kind) in enumerate(chunk):
                        ksl, qsl = tok5(r, st), tok5(r, iu)
                        P.op("pe", lambda e, sbk=sbk, j=j, hp=hp, hh=hh, ksl=ksl, qsl=qsl: e.matmul(
                            banks[sbk][:, j * 128:(j + 1) * 128], lhsT=KB[:, hp, ksl], rhs=QB[:, hp, hh, qsl],
                            start=True, stop=False), reads=[b_KBs, b_QBs], writes=[bbuf[sbk]])
                        P.op("pe", lambda e, sbk=sbk, j=j: e.matmul(
                            banks[sbk][:, j * 128:(j + 1) * 128], lhsT=identb[:], rhs=mb4n[:, j * 128:(j + 1) * 128],
                            start=False, stop=True), reads=[b_cb], writes=[bbuf[sbk]])
                    P.op("act", lambda e, sbk=sbk, gsz=gsz: e.activation(
                        out=psb5[sbk][:, 0:gsz * 128], in_=banks[sbk][:, 0:gsz * 128], func=AF.Exp, scale=0.125),
                        reads=[bbuf[sbk]], writes=[b_p5[sbk]])

                def pv5(idx, g=g, nsub=nsub, dil=dil):
                    hl, r, chunk, obs = items5[idx]
                    hp, hh = hl // 2, hl % 2
                    vofs = hp * 192 + hh * 64
                    es_ = idx % 3
                    for j, (st, iu, kind) in enumerate(chunk):
                        ob = obs[j]
                        jo = iu % 4
                        P.op("pe", lambda e, ob=ob, jo=jo, es_=es_, j=j, st=st, r=r, vofs=vofs, kind=kind, iu=iu: e.matmul(
                            banks[ob][:, jo * 128:(jo + 1) * 128], lhsT=VB[:, r * nsub + st, vofs:vofs + 128],
                            rhs=psb5[es_][:, j * 128:(j + 1) * 128],
                            start=(kind == "P" or iu == 0), stop=(kind == "D")),
                            reads=[b_VBs, b_p5[es_]], writes=[bbuf[ob]])
                        if kind == "D" and (iu % 4 == 3 or iu == nsub - 1):
                            iu_lo = iu - (iu % 4)
                            nt_ = iu - iu_lo + 1
                            b00 = r + dil * 128 * iu_lo
                            dst = acc[hl][:, b00:b00 + dil * (nt_ * 128 - 1) + 1:dil]
                            if g == 0:
                                P.op("dve", lambda e, ob=ob, nt_=nt_, dst=dst: e.tensor_copy(out=dst, in_=banks[ob][:, 0:nt_ * 128]),
                                     reads=[bbuf[ob]], writes=[b_acc[hl]])
                            else:
                                P.op("dve", lambda e, ob=ob, nt_=nt_, dst=dst: e.tensor_tensor(
                                    out=dst, in0=banks[ob][:, 0:nt_ * 128], in1=dst, op=ALU.add),
                                    reads=[bbuf[ob], b_acc[hl]], writes=[b_acc[hl]])

                for i_ in range(len(items5) + SK5):
                    if i_ < len(items5):
                        qk5(i_)
                    if i_ >= SK5:
                        pv5(i_ - SK5)
            for hl in range(4):
                hp, hh = hl // 2, hl % 2
                num = slice(0, 64) if hh == 0 else slice(64, 128)
                den = slice(64, 128) if hh == 0 else slice(0, 64)
                P.op("dve", lambda e, hl=hl, den=den, num=num: e.reciprocal(out=rd5[num, :], in_=acc[hl][den, :]),
                     reads=[b_acc[hl]], writes=[b_rd5])
                P.op("dve", lambda e, hl=hl, num=num, den=den, hp=hp: e.tensor_tensor(
                    out=obt[num, hp, :], in0=acc[hl][num, :], in1=rd5[num, :], op=ALU.mult),
                    reads=[b_acc[hl], b_rd5], writes=[b_obt])
            for c in range(2):
                P.op("sp", lambda e, c=c: e.dma_start(out=OB_T[c], in_=obt[:, c, :]), reads=[b_obt], pw=[b_OB], dma=True)

        def load_w(dst2d, src2d, ncols, stg, b_stg, b_dst, cw=2048, engs=("pool", "dve", "act")):
            for i, c0 in enumerate(range(0, ncols, cw)):
                c1 = min(ncols, c0 + cw)
                sl = i % len(stg)
                P.op("sp", lambda e, c0=c0, c1=c1, sl=sl: e.dma_start(out=stg[sl][:, 0:c1 - c0], in_=src2d[:, c0:c1]),
                     writes=[b_stg[sl]], dma=True)
                ceng = engs[i % len(engs)]
                if ceng == "act":
                    P.op("act", lambda e, c0=c0, c1=c1, sl=sl: e.activation(out=dst2d[:, c0:c1], in_=stg[sl][:, 0:c1 - c0], func=AF.Copy),
                         reads=[b_stg[sl]], pw=[b_dst])
                else:
                    P.op(ceng, lambda e, c0=c0, c1=c1, sl=sl: e.tensor_copy(out=dst2d[:, c0:c1], in_=stg[sl][:, 0:c1 - c0]),
                         reads=[b_stg[sl]], pw=[b_dst])

        def layer_norm(src, b_src, dst, b_dst, gbc, bbc, b_ln, stats, mv, b_st):
            for hf in range(2):
                P.op("dve", lambda e, hf=hf: e.bn_stats(out=stats[:, hf * 6:(hf + 1) * 6], in_=src[:, hf * 512:(hf + 1) * 512]),
                     reads=[b_src], writes=[b_st])
            P.op("dve", lambda e: e.bn_aggr(out=mv[:, 0:2], in_=stats[:, 0:12]), reads=[b_st], writes=[b_st])
            P.op("dve", lambda e: e.tensor_scalar(out=mv[:, 2:3], in0=mv[:, 1:2], scalar1=LN_EPS, scalar2=None, op0=ALU.add),
                 reads=[b_st], writes=[b_st])
            P.op("act", lambda e: e.activation(out=mv[:, 2:3], in_=mv[:, 2:3], func=AF.Sqrt), reads=[b_st], writes=[b_st])
            P.op("dve", lambda e: e.reciprocal(out=mv[:, 3:4], in_=mv[:, 2:3]), reads=[b_st], writes=[b_st])
            P.op("dve", lambda e: e.tensor_scalar(out=dst[:], in0=src[:], scalar1=mv[:, 0:1], scalar2=mv[:, 3:4],
                                                  op0=ALU.subtract, op1=ALU.mult), reads=[b_src, b_st], writes=[b_dst])
            P.op("pool", lambda e: e.tensor_tensor(out=dst[:], in0=dst[:], in1=gbc[:], op=ALU.mult), reads=[b_dst, b_ln], writes=[b_dst])
            P.op("pool", lambda e: e.tensor_tensor(out=dst[:], in0=dst[:], in1=bbc[:], op=ALU.add), reads=[b_dst, b_ln], writes=[b_dst])

        barrier(6)
        with ExitStack() as s6:
            stg = [sb("stg%d" % i, [128, 2048], F32, s6) for i in range(3)]
            b_stg = [Buf("stg%d" % i) for i in range(3)]
            WA = sb("WA", [128, 6 * 1024], BF16, s6)
            WB = sb("WB", [128, 2 * 1024], BF16, s6)
            WO = sb("WO", [128, 8 * 1024], BF16, s6)
            b_WA, b_WB, b_WO = Buf("WA"), Buf("WB"), Buf("WO")
            load_w(WA, wa_d, 6 * 1024, stg, b_stg, b_WA)
            load_w(WB, wb_d, 2 * 1024, stg, b_stg, b_WB)
            load_w(WO, wo_d, 8 * 1024, stg, b_stg, b_WO)
            gbc = sb("gbc1", [128, D], F32, s6)
            bbc = sb("bbc1", [128, D], F32, s6)
            b_ln1 = Buf("ln1")
            P.op("sp", lambda e, gbc=gbc: e.dma_start(out=gbc[:], in_=ln_d[0:1, :].partition_broadcast(128)), writes=[b_ln1], dma=True)
            P.op("sp", lambda e, bbc=bbc: e.dma_start(out=bbc[:], in_=ln_d[1:2, :].partition_broadcast(128)), writes=[b_ln1], dma=True)
            oac = [sb("oac%d" % i, [128, 6, 512], BF16, s6) for i in range(2)]
            obc = [sb("obc%d" % i, [128, 2, 512], BF16, s6) for i in range(2)]
            gch = [sb("gch%d" % i, [128, 16, 512], BF16, s6) for i in range(2)]
            b_oac = [Buf("oac%d" % i) for i in range(2)]
            b_obc = [Buf("obc%d" % i) for i in range(2)]
            b_gch = [Buf("gch%d" % i) for i in range(2)]
            mg = [sb("mg%d" % i, [128, 8, 512], BF16, s6) for i in range(2)]
            b_mg = [Buf("mg%d" % i) for i in range(2)]
            mt1 = [sb("mt1_%d" % i, [128, 512], F32, s6) for i in range(2)]
            b_mt1 = [Buf("mt1_%d" % i) for i in range(2)]
            xt6 = [sb("xt6_%d" % i, [128, D], F32, s6) for i in range(2)]
            b_xt6 = [Buf("xt6_%d" % i) for i in range(2)]
            hp6 = [sb("hp6_%d" % i, [128, D], F32, s6) for i in range(2)]
            b_hp6 = [Buf("hp6_%d" % i) for i in range(2)]
            h1s = [sb("h1s_%d" % i, [128, D], F32, s6) for i in range(2)]
            b_h1s = [Buf("h1s_%d" % i) for i in range(2)]
            h1t = [sb("h1t_%d" % i, [128, 8, 128], BF16, s6) for i in range(2)]
            b_h1t = [Buf("h1t_%d" % i) for i in range(2)]
            stats = sb("stats6", [128, 12], F32, s6)
            mv = sb("mv6", [128, 4], F32, s6)
            b_st = Buf("st6")
            mc = 0
            tcnt = 0
            pend6 = None

            def p6a_back(tt, ts_):
                for hf in range(2):
                    bk = 6 + hf
                    for j in range(4):
                        kc = hf * 4 + j
                        P.op("pe", lambda e, bk=bk, j=j, kc=kc: e.transpose(
                            out=banks[bk][:, j * 128:(j + 1) * 128], in_=h1s[ts_][:, kc * 128:(kc + 1) * 128], identity=ident_f),
                            reads=[b_h1s[ts_], b_cst], writes=[bbuf[bk]])
                    P.op("act", lambda e, bk=bk, hf=hf: e.activation(
                        out=h1t[ts_][:, hf * 4:hf * 4 + 4, :], in_=banks[bk][:].rearrange("p (a b) -> p a b", a=4), func=AF.Copy),
                        reads=[bbuf[bk]], writes=[b_h1t[ts_]])
                P.op("sp", lambda e: e.dma_start(
                    out=H1T[:, :, tt * 128:(tt + 1) * 128].rearrange("c p t -> p c t"), in_=h1t[ts_][:]),
                    reads=[b_h1t[ts_]], pw=[b_H1T], dma=True)

            for n in range(NCH):
                cs = n % 2
                tsl = slice(n * 512, (n + 1) * 512)
                P.op("sp", lambda e, cs=cs, tsl=tsl: e.dma_start(out=oac[cs][:], in_=OA_T[:, :, tsl].rearrange("c p t -> p c t")),
                     reads=[b_OA], writes=[b_oac[cs]], dma=True)
                P.op("sp", lambda e, cs=cs, tsl=tsl: e.dma_start(out=obc[cs][:], in_=OB_T[:, :, tsl].rearrange("c p t -> p c t")),
                     reads=[b_OB], writes=[b_obc[cs]], dma=True)
                P.op("sp", lambda e, cs=cs, tsl=tsl: e.dma_start(out=gch[cs][:], in_=G_T[:, :, tsl].rearrange("c p t -> p c t")),
                     reads=[b_G], writes=[b_gch[cs]], dma=True)
                for oc_ in range(8):
                    ba = (mc % 2) * 2
                    bb = ba + 1
                    m1 = mc % 2
                    mc += 1
                    for kc in range(6):
                        P.op("pe", lambda e, ba=ba, kc=kc, oc_=oc_, cs=cs: e.matmul(
                            banks[ba][:], lhsT=WA[:, kc * 1024 + oc_ * 128:kc * 1024 + oc_ * 128 + 128], rhs=oac[cs][:, kc, :],
                            start=(kc == 0), stop=(kc == 5)), reads=[b_WA, b_oac[cs]], writes=[bbuf[ba]])
                    for kc in range(2):
                        P.op("pe", lambda e, bb=bb, kc=kc, oc_=oc_, cs=cs: e.matmul(
                            banks[bb][:], lhsT=WB[:, kc * 1024 + oc_ * 128:kc * 1024 + oc_ * 128 + 128], rhs=obc[cs][:, kc, :],
                            start=(kc == 0), stop=(kc == 1)), reads=[b_WB, b_obc[cs]], writes=[bbuf[bb]])
                    P.op("dve", lambda e, ba=ba, m1=m1, oc_=oc_, cs=cs: e.tensor_tensor(
                        out=mt1[m1][:], in0=banks[ba][:], in1=gch[cs][:, oc_, :], op=ALU.mult),
                        reads=[bbuf[ba], b_gch[cs]], writes=[b_mt1[m1]])
                    P.op("dve", lambda e, bb=bb, m1=m1, oc_=oc_, cs=cs: e.tensor_tensor(
                        out=mg[cs][:, oc_, :], in0=banks[bb][:], in1=gch[cs][:, 8 + oc_, :], op=ALU.mult),
                        reads=[bbuf[bb], b_gch[cs]], writes=[b_mg[cs]])
                    P.op("pool", lambda e, m1=m1, oc_=oc_, cs=cs: e.tensor_tensor(
                        out=mg[cs][:, oc_, :], in0=mg[cs][:, oc_, :], in1=mt1[m1][:], op=ALU.add),
                        reads=[b_mt1[m1], b_mg[cs]], writes=[b_mg[cs]])
                for t4 in range(4):
                    tt = n * 4 + t4
                    ts_ = tcnt % 2
                    tcnt += 1
                    P.op("sp", lambda e, ts_=ts_, tt=tt: e.dma_start(out=xt6[ts_][:], in_=x_d[tt * 128:(tt + 1) * 128, :]),
                         writes=[b_xt6[ts_]], dma=True)
                    for hf in range(2):
                        bk = (4 if ts_ == 0 else 2) + hf
                        for kc in range(8):
                            P.op("pe", lambda e, bk=bk, kc=kc, t4=t4, hf=hf, cs=cs: e.matmul(
                                banks[bk][:], lhsT=mg[cs][:, kc, t4 * 128:(t4 + 1) * 128],
                                rhs=WO[:, kc * 1024 + hf * 512:kc * 1024 + hf * 512 + 512], start=(kc == 0), stop=(kc == 7)),
                                reads=[b_mg[cs], b_WO], writes=[bbuf[bk]])
                        P.op("dve", lambda e, bk=bk, hf=hf, ts_=ts_: e.scalar_tensor_tensor(
                            out=hp6[ts_][:, hf * 512:(hf + 1) * 512], in0=xt6[ts_][:, hf * 512:(hf + 1) * 512], scalar=ALPHA,
                            in1=banks[bk][:], op0=ALU.mult, op1=ALU.add),
                            reads=[b_xt6[ts_], bbuf[bk]], writes=[b_hp6[ts_]])
                    if pend6 is not None:
                        p6a_back(*pend6)
                    layer_norm(hp6[ts_], b_hp6[ts_], h1s[ts_], b_h1s[ts_], gbc, bbc, b_ln1, stats, mv, b_st)
                    P.op("sp", lambda e, ts_=ts_, tt=tt: e.dma_start(out=H1[tt * 128:(tt + 1) * 128, :], in_=h1s[ts_][:]),
                         reads=[b_h1s[ts_]], pw=[b_H1], dma=True)
                    pend6 = (tt, ts_)
            if pend6 is not None:
                p6a_back(*pend6)

        barrier(7)
        with ExitStack() as s7:
            stg = [sb("stgb%d" % i, [128, 1024], F32, s7) for i in range(2)]
            b_stg = [Buf("stgb%d" % i) for i in range(2)]
            WG = sb("WG", [128, 8 * FFN], BF16, s7)
            WU = sb("WU", [128, 8 * FFN], BF16, s7)
            WD = sb("WD", [128, NHC * 1024], BF16, s7)
            b_WG, b_WU, b_WD = Buf("WG"), Buf("WU"), Buf("WD")
            load_w(WG, wg_d, 8 * FFN, stg, b_stg, b_WG, 1024)
            load_w(WU, wu_d, 8 * FFN, stg, b_stg, b_WU, 1024)
            load_w(WD, wd_d, NHC * 1024, stg, b_stg, b_WD, 1024, engs=("pool",))
            gbc = sb("gbc2", [128, D], F32, s7)
            bbc = sb("bbc2", [128, D], F32, s7)
            b_ln2 = Buf("ln2")
            P.op("sp", lambda e, gbc=gbc: e.dma_start(out=gbc[:], in_=ln_d[2:3, :].partition_broadcast(128)), writes=[b_ln2], dma=True)
            P.op("sp", lambda e, bbc=bbc: e.dma_start(out=bbc[:], in_=ln_d[3:4, :].partition_broadcast(128)), writes=[b_ln2], dma=True)
            hch = [sb("hch%d" % i, [128, 8, 512], BF16, s7) for i in range(1)]
            b_hch = [Buf("hch%d" % i) for i in range(1)]
            hid = [sb("hid%d" % i, [128, NHC, 512], BF16, s7) for i in range(1)]
            b_hid = [Buf("hid%d" % i) for i in range(1)]
            sg = [sb("sg%d" % i, [128, 512], F32, s7) for i in range(2)]
            b_sg = [Buf("sg%d" % i) for i in range(2)]
            h1r = [sb("h1r%d" % i, [128, D], F32, s7) for i in range(2)]
            b_h1r = [Buf("h1r%d" % i) for i in range(2)]
            hp7 = [sb("hp7_%d" % i, [128, D], F32, s7) for i in range(2)]
            b_hp7 = [Buf("hp7_%d" % i) for i in range(2)]
            o7, b_o7 = hp7, b_hp7
            stats = sb("stats7", [128, 12], F32, s7)
            mv = sb("mv7", [128, 4], F32, s7)
            b_st = Buf("st7")
            gc_ = 0
            tcnt = 0
            for n in range(NCH):
                cs = 0
                tsl = slice(n * 512, (n + 1) * 512)
                P.op("sp", lambda e, cs=cs, tsl=tsl: e.dma_start(out=hch[cs][:], in_=H1T[:, :, tsl].rearrange("c p t -> p c t")),
                     reads=[b_H1T], writes=[b_hch[cs]], dma=True)
                for hc in range(NHC):
                    bg_ = (gc_ % 2) * 2
                    bu_ = bg_ + 1
                    s1 = gc_ % 2
                    gc_ += 1
                    for kc in range(8):
                        P.op("pe", lambda e, bg_=bg_, kc=kc, hc=hc, cs=cs: e.matmul(
                            banks[bg_][:], lhsT=WG[:, kc * FFN + hc * 128:kc * FFN + hc * 128 + 128], rhs=hch[cs][:, kc, :],
                            start=(kc == 0), stop=(kc == 7)), reads=[b_WG, b_hch[cs]], writes=[bbuf[bg_]])
                    for kc in range(8):
                        P.op("pe", lambda e, bu_=bu_, kc=kc, hc=hc, cs=cs: e.matmul(
                            banks[bu_][:], lhsT=WU[:, kc * FFN + hc * 128:kc * FFN + hc * 128 + 128], rhs=hch[cs][:, kc, :],
                            start=(kc == 0), stop=(kc == 7)), reads=[b_WU, b_hch[cs]], writes=[bbuf[bu_]])
                    P.op("act", lambda e, bg_=bg_, s1=s1: e.activation(out=sg[s1][:], in_=banks[bg_][:], func=AF.Silu),
                         reads=[bbuf[bg_]], writes=[b_sg[s1]])
                    P.op("dve", lambda e, bu_=bu_, s1=s1, hc=hc, cs=cs: e.tensor_tensor(
                        out=hid[cs][:, hc, :], in0=banks[bu_][:], in1=sg[s1][:], op=ALU.mult),
                        reads=[bbuf[bu_], b_sg[s1]], writes=[b_hid[cs]])
                for t4 in range(4):
                    tt = n * 4 + t4
                    ts_ = tcnt % 2
                    tcnt += 1
                    P.op("sp", lambda e, ts_=ts_, tt=tt: e.dma_start(out=h1r[ts_][:], in_=H1[tt * 128:(tt + 1) * 128, :]),
                         reads=[b_H1], writes=[b_h1r[ts_]], dma=True)
                    for hf in range(2):
                        bk = 4 + (tcnt % 2) * 2 + hf
                        for hc in range(NHC):
                            P.op("pe", lambda e, bk=bk, hc=hc, t4=t4, hf=hf, cs=cs: e.matmul(
                                banks[bk][:], lhsT=hid[cs][:, hc, t4 * 128:(t4 + 1) * 128],
                                rhs=WD[:, hc * 1024 + hf * 512:hc * 1024 + hf * 512 + 512], start=(hc == 0), stop=(hc == NHC - 1)),
                                reads=[b_hid[cs], b_WD], writes=[bbuf[bk]])
                        P.op("dve", lambda e, bk=bk, hf=hf, ts_=ts_: e.scalar_tensor_tensor(
                            out=hp7[ts_][:, hf * 512:(hf + 1) * 512], in0=h1r[ts_][:, hf * 512:(hf + 1) * 512], scalar=ALPHA,
                            in1=banks[bk][:], op0=ALU.mult, op1=ALU.add),
                            reads=[b_h1r[ts_], bbuf[bk]], writes=[b_hp7[ts_]])
                    layer_norm(hp7[ts_], b_hp7[ts_], o7[ts_], b_o7[ts_], gbc, bbc, b_ln2, stats, mv, b_st)
                    P.op("sp", lambda e, ts_=ts_, tt=tt: e.dma_start(out=out_d[tt * 128:(tt + 1) * 128, :], in_=o7[ts_][:]),
                         reads=[b_o7[ts_]], writes=[Buf("outd")], dma=True, final=True)

        with ExitStack() as ee:
            block = ee.enter_context(nc.Block())
            P.emit(nc, ee, block)
    return nc


def _consts():
    c = np.zeros((128, NCONST), np.float32)
    idx = np.arange(128)
    c[:, C_ID:C_ID + 128] = np.eye(128, dtype=np.float32)
    perm = np.zeros((128, 128), np.float32)
    for m in range(128):
        mm = m % 64
        if mm < 8:
            perm[m + 8, m] = 1.0
        elif mm < 16:
            perm[m - 8, m] = 1.0
    c[:, C_PERM:C_PERM + 128] = perm
    qq, ss = idx[:, None], idx[None, :]
    c[:, C_TRIM:C_TRIM + 128] = np.where(ss <= qq, 0.0, -BIG)
    c[:, C_TRIP:C_TRIP + 128] = np.where(ss <= qq, 0.0, BIG)
    s_, u_ = idx[:, None], idx[None, :]
    c[:, C_MD:C_MD + 128] = (s_ <= u_).astype(np.float32)
    c[:, C_MP:C_MP + 128] = (s_ >= u_).astype(np.float32)
    inv_freq = (ROPE_THETA ** (-np.arange(0, 16, 2, dtype=np.float32) / 16.0)).astype(np.float32)
    for m in range(128):
        mm = m % 64
        if mm < 8:
            c[m, C_ROPE + 0] = -inv_freq[mm] / (2 * np.pi)
            c[m, C_ROPE + 1] = inv_freq[mm] / (2 * np.pi)
        elif mm < 16:
            c[m, C_ROPE + 0] = inv_freq[mm - 8] / (2 * np.pi)
            c[m, C_ROPE + 1] = inv_freq[mm - 8] / (2 * np.pi)
    c[:, C_ROPE + 2] = 0.25
    for i in range(NIT + 1):
        c[:, C_POW + i] = 2.0 ** (-(i + 1))
    return c


def _prep_weights(w_in, b_gate, w_branch_a, w_branch_b, w_out, ln1_g, ln1_b, w_ffn_gate, w_ffn_up, w_ffn_down, ln2_g, ln2_b):
    w_in = np.asarray(w_in, np.float32)[0]
    cols = []
    for base in (0, 768, 2304, 3072):
        for c in range(6):
            cols.append(np.arange(base + c * 128, base + (c + 1) * 128))
    for c in range(4):
        cols.append(np.arange(4608 + c * 128, 4608 + (c + 1) * 128))
    ki = np.arange(5120, 5184)
    cols.append(np.concatenate([ki, ki]))
    for c in range(16):
        cols.append(np.arange(5192 + c * 128, 5192 + (c + 1) * 128))
    w_fm = np.stack([w_in[:, cc].reshape(8, 128, 128).transpose(1, 0, 2).reshape(128, 1024) for cc in cols], 0)
    tmc = np.concatenate([np.arange(1536, 1920), np.arange(1920, 2304), np.arange(5184, 5192),
                          np.arange(3840, 4224), np.arange(4224, 4608)])
    w_tm = w_in[:, tmc].reshape(8, 128, 1544).transpose(1, 0, 2).reshape(128, 8 * 1544)

    def kmaj(w, nk):
        w = np.asarray(w, np.float32)[0]
        return np.ascontiguousarray(w.reshape(nk, 128, w.shape[1]).transpose(1, 0, 2).reshape(128, nk * w.shape[1]))

    d = {
        "w_fm": np.ascontiguousarray(w_fm),
        "w_tm": np.ascontiguousarray(w_tm),
        "bg": np.ascontiguousarray(np.asarray(b_gate, np.float32)[0].reshape(16, 128).T),
        "wa": kmaj(w_branch_a, 6),
        "wb": kmaj(w_branch_b, 2),
        "wo": kmaj(w_out, 8),
        "wg": kmaj(w_ffn_gate, 8),
        "wu": kmaj(w_ffn_up, 8),
        "wd": kmaj(w_ffn_down, NHC),
        "ln": np.ascontiguousarray(np.stack([np.asarray(a, np.float32)[0] for a in (ln1_g, ln1_b, ln2_g, ln2_b)], 0)),
        "cst": _consts(),
    }
    return d


_NC_CACHE = {}


def run(x, positions, weights, S, ktop):
    B = x.shape[0]
    key = (S, ktop)
    if key not in _NC_CACHE:
        _NC_CACHE[key] = build(S, ktop)
    nc = _NC_CACHE[key]
    wd = _prep_weights(**weights)
    in_maps = []
    for b in range(B):
        m = dict(wd)
        m["x"] = np.ascontiguousarray(np.asarray(x[b], np.float32))
        m["pos"] = np.ascontiguousarray(np.asarray(positions[b], np.int32).reshape(1, S))
        in_maps.append(m)
    res = run_bass_kernel_spmd(nc, in_maps, core_ids=list(range(B)))
    return np.stack([np.asarray(r["out"], np.float32) for r in res.results], 0)


def kernel(x, positions, w_in, b_gate, w_branch_a, w_branch_b, w_out, ln1_g, ln1_b,
           w_ffn_gate, w_ffn_up, w_ffn_down, ln2_g, ln2_b):
    x = np.asarray(x)
    S = x.shape[1]
    weights = dict(w_in=w_in, b_gate=b_gate, w_branch_a=w_branch_a, w_branch_b=w_branch_b, w_out=w_out,
                   ln1_g=ln1_g, ln1_b=ln1_b, w_ffn_gate=w_ffn_gate, w_ffn_up=w_ffn_up, w_ffn_down=w_ffn_down,
                   ln2_g=ln2_g, ln2_b=ln2_b)
    return run(x, np.asarray(positions), weights, S, min(256, S // 4))
```

```python
import math
from contextlib import ExitStack

import numpy as np
import concourse.bass as bass
import concourse.mybir as mybir
from concourse.bass_utils import run_bass_kernel_spmd

F32 = mybir.dt.float32
BF16 = mybir.dt.bfloat16
I32 = mybir.dt.int32
AF = mybir.ActivationFunctionType
ALU = mybir.AluOpType
AX = mybir.AxisListType

D = 1024
HD = 64
NQA = 768
FFN = 2816
NHC = FFN // 128
N_IN = 7240
ROPE_THETA = 500000.0
ALPHA = 2.0 ** 0.25
LN_EPS = 1e-5
NIT = 16
BIG = 1.0e30
MASK_NEG = 30000.0
SAME_ENGINE_SYNC = True
import os
KSTOP = int(os.environ.get('KSTOP', '99'))
KMAX = int(os.environ.get('KMAX', '100000000'))

C_ID, C_PERM, C_TRIM, C_TRIP, C_MD, C_MP, C_ROPE, C_POW = 0, 128, 256, 384, 512, 640, 768, 772
NCONST = C_POW + NIT + 1


class Buf:
    __slots__ = ("name", "W", "Wf", "R", "Rprev", "excl", "last")

    def __init__(self, name, excl=False):
        self.name = name
        self.excl = excl
        self.last = {}
        self.W = []
        self.Wf = []
        self.R = []
        self.Rprev = []


class Op:
    __slots__ = ("eng", "fn", "deps", "dma", "ms", "msidx", "dsem", "dval", "prev_dval", "seq", "cdeps")

    def __init__(self, eng, fn, dma):
        self.eng = eng
        self.fn = fn
        self.deps = []
        self.dma = dma
        self.ms = False
        self.msidx = 0
        self.dsem = None
        self.dval = 0
        self.prev_dval = 0


class Prog:
    ENGS = ("pe", "act", "dve", "pool", "sp")

    def __init__(self, n_dma_sems=24):
        self.ops = {e: [] for e in self.ENGS}
        self.n_dma_sems = n_dma_sems
        self.n_dma = 0
        self.final_dmas = []
        self.phase = Buf("phase")
        self.stopped = False

    def op(self, eng, fn, reads=(), writes=(), pw=(), dma=False, final=False, _barrier=False):
        if self.stopped:
            return None
        self.nops = getattr(self, "nops", 0) + 1
        if self.nops > KMAX:
            self.stopped = True
            return None
        o = Op(eng, fn, dma)
        deps = []
        for b in list(reads) + list(writes) + list(pw):
            if b.excl:
                deps.extend(op_ for e2, op_ in b.last.items() if e2 != eng)
                b.last[eng] = o
        reads = [b for b in reads if not b.excl]
        writes = [b for b in writes if not b.excl]
        pw = [b for b in pw if not b.excl]
        wset = set(id(b) for b in writes) | set(id(b) for b in pw)
        rlist = [b for b in reads if id(b) not in wset]
        if not _barrier:
            rlist = rlist + [self.phase]
        for b in rlist:
            deps.extend(b.W)
        for b in list(writes) + list(pw):
            if b.R:
                b.Rprev = b.R
                b.R = []
                oldW = b.W
                b.W = []
                b.Wf = []
                if any(id(b) == id(x) for x in reads):
                    deps.extend(oldW)
            deps.extend(b.Rprev)
            if any(id(b) == id(x) for x in writes):
                deps.extend(b.W)
            else:
                deps.extend(b.Wf)
        seen = set()
        for d in deps:
            if id(d) in seen or d is o:
                continue
            seen.add(id(d))
            if (not d.dma) and (not dma) and d.eng == eng and (eng == "pe" or not SAME_ENGINE_SYNC):
                continue
            o.deps.append(d)
        for b in rlist:
            b.R.append(o)
        for b in list(writes) + list(pw):
            b.W.append(o)
        for b in writes:
            b.Wf.append(o)
        if dma:
            k = self.n_dma % self.n_dma_sems
            o.dsem = k
            o.dval = 16 * (self.n_dma // self.n_dma_sems + 1)
            o.prev_dval = o.dval - 16
            self.n_dma += 1
            if final:
                self.final_dmas.append(o)
        o.seq = len(self.ops[eng])
        self.ops[eng].append(o)
        return o

    def barrier(self, fn):
        return self.op("pool", fn, writes=[self.phase], _barrier=True)

    def emit(self, nc, es, block):
        for e in self.ENGS:
            for o in self.ops[e]:
                best = {}
                for d in o.deps:
                    if not d.dma:
                        if d.eng not in best or best[d.eng].seq < d.seq:
                            best[d.eng] = d
                for d in best.values():
                    d.ms = True
                o.cdeps = list(best.values()) + [d for d in o.deps if d.dma]
        for e in self.ENGS:
            c = 0
            for o in self.ops[e]:
                if o.ms and not o.dma:
                    c += 1
                    o.msidx = c
        esem = {e: es.enter_context(nc.semaphore("sem_" + e)) for e in self.ENGS}
        dsems = [es.enter_context(nc.semaphore("dsem%d" % i)) for i in range(self.n_dma_sems)]
        prog = self

        def run(eng_name):
            def body(e):
                waited = {}

                def wait(key, sem, val):
                    if val <= 0:
                        return
                    if waited.get(key, 0) >= val:
                        return
                    waited[key] = val
                    e.wait_ge(sem, val)

                for o in prog.ops[eng_name]:
                    need = {}
                    for d in o.cdeps:
                        if d.dma:
                            k_, s_, v_ = ("d", d.dsem), dsems[d.dsem], d.dval
                        else:
                            k_, s_, v_ = ("e", d.eng), esem[d.eng], d.msidx
                        if k_ not in need or need[k_][1] < v_:
                            need[k_] = (s_, v_)
                    if o.dma:
                        k_ = ("d", o.dsem)
                        if k_ not in need or need[k_][1] < o.prev_dval:
                            need[k_] = (dsems[o.dsem], o.prev_dval)
                    for k_, (s_, v_) in need.items():
                        wait(k_, s_, v_)
                    inst = o.fn(e)
                    if o.dma:
                        inst.then_inc(dsems[o.dsem], 16)
                    elif o.ms:
                        inst.then_inc(esem[eng_name], 1)
                if eng_name == "sp":
                    for o in prog.final_dmas:
                        wait(("d", o.dsem), dsems[o.dsem], o.dval)
            return body

        block.tensor(run("pe"))
        block.scalar(run("act"))
        block.vector(run("dve"))
        block.gpsimd(run("pool"))
        block.sync(run("sp"))


def build(S, KTOP):
    NT = S // 128
    NCH = S // 512
    nc = bass.Bass("TRN2", target_bir_lowering=False)
    P = Prog()

    def din(name, shape, dt=F32):
        return nc.dram_tensor(name, list(shape), dt, kind="ExternalInput").ap()

    def dscr(name, shape, dt):
        return nc.dram_tensor(name, list(shape), dt, kind="Internal").ap()

    x_d = din("x", [S, D])
    pos_d = din("pos", [1, S], I32)
    wfm_d = din("w_fm", [45, 128, 1024])
    wtm_d = din("w_tm", [128, 8 * 1544])
    bg_d = din("bg", [128, 16])
    wa_d = din("wa", [128, 6 * 1024])
    wb_d = din("wb", [128, 2 * 1024])
    wo_d = din("wo", [128, 8 * 1024])
    wg_d = din("wg", [128, 8 * FFN])
    wu_d = din("wu", [128, 8 * FFN])
    wd_d = din("wd", [128, NHC * 1024])
    ln_d = din("ln", [4, D])
    cst_d = din("cst", [128, NCONST])
    out_d = nc.dram_tensor("out", [S, D], F32, kind="ExternalOutput").ap()

    QA_T = dscr("s_qat", [6, 128, S], BF16)
    KA_T = dscr("s_kat", [6, 128, S], BF16)
    QB_T = dscr("s_qbt", [6, 128, S], BF16)
    KB_T = dscr("s_kbt", [6, 128, S], BF16)
    QI_T = dscr("s_qit", [4, 128, S], BF16)
    KI_T = dscr("s_kit", [128, S], BF16)
    G_T = dscr("s_gt", [16, 128, S], BF16)
    VA_S = dscr("s_va", [S, 6 * 192], BF16)
    VB_S = dscr("s_vb", [S, 6 * 192], BF16)
    OA_T = dscr("s_oat", [6, 128, S], BF16)
    OB_T = dscr("s_obt", [2, 128, S], BF16)
    H1 = dscr("s_h1", [S, D], F32)
    H1T = dscr("s_h1t", [8, 128, S], BF16)
    b_QA, b_KA, b_QB, b_KB, b_QI, b_KI, b_G = (Buf(n) for n in ("QA", "KA", "QB", "KB", "QI", "KI", "G"))
    b_VA, b_VB, b_OA, b_OB, b_H1, b_H1T = (Buf(n) for n in ("VA", "VB", "OA", "OB", "H1", "H1T"))

    with ExitStack() as es:
        def sb(name, shape, dt, st=None):
            return (st or es).enter_context(nc.sbuf_tensor("sb_" + name, list(shape), dt))

        banks = [es.enter_context(nc.psum_tensor("bank%d" % i, [128, 512], F32)) for i in range(8)]
        bbuf = [Buf("bank%d" % i, excl=True) for i in range(8)]

        cst = sb("cst", [128, NCONST], F32)
        b_cst = Buf("cst")
        identb = sb("identb", [128, 128], BF16)
        permb = sb("permb", [128, 128], BF16)
        mb4 = sb("mb4", [128, 512], BF16)
        b_cb = Buf("constb")
        P.op("sp", lambda e: e.dma_start(out=cst[:], in_=cst_d[:, :]), writes=[b_cst], dma=True)
        P.op("dve", lambda e: e.tensor_copy(out=identb[:], in_=cst[:, C_ID:C_ID + 128]), reads=[b_cst], writes=[b_cb])
        P.op("dve", lambda e: e.tensor_copy(out=permb[:], in_=cst[:, C_PERM:C_PERM + 128]), reads=[b_cst], writes=[b_cb])
        for j in range(4):
            src = C_MD if j % 2 == 0 else C_MP
            P.op("dve", lambda e, j=j, src=src: e.tensor_copy(out=mb4[:, j * 128:(j + 1) * 128], in_=cst[:, src:src + 128]),
                 reads=[b_cst], writes=[b_cb])
        ident_f = cst[:, C_ID:C_ID + 128]
        negb = sb("negb", [128, 1], F32)
        P.op("dve", lambda e: e.memset(negb[:], -MASK_NEG), writes=[b_cb])

        WI = sb("WI", [128, NT, 8], F32)
        b_WI = Buf("WI")
        bar = sb("bar", [128, 8], F32)

        def barrier(k=0):
            if os.environ.get("KDBG"):
                print("barrier", k, getattr(P, "nops", 0))
            if k > KSTOP:
                P.stopped = True
            P.barrier(lambda e: e.memset(bar[:], 0.0))

        with ExitStack() as s2:
            XT = sb("XT", [128, 8, S], BF16, s2)
            b_XT = [Buf("XT%d" % t) for t in range(NT)]
            xs = [sb("xs%d" % i, [128, D], F32, s2) for i in range(2)]
            b_xs = [Buf("xs%d" % i) for i in range(2)]
            for tt in range(NT):
                sl = tt % 2
                P.op("sp", lambda e, tt=tt, sl=sl: e.dma_start(out=xs[sl][:], in_=x_d[tt * 128:(tt + 1) * 128, :]),
                     writes=[b_xs[sl]], dma=True)
                for half in range(2):
                    bk = (tt * 2 + half) % 2
                    for j in range(4):
                        kc = half * 4 + j
                        P.op("pe", lambda e, bk=bk, j=j, kc=kc, sl=sl: e.transpose(
                            out=banks[bk][:, j * 128:(j + 1) * 128], in_=xs[sl][:, kc * 128:(kc + 1) * 128], identity=ident_f),
                            reads=[b_xs[sl], b_cst], writes=[bbuf[bk]])
                    eng = "act" if half == 0 else "dve"
                    if eng == "act":
                        P.op("act", lambda e, bk=bk, half=half, tt=tt: e.activation(
                            out=XT[:, half * 4:half * 4 + 4, tt * 128:(tt + 1) * 128],
                            in_=banks[bk][:].rearrange("p (a b) -> p a b", a=4), func=AF.Copy),
                            reads=[bbuf[bk]], pw=[b_XT[tt]])
                    else:
                        P.op("dve", lambda e, bk=bk, half=half, tt=tt: e.tensor_copy(
                            out=XT[:, half * 4:half * 4 + 4, tt * 128:(tt + 1) * 128],
                            in_=banks[bk][:].rearrange("p (a b) -> p a b", a=4)),
                            reads=[bbuf[bk]], pw=[b_XT[tt]])

            CT = sb("CT", [128, S], F32, s2)
            ST = sb("ST", [128, S], F32, s2)
            b_CT, b_ST = Buf("CT"), Buf("ST")
            with ExitStack() as sr:
                posi = sb("posi", [128, S], I32, sr)
                posf = sb("posf", [128, S], F32, sr)
                tk = sb("ropek", [128, S], F32, sr)
                b_posi, b_posf, b_tk = Buf("posi"), Buf("posf"), Buf("tk")
                P.op("sp", lambda e: e.dma_start(out=posi[:], in_=pos_d[0:1, :].partition_broadcast(128)),
                     writes=[b_posi], dma=True)
                P.op("dve", lambda e: e.tensor_copy(out=posf[:], in_=posi[:]), reads=[b_posi], writes=[b_posf])
                MAGIC = 12582912.0
                for (T, bT, ca, cb) in ((ST, b_ST, C_ROPE + 0, None), (CT, b_CT, C_ROPE + 1, C_ROPE + 2)):
                    if cb is None:
                        P.op("dve", lambda e, T=T, ca=ca: e.tensor_scalar(
                            out=T[:], in0=posf[:], scalar1=cst[:, ca:ca + 1], scalar2=None, op0=ALU.mult),
                            reads=[b_posf, b_cst], writes=[bT])
                    else:
                        P.op("dve", lambda e, T=T, ca=ca, cb=cb: e.tensor_scalar(
                            out=T[:], in0=posf[:], scalar1=cst[:, ca:ca + 1], scalar2=cst[:, cb:cb + 1],
                            op0=ALU.mult, op1=ALU.add), reads=[b_posf, b_cst], writes=[bT])
                    P.op("dve", lambda e, T=T: e.tensor_scalar(out=tk[:], in0=T[:], scalar1=MAGIC, scalar2=None, op0=ALU.add),
                         reads=[bT], writes=[b_tk])
                    P.op("dve", lambda e, T=T: e.tensor_scalar(out=tk[:], in0=tk[:], scalar1=MAGIC, scalar2=None, op0=ALU.subtract),
                         reads=[b_tk], writes=[b_tk])
                    P.op("dve", lambda e, T=T: e.tensor_tensor(out=T[:], in0=T[:], in1=tk[:], op=ALU.subtract),
                         reads=[bT, b_tk], writes=[bT])
                    P.op("dve", lambda e, T=T: e.tensor_scalar(out=tk[:], in0=T[:], scalar1=0.5, scalar2=None, op0=ALU.is_gt),
                         reads=[bT], writes=[b_tk])
                    P.op("dve", lambda e, T=T: e.tensor_tensor(out=T[:], in0=T[:], in1=tk[:], op=ALU.subtract),
                         reads=[bT, b_tk], writes=[bT])
                    P.op("dve", lambda e, T=T: e.tensor_scalar(out=tk[:], in0=T[:], scalar1=-0.5, scalar2=None, op0=ALU.is_lt),
                         reads=[bT], writes=[b_tk])
                    P.op("dve", lambda e, T=T: e.tensor_tensor(out=T[:], in0=T[:], in1=tk[:], op=ALU.add),
                         reads=[bT, b_tk], writes=[bT])
                    P.op("act", lambda e, T=T: e.activation(out=T[:], in_=T[:], func=AF.Sin, scale=2.0 * math.pi * (1.0 - 1e-6)),
                         reads=[bT], writes=[bT])

            barrier(2)
            wst = [sb("wst%d" % i, [128, 1024], F32, s2) for i in range(2)]
            wbf = [sb("wbf%d" % i, [128, 8, 128], BF16, s2) for i in range(2)]
            b_wst = [Buf("wst%d" % i) for i in range(2)]
            b_wbf = [Buf("wbf%d" % i) for i in range(2)]
            bgt = sb("bgt", [128, 16], F32, s2)
            b_bgt = Buf("bgt")
            P.op("sp", lambda e: e.dma_start(out=bgt[:], in_=bg_d[:, :]), writes=[b_bgt], dma=True)
            qtmp = [sb("qtmp%d" % i, [128, 512], BF16, s2) for i in range(2)]
            b_qtmp = [Buf("qtmp%d" % i) for i in range(2)]
            rt1 = [sb("rt1_%d" % i, [128, 512], F32, s2) for i in range(2)]
            rt2 = [sb("rt2_%d" % i, [128, 512], F32, s2) for i in range(2)]
            b_rt1 = [Buf("rt1_%d" % i) for i in range(2)]
            b_rt2 = [Buf("rt2_%d" % i) for i in range(2)]
            ost = [sb("ost%d" % i, [128, 512], BF16, s2) for i in range(3)]
            b_ost = [Buf("ost%d" % i) for i in range(3)]
            tiles = []
            for c in range(6):
                tiles.append((QA_T[c], b_QA, "rope"))
            for c in range(6):
                tiles.append((KA_T[c], b_KA, "rope"))
            for c in range(6):
                tiles.append((QB_T[c], b_QB, "rope"))
            for c in range(6):
                tiles.append((KB_T[c], b_KB, "rope"))
            for c in range(4):
                tiles.append((QI_T[c], b_QI, "rope"))
            tiles.append((KI_T, b_KI, "rope"))
            for c in range(16):
                tiles.append((G_T[c], b_G, "gate"))
            def fm_main(idx):
                ci, n = idx // NCH, idx % NCH
                dst, bdst, kind = tiles[ci]
                ws = ci % 2
                if n == 0:
                    for cj in ([0, 1] if ci == 0 else [ci + 1]):
                        if cj < len(tiles):
                            wj = cj % 2
                            P.op("act", lambda e, cj=cj, wj=wj: e.dma_start(out=wst[wj][:], in_=wfm_d[cj]), writes=[b_wst[wj]], dma=True)
                            P.op("pool", lambda e, wj=wj: e.tensor_copy(out=wbf[wj][:].rearrange("p a b -> p (a b)"), in_=wst[wj][:]),
                                 reads=[b_wst[wj]], writes=[b_wbf[wj]])
                bk = idx % 2
                tsl = slice(n * 512, (n + 1) * 512)
                xdeps = [b_XT[t] for t in range(n * 4, n * 4 + 4)]
                for kc in range(8):
                    P.op("pe", lambda e, bk=bk, kc=kc, ws=ws, tsl=tsl: e.matmul(
                        banks[bk][:], lhsT=wbf[ws][:, kc, :], rhs=XT[:, kc, tsl], start=(kc == 0), stop=(kc == 7)),
                        reads=[b_wbf[ws]] + xdeps, writes=[bbuf[bk]])
                o3 = idx % 3
                if kind == "gate":
                    gc = ci - 29
                    P.op("act", lambda e, bk=bk, o3=o3, gc=gc: e.activation(
                        out=ost[o3][:], in_=banks[bk][:], func=AF.Sigmoid, bias=bgt[:, gc:gc + 1]),
                        reads=[bbuf[bk], b_bgt], writes=[b_ost[o3]])
                    P.op("sp", lambda e, o3=o3, dst=dst, tsl=tsl: e.dma_start(out=dst[:, tsl], in_=ost[o3][:]),
                         reads=[b_ost[o3]], pw=[bdst], dma=True)
                else:
                    P.op("act", lambda e, bk=bk: e.activation(out=qtmp[bk][:], in_=banks[bk][:], func=AF.Copy),
                         reads=[bbuf[bk]], writes=[b_qtmp[bk]])

            def fm_rope(idx):
                ci, n = idx // NCH, idx % NCH
                dst, bdst, kind = tiles[ci]
                if kind == "gate":
                    return
                bk = idx % 2
                q2 = bk
                o3 = idx % 3
                tsl = slice(n * 512, (n + 1) * 512)
                P.op("pe", lambda e, bk=bk, q2=q2: e.matmul(banks[2 + bk][:], lhsT=permb[:], rhs=qtmp[q2][:], start=True, stop=True),
                     reads=[b_qtmp[q2], b_cb], writes=[bbuf[2 + bk]])
                P.op("pool", lambda e, q2=q2, tsl=tsl: e.tensor_tensor(out=rt1[q2][:], in0=qtmp[q2][:], in1=CT[:, tsl], op=ALU.mult),
                     reads=[b_qtmp[q2], b_CT], writes=[b_rt1[q2]])
                P.op("dve", lambda e, bk=bk, q2=q2, tsl=tsl: e.tensor_tensor(out=rt2[q2][:], in0=banks[2 + bk][:], in1=ST[:, tsl], op=ALU.mult),
                     reads=[bbuf[2 + bk], b_ST], writes=[b_rt2[q2]])
                P.op("dve", lambda e, q2=q2, o3=o3: e.tensor_tensor(out=ost[o3][:], in0=rt1[q2][:], in1=rt2[q2][:], op=ALU.add),
                     reads=[b_rt1[q2], b_rt2[q2]], writes=[b_ost[o3]])
                P.op("sp", lambda e, o3=o3, dst=dst, tsl=tsl: e.dma_start(out=dst[:, tsl], in_=ost[o3][:]),
                     reads=[b_ost[o3]], pw=[bdst], dma=True)

            nitems = len(tiles) * NCH
            for idx in range(nitems + 1):
                if idx < nitems:
                    fm_main(idx)
                if idx >= 1:
                    fm_rope(idx - 1)

            if os.environ.get("KDBG"):
                print("tm start", getattr(P, "nops", 0))
            wtm = sb("wtm", [128, 8, 1544], BF16, s2)
            b_wtm = Buf("wtm")
            for kc in range(8):
                ws = kc % 2
                P.op("sp", lambda e, kc=kc, ws=ws: e.dma_start(out=wst[ws][:, 0:1544 - 1024], in_=wtm_d[:, kc * 1544 + 1024:(kc + 1) * 1544]),
                     writes=[b_wst[ws]], dma=True)
                P.op("pool", lambda e, kc=kc, ws=ws: e.tensor_copy(out=wtm[:, kc, 1024:1544], in_=wst[ws][:, 0:520]),
                     reads=[b_wst[ws]], pw=[b_wtm])
                ws = (kc + 1) % 2
                P.op("sp", lambda e, kc=kc, ws=ws: e.dma_start(out=wst[ws][:], in_=wtm_d[:, kc * 1544:kc * 1544 + 1024]),
                     writes=[b_wst[ws]], dma=True)
                P.op("pool", lambda e, kc=kc, ws=ws: e.tensor_copy(out=wtm[:, kc, 0:1024], in_=wst[ws][:]),
                     reads=[b_wst[ws]], pw=[b_wtm])
            vst = [sb("vst%d" % i, [128, 2, 6, 192], BF16, s2) for i in range(2)]
            b_vst = [Buf("vst%d" % i) for i in range(2)]
            for i in range(2):
                P.op("pool", lambda e, i=i: e.memset(vst[i][:].rearrange("p a b c -> p (a b c)"), 1.0), writes=[b_vst[i]])
            colofs = [0, 384, 776, 1160]
            colw = [384, 392, 384, 384]
            for tt in range(NT):
                vs = tt % 2
                for part in range(4):
                    bk = 4 + (tt * 4 + part) % 4
                    for kc in range(8):
                        P.op("pe", lambda e, bk=bk, kc=kc, tt=tt, part=part: e.matmul(
                            banks[bk][:, 0:colw[part]], lhsT=XT[:, kc, tt * 128:(tt + 1) * 128],
                            rhs=wtm[:, kc, colofs[part]:colofs[part] + colw[part]], start=(kc == 0), stop=(kc == 7)),
                            reads=[b_wtm, b_XT[tt]], writes=[bbuf[bk]])
                    ab = part // 2
                    pr0 = (part % 2) * 3
                    src = banks[bk][:, 0:384].rearrange("p (a b) -> p a b", a=3)
                    P.op("act", lambda e, vs=vs, ab=ab, pr0=pr0, src=src: e.activation(
                        out=vst[vs][:, ab, pr0:pr0 + 3, 0:64], in_=src[:, :, 0:64], func=AF.Copy),
                        reads=[bbuf[bk]], pw=[b_vst[vs]])
                    P.op("dve", lambda e, vs=vs, ab=ab, pr0=pr0, src=src: e.tensor_copy(
                        out=vst[vs][:, ab, pr0:pr0 + 3, 128:192], in_=src[:, :, 64:128]),
                        reads=[bbuf[bk]], pw=[b_vst[vs]])
                    if part == 1:
                        P.op("dve", lambda e, bk=bk, tt=tt: e.tensor_copy(out=WI[:, tt, :], in_=banks[bk][:, 384:392]),
                             reads=[bbuf[bk]], writes=[b_WI])
                P.op("sp", lambda e, vs=vs, tt=tt: e.dma_start(out=VA_S[tt * 128:(tt + 1) * 128, :],
                                                              in_=vst[vs][:, 0].rearrange("p b c -> p (b c)")),
                     reads=[b_vst[vs]], pw=[b_VA], dma=True)
                P.op("sp", lambda e, vs=vs, tt=tt: e.dma_start(out=VB_S[tt * 128:(tt + 1) * 128, :],
                                                              in_=vst[vs][:, 1].rearrange("p b c -> p (b c)")),
                     reads=[b_vst[vs]], pw=[b_VB], dma=True)

        barrier(3)
        with ExitStack() as s3:
            KA = sb("KA", [128, 6, S], BF16, s3)
            VA = sb("VA", [128, NT, 6 * 192], BF16, s3)
            KI = sb("KI", [128, S], BF16, s3)
            b_KAs, b_VAs, b_KIs = Buf("KAs"), Buf("VAs"), Buf("KIs")
            for c in range(6):
                P.op("sp", lambda e, c=c: e.dma_start(out=KA[:, c, :], in_=KA_T[c]), reads=[b_KA], pw=[b_KAs], dma=True)
            for t0 in range(0, NT, 8):
                t1 = min(NT, t0 + 8)
                P.op("sp", lambda e, t0=t0, t1=t1: e.dma_start(
                    out=VA[:, t0:t1, :], in_=VA_S[t0 * 128:t1 * 128, :].rearrange("(t p) c -> p t c", p=128)),
                    reads=[b_VA], pw=[b_VAs], dma=True)
            P.op("sp", lambda e: e.dma_start(out=KI[:], in_=KI_T), reads=[b_KI], writes=[b_KIs], dma=True)

            qch = sb("qch", [128, 6, 2, 256], BF16, s3)
            qich = sb("qich", [128, 4, 2, 256], BF16, s3)
            b_qch, b_qich = Buf("qch"), Buf("qich")
            P.op("pool", lambda e: e.memset(qch[:].rearrange("p a b c -> p (a b c)"), 0.0), writes=[b_qch])
            P.op("pool", lambda e: e.memset(qich[:].rearrange("p a b c -> p (a b c)"), 0.0), writes=[b_qich])
            scs = [sb("sc%d" % i, [128, S], F32, s3) for i in range(2)]
            b_scs = [Buf("sc%d" % i) for i in range(2)]
            junk = sb("junk", [128, S], BF16, s3)
            b_junk = Buf("junk")
            MT = [sb("MT%d" % i, [128, NT, 128], BF16, s3) for i in range(1)]
            b_MT = [Buf("MT%d" % i) for i in range(1)]
            rsb = [sb("rsb%d" % i, [128, 512], BF16, s3) for i in range(2)]
            b_rsb = [Buf("rsb%d" % i) for i in range(2)]
            esb = [sb("esb%d" % i, [128, 512], BF16, s3) for i in range(3)]
            b_esb = [Buf("esb%d" % i) for i in range(3)]
            otile = [sb("otile%d" % i, [128, 6, 128], BF16, s3) for i in range(2)]
            b_otile = [Buf("otile%d" % i) for i in range(2)]
            dsg = [sb("dsg%d" % i, [128, 8, 128], BF16, s3) for i in range(2)]
            b_dsg = [Buf("dsg%d" % i) for i in range(2)]
            wab = [sb("wab%d" % i, [128, 8], F32, s3) for i in range(2)]
            wsg = [sb("wsg%d" % i, [128, 8], F32, s3) for i in range(2)]
            b_wab = [Buf("wab%d" % i) for i in range(2)]
            b_wsg = [Buf("wsg%d" % i) for i in range(2)]
            dmin = [sb("dmin%d" % i, [128, 1], F32, s3) for i in range(2)]
            b_dmin = [Buf("dmin%d" % i) for i in range(2)]
            small = sb("small", [128, 16], F32, s3)
            b_small = Buf("small")
            hw = sb("hw", [128, NIT + 1], F32, s3)
            b_hw = Buf("hw")
            dtmp = sb("dtmp", [128, 128], F32, s3)
            b_dtmp = Buf("dtmp")
            rden = [sb("rden%d" % i, [128, 128], F32, s3) for i in range(2)]
            b_rden = [Buf("rden%d" % i) for i in range(2)]
            cnts = {"z": 0, "e": 0, "o": 0}

            def stageB1(qt):
                qc, qo = qt // 2, (qt % 2) * 128
                if qt % 2 == 0:
                    for hh_ in range(2):
                        P.op("sp", lambda e, qc=qc, hh_=hh_: e.dma_start(
                            out=qich[hh_ * 64:(hh_ + 1) * 64, :, hh_, :],
                            in_=QI_T[:, hh_ * 64:(hh_ + 1) * 64, qc * 256:(qc + 1) * 256].rearrange("c p t -> p c t")),
                            reads=[b_QI], writes=[b_qich], dma=True)
                n_s = qt + 1
                sc, b_sc = scs[qt % 2], b_scs[qt % 2]
                n = 128 * n_s
                nchk = (n_s + 3) // 4
                ds = qt % 2
                P.op("dve", lambda e, ds=ds, qt=qt: e.tensor_scalar(out=wsg[ds][:], in0=WI[:, qt, :], scalar1=0.0, scalar2=-0.5,
                                                                   op0=ALU.is_gt, op1=ALU.add),
                     reads=[b_WI], writes=[b_wsg[ds]])
                P.op("dve", lambda e, ds=ds, qt=qt: e.scalar_tensor_tensor(out=wsg[ds][:], in0=WI[:, qt, :], scalar=0.0, in1=wsg[ds][:],
                                                                          op0=ALU.is_lt, op1=ALU.subtract),
                     reads=[b_WI, b_wsg[ds]], writes=[b_wsg[ds]])
                P.op("dve", lambda e, ds=ds: e.tensor_scalar(out=wsg[ds][:], in0=wsg[ds][:], scalar1=-1.0, scalar2=0.5,
                                                            op0=ALU.mult, op1=ALU.add),
                     reads=[b_wsg[ds]], writes=[b_wsg[ds]])
                P.op("dve", lambda e, ds=ds, qt=qt: e.tensor_tensor(out=wab[ds][:], in0=WI[:, qt, :], in1=wsg[ds][:], op=ALU.mult),
                     reads=[b_WI, b_wsg[ds]], writes=[b_wab[ds]])
                for h in range(8):
                    P.op("dve", lambda e, ds=ds, h=h: e.tensor_scalar(out=dsg[ds][:, h, :], in0=cst[:, C_ID:C_ID + 128],
                                                                     scalar1=wsg[ds][:, h:h + 1], scalar2=None, op0=ALU.mult),
                         reads=[b_cst, b_wsg[ds]], writes=[b_dsg[ds]])
                yield
                items = [(c, h) for c in range(nchk) for h in range(8)]

                def z_part(c, h):
                    ncols = min(512, n - 512 * c)
                    hp, hh = h // 2, h % 2
                    zb = (c * 8 + h) % 2
                    rows = slice(hh * 64, hh * 64 + 64)
                    P.op("pe", lambda e, zb=zb, hh=hh, hp=hp, c=c, ncols=ncols: e.matmul(
                        banks[zb][:, 0:ncols], lhsT=qich[:, hp, hh, qo:qo + 128], rhs=KI[:, c * 512:c * 512 + ncols],
                        start=True, stop=True), reads=[b_qich, b_KIs], writes=[bbuf[zb]])
                    P.op("act", lambda e, zb=zb, ncols=ncols, h=h: e.activation(
                        out=rsb[zb][:, 0:ncols], in_=banks[zb][:, 0:ncols], func=AF.Relu, scale=wab[ds][:, h:h + 1]),
                        reads=[bbuf[zb], b_wab[ds]], writes=[b_rsb[zb]])

                def d_part(c, h):
                    ncols = min(512, n - 512 * c)
                    zb = (c * 8 + h) % 2
                    ab = 2 if c % 2 == 0 else 7
                    P.op("pe", lambda e, ab=ab, zb=zb, ncols=ncols, h=h: e.matmul(
                        banks[ab][:, 0:ncols], lhsT=dsg[ds][:, h, :], rhs=rsb[zb][:, 0:ncols], start=(h == 0), stop=(h == 7)),
                        reads=[b_dsg[ds], b_rsb[zb]], writes=[bbuf[ab]])
                    if h != 7:
                        return
                    last = (c == nchk - 1)
                    nfull = ncols - 128 if last else ncols
                    if nfull > 0:
                        P.op("dve", lambda e, ab=ab, c=c, nfull=nfull: e.tensor_copy(out=sc[:, c * 512:c * 512 + nfull], in_=banks[ab][:, 0:nfull]),
                             reads=[bbuf[ab]], writes=[b_sc])
                    if last:
                        dsl = slice(n - 128, n)
                        P.op("dve", lambda e, ab=ab, nfull=nfull: e.tensor_tensor(out=dtmp[:], in0=banks[ab][:, nfull:nfull + 128],
                                                                                 in1=cst[:, C_TRIP:C_TRIP + 128], op=ALU.add),
                             reads=[bbuf[ab], b_cst], writes=[b_dtmp])
                        P.op("dve", lambda e: e.tensor_reduce(out=dmin[qt % 2][:, 0:1], in_=dtmp[:], axis=AX.X, op=ALU.min),
                             reads=[b_dtmp], writes=[b_dmin[qt % 2]])
                        P.op("dve", lambda e, ab=ab, nfull=nfull, dsl=dsl: e.tensor_tensor(out=sc[:, dsl], in0=banks[ab][:, nfull:nfull + 128],
                                                                                          in1=cst[:, C_TRIM:C_TRIM + 128], op=ALU.add),
                             reads=[bbuf[ab], b_cst], writes=[b_sc])

                for i_ in range(len(items) + 1):
                    if i_ < len(items):
                        z_part(*items[i_])
                    if i_ >= 1:
                        d_part(*items[i_ - 1])
                    if i_ % 2 == 1:
                        yield

            def stageB2(qt):
                n_s = qt + 1
                n = 128 * n_s
                nchk = (n_s + 3) // 4
                sc, b_sc = scs[qt % 2], b_scs[qt % 2]
                if n > KTOP:
                    P.op("dve", lambda e, n=n: e.tensor_reduce(out=small[:, 0:1], in_=sc[:, 0:n], axis=AX.X, op=ALU.max),
                         reads=[b_sc], writes=[b_small])
                    P.op("dve", lambda e, n=n: e.tensor_reduce(out=small[:, 1:2], in_=sc[:, 0:n - 128], axis=AX.X, op=ALU.min),
                         reads=[b_sc], writes=[b_small])
                    P.op("dve", lambda e: e.tensor_tensor(out=small[:, 1:2], in0=small[:, 1:2], in1=dmin[qt % 2][:, 0:1], op=ALU.min),
                         reads=[b_small, b_dmin[qt % 2]], writes=[b_small])
                    P.op("dve", lambda e: e.tensor_tensor(out=small[:, 3:4], in0=small[:, 0:1], in1=small[:, 1:2], op=ALU.subtract),
                         reads=[b_small], writes=[b_small])
                    P.op("dve", lambda e: e.tensor_scalar(out=small[:, 3:4], in0=small[:, 3:4], scalar1=1.0001, scalar2=1e-6,
                                                          op0=ALU.mult, op1=ALU.add), reads=[b_small], writes=[b_small])
                    P.op("dve", lambda e: e.tensor_scalar(out=hw[:], in0=cst[:, C_POW:C_POW + NIT + 1], scalar1=small[:, 3:4], scalar2=None,
                                                          op0=ALU.mult), reads=[b_small, b_cst], writes=[b_hw])
                    P.op("dve", lambda e: e.tensor_tensor(out=small[:, 4:5], in0=small[:, 1:2], in1=hw[:, 0:1], op=ALU.add),
                         reads=[b_small, b_hw], writes=[b_small])
                    yield
                    for it in range(NIT):
                        P.op("dve", lambda e, n=n: e.tensor_scalar(out=junk[:, 0:n], in0=sc[:, 0:n], scalar1=small[:, 4:5], scalar2=None,
                                                                   op0=ALU.is_ge, op1=ALU.add, accum_out=small[:, 5:6]),
                             reads=[b_sc, b_small], writes=[b_junk, b_small])
                        P.op("dve", lambda e, it=it: e.tensor_scalar(out=small[:, 6:7], in0=small[:, 5:6], scalar1=float(KTOP) - 0.5,
                                                                     scalar2=hw[:, it:it + 1], op0=ALU.is_ge, op1=ALU.mult),
                             reads=[b_small, b_hw], writes=[b_small])
                        P.op("dve", lambda e, it=it: e.scalar_tensor_tensor(out=small[:, 4:5], in0=small[:, 6:7], scalar=hw[:, it + 1:it + 2],
                                                                            in1=small[:, 4:5], op0=ALU.subtract, op1=ALU.add),
                             reads=[b_small, b_hw], writes=[b_small])
                        yield
                    P.op("dve", lambda e: e.scalar_tensor_tensor(out=small[:, 7:8], in0=hw[:, NIT:NIT + 1], scalar=-2.0, in1=small[:, 4:5],
                                                                 op0=ALU.mult, op1=ALU.add),
                         reads=[b_small, b_hw], writes=[b_small])
                else:
                    P.op("dve", lambda e: e.memset(small[:, 7:8], -1.0e29), writes=[b_small])
                P.op("dve", lambda e, n=n: e.tensor_scalar(out=junk[:, 0:n], in0=sc[:, 0:n], scalar1=small[:, 7:8], scalar2=None, op0=ALU.is_ge),
                     reads=[b_sc, b_small], writes=[b_junk])
                yield

            def stageB2b(qt):
                n_s = qt + 1
                nchk = (n_s + 3) // 4
                ms = 0
                for c in range(nchk):
                    g = min(4, n_s - 4 * c)
                    tb = 7 if c % 2 == 0 else 2
                    tbank = banks[tb][:, 0:256].bitcast(BF16)
                    for j in range(g):
                        st = 4 * c + j
                        P.op("pe", lambda e, tbank=tbank, j=j, st=st: e.transpose(
                            out=tbank[:, j * 128:(j + 1) * 128], in_=junk[:, st * 128:(st + 1) * 128], identity=identb[:]),
                            reads=[b_junk, b_cb], writes=[bbuf[tb]])
                    P.op("act", lambda e, tbank=tbank, ms=ms, c=c, g=g: e.activation(
                        out=MT[ms][:, 4 * c:4 * c + g, :].rearrange("p a b -> p (a b)"), in_=tbank[:, 0:g * 128], func=AF.Identity,
                        scale=MASK_NEG, bias=negb[:, 0:1]),
                        reads=[bbuf[tb], b_cb], pw=[b_MT[ms]])
                    yield

            def stageC(qt):
                qc, qo = qt // 2, (qt % 2) * 128
                if qt % 2 == 0:
                    for hh_ in range(2):
                        P.op("sp", lambda e, qc=qc, hh_=hh_: e.dma_start(
                            out=qch[hh_ * 64:(hh_ + 1) * 64, :, hh_, :],
                            in_=QA_T[:, hh_ * 64:(hh_ + 1) * 64, qc * 256:(qc + 1) * 256].rearrange("c p t -> p c t")),
                            reads=[b_QA], writes=[b_qch], dma=True)
                n_s = qt + 1
                nchk = (n_s + 3) // 4
                ms = 0
                ot = qt % 2
                SK = 2
                items = [(h, c) for h in range(12) for c in range(nchk)]
                e0 = cnts["e"]
                cnts["e"] += len(items)
                o0 = cnts["o"]
                cnts["o"] += 12

                def qk_part(idx):
                    h, c = items[idx]
                    hp, hh = h // 2, h % 2
                    rows = slice(hh * 64, hh * 64 + 64)
                    g = min(4, n_s - 4 * c)
                    sbk = 3 + (e0 + idx) % 2
                    es_ = (e0 + idx) % 3
                    for j in range(g):
                        st = 4 * c + j
                        P.op("pe", lambda e, sbk=sbk, j=j, st=st, hh=hh, hp=hp: e.matmul(
                            banks[sbk][:, j * 128:(j + 1) * 128], lhsT=KA[:, hp, st * 128:(st + 1) * 128],
                            rhs=qch[:, hp, hh, qo:qo + 128], start=True, stop=False),
                            reads=[b_KAs, b_qch], writes=[bbuf[sbk]])
                        P.op("pe", lambda e, sbk=sbk, j=j, st=st: e.matmul(
                            banks[sbk][:, j * 128:(j + 1) * 128], lhsT=identb[:], rhs=MT[ms][:, st, :], start=False, stop=True),
                            reads=[b_cb, b_MT[ms]], writes=[bbuf[sbk]])
                    P.op("act", lambda e, sbk=sbk, es_=es_, g=g: e.activation(
                        out=esb[es_][:, 0:g * 128], in_=banks[sbk][:, 0:g * 128], func=AF.Exp, scale=0.125),
                        reads=[bbuf[sbk]], writes=[b_esb[es_]])

                def pv_part(idx):
                    h, c = items[idx]
                    hp, hh = h // 2, h % 2
                    g = min(4, n_s - 4 * c)
                    es_ = (e0 + idx) % 3
                    ob = 5 + (o0 + h) % 2
                    vofs = hp * 192 + hh * 64
                    for j in range(g):
                        st = 4 * c + j
                        P.op("pe", lambda e, ob=ob, es_=es_, j=j, st=st, vofs=vofs: e.matmul(
                            banks[ob][:, 0:128], lhsT=VA[:, st, vofs:vofs + 128], rhs=esb[es_][:, j * 128:(j + 1) * 128],
                            start=(st == 0), stop=(st == n_s - 1)),
                            reads=[b_VAs, b_esb[es_]], writes=[bbuf[ob]])
                    if c != nchk - 1:
                        return
                    num = slice(0, 64) if hh == 0 else slice(64, 128)
                    den = slice(64, 128) if hh == 0 else slice(0, 64)
                    rd = h % 2
                    P.op("act", lambda e, ob=ob, den=den, rd=rd: e.activation(out=rden[rd][den, :], in_=banks[ob][den, 0:128], func=AF.Ln),
                         reads=[bbuf[ob]], writes=[b_rden[rd]])
                    P.op("act", lambda e, den=den, rd=rd: e.activation(out=rden[rd][den, :], in_=rden[rd][den, :], func=AF.Exp, scale=-1.0),
                         reads=[b_rden[rd]], writes=[b_rden[rd]])
                    P.op("dve", lambda e, ob=ob, num=num, den=den, rd=rd, hp=hp: e.tensor_tensor(
                        out=otile[ot][num, hp, :], in0=banks[ob][num, 0:128], in1=rden[rd][den, :], op=ALU.mult),
                        reads=[bbuf[ob], b_rden[rd]], pw=[b_otile[ot]])

                for i_ in range(len(items) + SK):
                    if i_ < len(items):
                        qk_part(i_)
                    if i_ >= SK:
                        pv_part(i_ - SK)
                    yield
                P.op("sp", lambda e, ot=ot, qt=qt: e.dma_start(
                    out=OA_T[:, :, qt * 128:(qt + 1) * 128].rearrange("c p t -> p c t"), in_=otile[ot][:]),
                    reads=[b_otile[ot]], pw=[b_OA], dma=True)
                yield

            def interleave(*gens):
                live = [g for g in gens if g is not None]
                while live:
                    nxt = []
                    for g in live:
                        try:
                            next(g)
                            nxt.append(g)
                        except StopIteration:
                            pass
                    live = nxt

            interleave(stageB1(0))
            interleave(stageB2(0), stageB1(1) if NT > 1 else None)
            interleave(stageB2b(0))
            for qt in range(NT):
                interleave(stageC(qt),
                           stageB2(qt + 1) if qt + 1 < NT else None,
                           stageB1(qt + 2) if qt + 2 < NT else None)
                if qt + 1 < NT:
                    interleave(stageB2b(qt + 1))

        barrier(5)
        with ExitStack() as s5:
            acc = [sb("acc%d" % i, [128, S], F32, s5) for i in range(4)]
            b_acc = [Buf("acc%d" % i) for i in range(4)]
            QB = sb("QBg", [128, 2, 2, S], BF16, s5)
            KB = sb("KBg", [128, 2, S], BF16, s5)
            VB = sb("VBg", [128, NT, 2 * 192], BF16, s5)
            b_QBs, b_KBs, b_VBs = Buf("QBs"), Buf("KBs"), Buf("VBs")
            psb5 = [sb("p5_%d" % i, [128, 512], BF16, s5) for i in range(3)]
            b_p5 = [Buf("p5_%d" % i) for i in range(3)]
            obt = sb("obt", [128, 2, S], BF16, s5)
            b_obt = Buf("obt")
            rd5 = sb("rd5", [128, S], F32, s5)
            b_rd5 = Buf("rd5")
            mb4n = sb("mb4n", [128, 512], BF16, s5)
            P.op("dve", lambda e: e.tensor_scalar(out=mb4n[:], in0=mb4[:], scalar1=MASK_NEG, scalar2=-MASK_NEG, op0=ALU.mult, op1=ALU.add),
                 reads=[b_cb], writes=[b_cb])
            P.op("pool", lambda e: e.memset(QB[:].rearrange("p a b c -> p (a b c)"), 0.0), writes=[b_QBs])
            oc = 0
            SK5 = 2
            for g, dil in enumerate((1, 4, 16)):
                L = S // dil
                nsub = L // 128
                for c in range(2):
                    for hh_ in range(2):
                        P.op("sp", lambda e, c=c, g=g, hh_=hh_: e.dma_start(
                            out=QB[hh_ * 64:(hh_ + 1) * 64, c, hh_, :], in_=QB_T[2 * g + c][hh_ * 64:(hh_ + 1) * 64, :]),
                            reads=[b_QB], pw=[b_QBs], dma=True)
                    P.op("sp", lambda e, c=c, g=g: e.dma_start(out=KB[:, c, :], in_=KB_T[2 * g + c]), reads=[b_KB], pw=[b_KBs], dma=True)
                for r in range(dil):
                    for iu0 in range(0, nsub, 8):
                        iu1 = min(nsub, iu0 + 8)
                        base = r + dil * 128 * iu0
                        cnt_rows = (iu1 - iu0) * 128
                        srcv = VB_S[base:base + dil * (cnt_rows - 1) + 1:dil, g * 384:(g + 1) * 384]
                        P.op("sp", lambda e, r=r, iu0=iu0, iu1=iu1, srcv=srcv, nsub=nsub: e.dma_start(
                            out=VB[:, r * nsub + iu0:r * nsub + iu1, :], in_=srcv.rearrange("(t p) c -> p t c", p=128)),
                            reads=[b_VB], pw=[b_VBs], dma=True)
                items5 = []
                for hl in range(4):
                    for r in range(dil):
                        blocks = [(0, 0, "D")]
                        for iu in range(1, nsub):
                            blocks.append((iu - 1, iu, "P"))
                            blocks.append((iu, iu, "D"))
                        cur_ob = None
                        for b0 in range(0, len(blocks), 4):
                            chunk = blocks[b0:b0 + 4]
                            obs = []
                            for (st, iu, kind) in chunk:
                                if iu % 4 == 0 and kind == ("D" if iu == 0 else "P"):
                                    cur_ob = 4 + oc % 4
                                    oc += 1
                                obs.append(cur_ob)
                            items5.append((hl, r, chunk, obs))

                def tok5(r, iu, dil=dil):
                    b0 = r + dil * 128 * iu
                    return slice(b0, b0 + dil * 127 + 1, dil)

                def qk5(idx, g=g, nsub=nsub):
                    hl, r, chunk, obs = items5[idx]
                    hp, hh = hl // 2, hl % 2
                    gsz = len(chunk)
                    sbk = idx % 3
                    for j, (st, iu, kind) in enumerate(chunk):
                        ksl, qsl = tok5(r, st), tok5(r, iu)
                        P.op("pe", lambda e, sbk=sbk, j=j, hp=hp, hh=hh, ksl=ksl, qsl=qsl: e.matmul(
                            banks[sbk][:, j * 128:(j + 1) * 128], lhsT=KB[:, hp, ksl], rhs=QB[:, hp, hh, qsl],
                            start=True, stop=False), reads=[b_KBs, b_QBs], writes=[bbuf[sbk]])
                        P.op("pe", lambda e, sbk=sbk, j=j: e.matmul(
                            banks[sbk][:, j * 128:(j + 1) * 128], lhsT=identb[:], rhs=mb4n[:, j * 128:(j + 1) * 128],
                            start=False, stop=True), reads=[b_cb], writes=[bbuf[sbk]])
                    P.op("act", lambda e, sbk=sbk, gsz=gsz: e.activation(
                        out=psb5[sbk][:, 0:gsz * 128], in_=banks[sbk][:, 0:gsz * 128], func=AF.Exp, scale=0.125),
                        reads=[bbuf[sbk]], writes=[b_p5[sbk]])

                def pv5(idx, g=g, nsub=nsub, dil=dil):
                    hl, r, chunk, obs = items5[idx]
                    hp, hh = hl // 2, hl % 2
                    vofs = hp * 192 + hh * 64
                    es_ = idx % 3
                    for j, (st, iu, kind) in enumerate(chunk):
                        ob = obs[j]
                        jo = iu % 4
                        P.op("pe", lambda e, ob=ob, jo=jo, es_=es_, j=j, st=st, r=r, vofs=vofs, kind=kind, iu=iu: e.matmul(
                            banks[ob][:, jo * 128:(jo + 1) * 128], lhsT=VB[:, r * nsub + st, vofs:vofs + 128],
                            rhs=psb5[es_][:, j * 128:(j + 1) * 128],
                            start=(kind == "P" or iu == 0), stop=(kind == "D")),
                            reads=[b_VBs, b_p5[es_]], writes=[bbuf[ob]])
                        if kind == "D" and (iu % 4 == 3 or iu == nsub - 1):
                            iu_lo = iu - (iu % 4)
                            nt_ = iu - iu_lo + 1
                            b00 = r + dil * 128 * iu_lo
                            dst = acc[hl][:, b00:b00 + dil * (nt_ * 128 - 1) + 1:dil]
                            if g == 0:
                                P.op("dve", lambda e, ob=ob, nt_=nt_, dst=dst: e.tensor_copy(out=dst, in_=banks[ob][:, 0:nt_ * 128]),
                                     reads=[bbuf[ob]], writes=[b_acc[hl]])
                            else:
                                P.op("dve", lambda e, ob=ob, nt_=nt_, dst=dst: e.tensor_tensor(
                                    out=dst, in0=banks[ob][:, 0:nt_ * 128], in1=dst, op=ALU.add),
                                    reads=[bbuf[ob], b_acc[hl]], writes=[b_acc[hl]])

                for i_ in range(len(items5) + SK5):
                    if i_ < len(items5):
                        qk5(i_)
                    if i_ >= SK5:
                        pv5(i_ - SK5)
            for hl in range(4):
                hp, hh = hl // 2, hl % 2
                num = slice(0, 64) if hh == 0 else slice(64, 128)
                den = slice(64, 128) if hh == 0 else slice(0, 64)
                P.op("dve", lambda e, hl=hl, den=den, num=num: e.reciprocal(out=rd5[num, :], in_=acc[hl][den, :]),
                     reads=[b_acc[hl]], writes=[b_rd5])
                P.op("dve", lambda e, hl=hl, num=num, den=den, hp=hp: e.tensor_tensor(
                    out=obt[num, hp, :], in0=acc[hl][num, :], in1=rd5[num, :], op=ALU.mult),
                    reads=[b_acc[hl], b_rd5], writes=[b_obt])
            for c in range(2):
                P.op("sp", lambda e, c=c: e.dma_start(out=OB_T[c], in_=obt[:, c, :]), reads=[b_obt], pw=[b_OB], dma=True)

        def load_w(dst2d, src2d, ncols, stg, b_stg, b_dst, cw=2048, engs=("pool", "dve", "act")):
            for i, c0 in enumerate(range(0, ncols, cw)):
                c1 = min(ncols, c0 + cw)
                sl = i % len(stg)
                P.op("sp", lambda e, c0=c0, c1=c1, sl=sl: e.dma_start(out=stg[sl][:, 0:c1 - c0], in_=src2d[:, c0:c1]),
                     writes=[b_stg[sl]], dma=True)
                ceng = engs[i % len(engs)]
                if ceng == "act":
                    P.op("act", lambda e, c0=c0, c1=c1, sl=sl: e.activation(out=dst2d[:, c0:c1], in_=stg[sl][:, 0:c1 - c0], func=AF.Copy),
                         reads=[b_stg[sl]], pw=[b_dst])
                else:
                    P.op(ceng, lambda e, c0=c0, c1=c1, sl=sl: e.tensor_copy(out=dst2d[:, c0:c1], in_=stg[sl][:, 0:c1 - c0]),
                         reads=[b_stg[sl]], pw=[b_dst])

        def layer_norm(src, b_src, dst, b_dst, gbc, bbc, b_ln, stats, mv, b_st):
            for hf in range(2):
                P.op("dve", lambda e, hf=hf: e.bn_stats(out=stats[:, hf * 6:(hf + 1) * 6], in_=src[:, hf * 512:(hf + 1) * 512]),
                     reads=[b_src], writes=[b_st])
            P.op("dve", lambda e: e.bn_aggr(out=mv[:, 0:2], in_=stats[:, 0:12]), reads=[b_st], writes=[b_st])
            P.op("dve", lambda e: e.tensor_scalar(out=mv[:, 2:3], in0=mv[:, 1:2], scalar1=LN_EPS, scalar2=None, op0=ALU.add),
                 reads=[b_st], writes=[b_st])
            P.op("act", lambda e: e.activation(out=mv[:, 2:3], in_=mv[:, 2:3], func=AF.Sqrt), reads=[b_st], writes=[b_st])
            P.op("dve", lambda e: e.reciprocal(out=mv[:, 3:4], in_=mv[:, 2:3]), reads=[b_st], writes=[b_st])
            P.op("dve", lambda e: e.tensor_scalar(out=dst[:], in0=src[:], scalar1=mv[:, 0:1], scalar2=mv[:, 3:4],
                                                  op0=ALU.subtract, op1=ALU.mult), reads=[b_src, b_st], writes=[b_dst])
            P.op("pool", lambda e: e.tensor_tensor(out=dst[:], in0=dst[:], in1=gbc[:], op=ALU.mult), reads=[b_dst, b_ln], writes=[b_dst])
            P.op("pool", lambda e: e.tensor_tensor(out=dst[:], in0=dst[:], in1=bbc[:], op=ALU.add), reads=[b_dst, b_ln], writes=[b_dst])

        barrier(6)
        with ExitStack() as s6:
            stg = [sb("stg%d" % i, [128, 2048], F32, s6) for i in range(3)]
            b_stg = [Buf("stg%d" % i) for i in range(3)]
            WA = sb("WA", [128, 6 * 1024], BF16, s6)
            WB = sb("WB", [128, 2 * 1024], BF16, s6)
            WO = sb("WO", [128, 8 * 1024], BF16, s6)
            b_WA, b_WB, b_WO = Buf("WA"), Buf("WB"), Buf("WO")
            load_w(WA, wa_d, 6 * 1024, stg, b_stg, b_WA)
            load_w(WB, wb_d, 2 * 1024, stg, b_stg, b_WB)
            load_w(WO, wo_d, 8 * 1024, stg, b_stg, b_WO)
            gbc = sb("gbc1", [128, D], F32, s6)
            bbc = sb("bbc1", [128, D], F32, s6)
            b_ln1 = Buf("ln1")
            P.op("sp", lambda e, gbc=gbc: e.dma_start(out=gbc[:], in_=ln_d[0:1, :].partition_broadcast(128)), writes=[b_ln1], dma=True)
            P.op("sp", lambda e, bbc=bbc: e.dma_start(out=bbc[:], in_=ln_d[1:2, :].partition_broadcast(128)), writes=[b_ln1], dma=True)
            oac = [sb("oac%d" % i, [128, 6, 512], BF16, s6) for i in range(2)]
            obc = [sb("obc%d" % i, [128, 2, 512], BF16, s6) for i in range(2)]
            gch = [sb("gch%d" % i, [128, 16, 512], BF16, s6) for i in range(2)]
            b_oac = [Buf("oac%d" % i) for i in range(2)]
            b_obc = [Buf("obc%d" % i) for i in range(2)]
            b_gch = [Buf("gch%d" % i) for i in range(2)]
            mg = [sb("mg%d" % i, [128, 8, 512], BF16, s6) for i in range(2)]
            b_mg = [Buf("mg%d" % i) for i in range(2)]
            mt1 = [sb("mt1_%d" % i, [128, 512], F32, s6) for i in range(2)]
            b_mt1 = [Buf("mt1_%d" % i) for i in range(2)]
            xt6 = [sb("xt6_%d" % i, [128, D], F32, s6) for i in range(2)]
            b_xt6 = [Buf("xt6_%d" % i) for i in range(2)]
            hp6 = [sb("hp6_%d" % i, [128, D], F32, s6) for i in range(2)]
            b_hp6 = [Buf("hp6_%d" % i) for i in range(2)]
            h1s = [sb("h1s_%d" % i, [128, D], F32, s6) for i in range(2)]
            b_h1s = [Buf("h1s_%d" % i) for i in range(2)]
            h1t = [sb("h1t_%d" % i, [128, 8, 128], BF16, s6) for i in range(2)]
            b_h1t = [Buf("h1t_%d" % i) for i in range(2)]
            stats = [sb("stats6_%d" % i, [128, 12], F32, s6) for i in range(2)]
            mv = [sb("mv6_%d" % i, [128, 4], F32, s6) for i in range(2)]
            b_st = [Buf("st6_%d" % i) for i in range(2)]
            mc = 0
            tcnt = 0
            pend6 = None

            def p6a_back(tt, ts_):
                for hf in range(2):
                    bk = 6 + hf
                    for j in range(4):
                        kc = hf * 4 + j
                        P.op("pe", lambda e, bk=bk, j=j, kc=kc: e.transpose(
                            out=banks[bk][:, j * 128:(j + 1) * 128], in_=h1s[ts_][:, kc * 128:(kc + 1) * 128], identity=ident_f),
                            reads=[b_h1s[ts_], b_cst], writes=[bbuf[bk]])
                    P.op("act", lambda e, bk=bk, hf=hf: e.activation(
                        out=h1t[ts_][:, hf * 4:hf * 4 + 4, :], in_=banks[bk][:].rearrange("p (a b) -> p a b", a=4), func=AF.Copy),
                        reads=[bbuf[bk]], writes=[b_h1t[ts_]])
                P.op("sp", lambda e: e.dma_start(
                    out=H1T[:, :, tt * 128:(tt + 1) * 128].rearrange("c p t -> p c t"), in_=h1t[ts_][:]),
                    reads=[b_h1t[ts_]], pw=[b_H1T], dma=True)

            for n in range(NCH):
                cs = n % 2
                tsl = slice(n * 512, (n + 1) * 512)
                P.op("sp", lambda e, cs=cs, tsl=tsl: e.dma_start(out=oac[cs][:], in_=OA_T[:, :, tsl].rearrange("c p t -> p c t")),
                     reads=[b_OA], writes=[b_oac[cs]], dma=True)
                P.op("sp", lambda e, cs=cs, tsl=tsl: e.dma_start(out=obc[cs][:], in_=OB_T[:, :, tsl].rearrange("c p t -> p c t")),
                     reads=[b_OB], writes=[b_obc[cs]], dma=True)
                P.op("sp", lambda e, cs=cs, tsl=tsl: e.dma_start(out=gch[cs][:], in_=G_T[:, :, tsl].rearrange("c p t -> p c t")),
                     reads=[b_G], writes=[b_gch[cs]], dma=True)
                for oc_ in range(8):
                    ba = (mc % 2) * 2
                    bb = ba + 1
                    m1 = mc % 2
                    mc += 1
                    for kc in range(6):
                        P.op("pe", lambda e, ba=ba, kc=kc, oc_=oc_, cs=cs: e.matmul(
                            banks[ba][:], lhsT=WA[:, kc * 1024 + oc_ * 128:kc * 1024 + oc_ * 128 + 128], rhs=oac[cs][:, kc, :],
                            start=(kc == 0), stop=(kc == 5)), reads=[b_WA, b_oac[cs]], writes=[bbuf[ba]])
                    for kc in range(2):
                        P.op("pe", lambda e, bb=bb, kc=kc, oc_=oc_, cs=cs: e.matmul(
                            banks[bb][:], lhsT=WB[:, kc * 1024 + oc_ * 128:kc * 1024 + oc_ * 128 + 128], rhs=obc[cs][:, kc, :],
                            start=(kc == 0), stop=(kc == 1)), reads=[b_WB, b_obc[cs]], writes=[bbuf[bb]])
                    P.op("dve", lambda e, ba=ba, m1=m1, oc_=oc_, cs=cs: e.tensor_tensor(
                        out=mt1[m1][:], in0=banks[ba][:], in1=gch[cs][:, oc_, :], op=ALU.mult),
                        reads=[bbuf[ba], b_gch[cs]], writes=[b_mt1[m1]])
                    P.op("dve", lambda e, bb=bb, m1=m1, oc_=oc_, cs=cs: e.tensor_tensor(
                        out=mg[cs][:, oc_, :], in0=banks[bb][:], in1=gch[cs][:, 8 + oc_, :], op=ALU.mult),
                        reads=[bbuf[bb], b_gch[cs]], writes=[b_mg[cs]])
                    P.op("pool", lambda e, m1=m1, oc_=oc_, cs=cs: e.tensor_tensor(
                        out=mg[cs][:, oc_, :], in0=mg[cs][:, oc_, :], in1=mt1[m1][:], op=ALU.add),
                        reads=[b_mt1[m1], b_mg[cs]], writes=[b_mg[cs]])
                for t4 in range(4):
                    tt = n * 4 + t4
                    ts_ = tcnt % 2
                    tcnt += 1
                    P.op("sp", lambda e, ts_=ts_, tt=tt: e.dma_start(out=xt6[ts_][:], in_=x_d[tt * 128:(tt + 1) * 128, :]),
                         writes=[b_xt6[ts_]], dma=True)
                    for hf in range(2):
                        bk = (4 if ts_ == 0 else 2) + hf
                        for kc in range(8):
                            P.op("pe", lambda e, bk=bk, kc=kc, t4=t4, hf=hf, cs=cs: e.matmul(
                                banks[bk][:], lhsT=mg[cs][:, kc, t4 * 128:(t4 + 1) * 128],
                                rhs=WO[:, kc * 1024 + hf * 512:kc * 1024 + hf * 512 + 512], start=(kc == 0), stop=(kc == 7)),
                                reads=[b_mg[cs], b_WO], writes=[bbuf[bk]])
                        P.op("dve", lambda e, bk=bk, hf=hf, ts_=ts_: e.scalar_tensor_tensor(
                            out=hp6[ts_][:, hf * 512:(hf + 1) * 512], in0=xt6[ts_][:, hf * 512:(hf + 1) * 512], scalar=ALPHA,
                            in1=banks[bk][:], op0=ALU.mult, op1=ALU.add),
                            reads=[b_xt6[ts_], bbuf[bk]], writes=[b_hp6[ts_]])
                    if pend6 is not None:
                        p6a_back(*pend6)
                    layer_norm(hp6[ts_], b_hp6[ts_], h1s[ts_], b_h1s[ts_], gbc, bbc, b_ln1, stats[ts_], mv[ts_], b_st[ts_])
                    P.op("sp", lambda e, ts_=ts_, tt=tt: e.dma_start(out=H1[tt * 128:(tt + 1) * 128, :], in_=h1s[ts_][:]),
                         reads=[b_h1s[ts_]], pw=[b_H1], dma=True)
                    pend6 = (tt, ts_)
            if pend6 is not None:
                p6a_back(*pend6)

        barrier(7)
        with ExitStack() as s7:
            stg = [sb("stgb%d" % i, [128, 1024], F32, s7) for i in range(2)]
            b_stg = [Buf("stgb%d" % i) for i in range(2)]
            WG = sb("WG", [128, 8 * FFN], BF16, s7)
            WU = sb("WU", [128, 8 * FFN], BF16, s7)
            WD = sb("WD", [128, NHC * 1024], BF16, s7)
            b_WG, b_WU, b_WD = Buf("WG"), Buf("WU"), Buf("WD")
            load_w(WG, wg_d, 8 * FFN, stg, b_stg, b_WG, 1024)
            load_w(WU, wu_d, 8 * FFN, stg, b_stg, b_WU, 1024)
            load_w(WD, wd_d, NHC * 1024, stg, b_stg, b_WD, 1024, engs=("pool",))
            gbc = sb("gbc2", [128, D], F32, s7)
            bbc = sb("bbc2", [128, D], F32, s7)
            b_ln2 = Buf("ln2")
            P.op("sp", lambda e, gbc=gbc: e.dma_start(out=gbc[:], in_=ln_d[2:3, :].partition_broadcast(128)), writes=[b_ln2], dma=True)
            P.op("sp", lambda e, bbc=bbc: e.dma_start(out=bbc[:], in_=ln_d[3:4, :].partition_broadcast(128)), writes=[b_ln2], dma=True)
            hch = [sb("hch%d" % i, [128, 8, 512], BF16, s7) for i in range(1)]
            b_hch = [Buf("hch%d" % i) for i in range(1)]
            hid = [sb("hid%d" % i, [128, NHC, 512], BF16, s7) for i in range(1)]
            b_hid = [Buf("hid%d" % i) for i in range(1)]
            sg = [sb("sg%d" % i, [128, 512], F32, s7) for i in range(2)]
            b_sg = [Buf("sg%d" % i) for i in range(2)]
            h1r = [sb("h1r%d" % i, [128, D], F32, s7) for i in range(2)]
            b_h1r = [Buf("h1r%d" % i) for i in range(2)]
            hp7 = [sb("hp7_%d" % i, [128, D], F32, s7) for i in range(2)]
            b_hp7 = [Buf("hp7_%d" % i) for i in range(2)]
            o7, b_o7 = hp7, b_hp7
            stats = [sb("stats7_%d" % i, [128, 12], F32, s7) for i in range(2)]
            mv = [sb("mv7_%d" % i, [128, 4], F32, s7) for i in range(2)]
            b_st = [Buf("st7_%d" % i) for i in range(2)]
            gc_ = 0
            tcnt = 0
            for n in range(NCH):
                cs = 0
                tsl = slice(n * 512, (n + 1) * 512)
                P.op("sp", lambda e, cs=cs, tsl=tsl: e.dma_start(out=hch[cs][:], in_=H1T[:, :, tsl].rearrange("c p t -> p c t")),
                     reads=[b_H1T], writes=[b_hch[cs]], dma=True)
                for hc in range(NHC):
                    bg_ = (gc_ % 2) * 2
                    bu_ = bg_ + 1
                    s1 = gc_ % 2
                    gc_ += 1
                    for kc in range(8):
                        P.op("pe", lambda e, bg_=bg_, kc=kc, hc=hc, cs=cs: e.matmul(
                            banks[bg_][:], lhsT=WG[:, kc * FFN + hc * 128:kc * FFN + hc * 128 + 128], rhs=hch[cs][:, kc, :],
                            start=(kc == 0), stop=(kc == 7)), reads=[b_WG, b_hch[cs]], writes=[bbuf[bg_]])
                    for kc in range(8):
                        P.op("pe", lambda e, bu_=bu_, kc=kc, hc=hc, cs=cs: e.matmul(
                            banks[bu_][:], lhsT=WU[:, kc * FFN + hc * 128:kc * FFN + hc * 128 + 128], rhs=hch[cs][:, kc, :],
                            start=(kc == 0), stop=(kc == 7)), reads=[b_WU, b_hch[cs]], writes=[bbuf[bu_]])
                    P.op("act", lambda e, bg_=bg_, s1=s1: e.activation(out=sg[s1][:], in_=banks[bg_][:], func=AF.Silu),
                         reads=[bbuf[bg_]], writes=[b_sg[s1]])
                    P.op("dve", lambda e, bu_=bu_, s1=s1, hc=hc, cs=cs: e.tensor_tensor(
                        out=hid[cs][:, hc, :], in0=banks[bu_][:], in1=sg[s1][:], op=ALU.mult),
                        reads=[bbuf[bu_], b_sg[s1]], writes=[b_hid[cs]])
                for t4 in range(4):
                    tt = n * 4 + t4
                    ts_ = tcnt % 2
                    tcnt += 1
                    P.op("sp", lambda e, ts_=ts_, tt=tt: e.dma_start(out=h1r[ts_][:], in_=H1[tt * 128:(tt + 1) * 128, :]),
                         reads=[b_H1], writes=[b_h1r[ts_]], dma=True)
                    for hf in range(2):
                        bk = 4 + (tcnt % 2) * 2 + hf
                        for hc in range(NHC):
                            P.op("pe", lambda e, bk=bk, hc=hc, t4=t4, hf=hf, cs=cs: e.matmul(
                                banks[bk][:], lhsT=hid[cs][:, hc, t4 * 128:(t4 + 1) * 128],
                                rhs=WD[:, hc * 1024 + hf * 512:hc * 1024 + hf * 512 + 512], start=(hc == 0), stop=(hc == NHC - 1)),
                                reads=[b_hid[cs], b_WD], writes=[bbuf[bk]])
                        P.op("dve", lambda e, bk=bk, hf=hf, ts_=ts_: e.scalar_tensor_tensor(
                            out=hp7[ts_][:, hf * 512:(hf + 1) * 512], in0=h1r[ts_][:, hf * 512:(hf + 1) * 512], scalar=ALPHA,
                            in1=banks[bk][:], op0=ALU.mult, op1=ALU.add),
                            reads=[b_h1r[ts_], bbuf[bk]], writes=[b_hp7[ts_]])
                    layer_norm(hp7[ts_], b_hp7[ts_], o7[ts_], b_o7[ts_], gbc, bbc, b_ln2, stats[ts_], mv[ts_], b_st[ts_])
                    P.op("sp", lambda e, ts_=ts_, tt=tt: e.dma_start(out=out_d[tt * 128:(tt + 1) * 128, :], in_=o7[ts_][:]),
                         reads=[b_o7[ts_]], writes=[Buf("outd")], dma=True, final=True)

        with ExitStack() as ee:
            block = ee.enter_context(nc.Block())
            P.emit(nc, ee, block)
    return nc


def _consts():
    c = np.zeros((128, NCONST), np.float32)
    idx = np.arange(128)
    c[:, C_ID:C_ID + 128] = np.eye(128, dtype=np.float32)
    perm = np.zeros((128, 128), np.float32)
    for m in range(128):
        mm = m % 64
        if mm < 8:
            perm[m + 8, m] = 1.0
        elif mm < 16:
            perm[m - 8, m] = 1.0
    c[:, C_PERM:C_PERM + 128] = perm
    qq, ss = idx[:, None], idx[None, :]
    c[:, C_TRIM:C_TRIM + 128] = np.where(ss <= qq, 0.0, -BIG)
    c[:, C_TRIP:C_TRIP + 128] = np.where(ss <= qq, 0.0, BIG)
    s_, u_ = idx[:, None], idx[None, :]
    c[:, C_MD:C_MD + 128] = (s_ <= u_).astype(np.float32)
    c[:, C_MP:C_MP + 128] = (s_ >= u_).astype(np.float32)
    inv_freq = (ROPE_THETA ** (-np.arange(0, 16, 2, dtype=np.float32) / 16.0)).astype(np.float32)
    for m in range(128):
        mm = m % 64
        if mm < 8:
            c[m, C_ROPE + 0] = -inv_freq[mm] / (2 * np.pi)
            c[m, C_ROPE + 1] = inv_freq[mm] / (2 * np.pi)
        elif mm < 16:
            c[m, C_ROPE + 0] = inv_freq[mm - 8] / (2 * np.pi)
            c[m, C_ROPE + 1] = inv_freq[mm - 8] / (2 * np.pi)
    c[:, C_ROPE + 2] = 0.25
    for i in range(NIT + 1):
        c[:, C_POW + i] = 2.0 ** (-(i + 1))
    return c


def _prep_weights(w_in, b_gate, w_branch_a, w_branch_b, w_out, ln1_g, ln1_b, w_ffn_gate, w_ffn_up, w_ffn_down, ln2_g, ln2_b):
    w_in = np.asarray(w_in, np.float32)[0]
    cols = []
    for base in (0, 768, 2304, 3072):
        for c in range(6):
            cols.append(np.arange(base + c * 128, base + (c + 1) * 128))
    for c in range(4):
        cols.append(np.arange(4608 + c * 128, 4608 + (c + 1) * 128))
    ki = np.arange(5120, 5184)
    cols.append(np.concatenate([ki, ki]))
    for c in range(16):
        cols.append(np.arange(5192 + c * 128, 5192 + (c + 1) * 128))
    w_fm = np.stack([w_in[:, cc].reshape(8, 128, 128).transpose(1, 0, 2).reshape(128, 1024) for cc in cols], 0)
    tmc = np.concatenate([np.arange(1536, 1920), np.arange(1920, 2304), np.arange(5184, 5192),
                          np.arange(3840, 4224), np.arange(4224, 4608)])
    w_tm = w_in[:, tmc].reshape(8, 128, 1544).transpose(1, 0, 2).reshape(128, 8 * 1544)

    def kmaj(w, nk):
        w = np.asarray(w, np.float32)[0]
        return np.ascontiguousarray(w.reshape(nk, 128, w.shape[1]).transpose(1, 0, 2).reshape(128, nk * w.shape[1]))

    d = {
        "w_fm": np.ascontiguousarray(w_fm),
        "w_tm": np.ascontiguousarray(w_tm),
        "bg": np.ascontiguousarray(np.asarray(b_gate, np.float32)[0].reshape(16, 128).T),
        "wa": kmaj(w_branch_a, 6),
        "wb": kmaj(w_branch_b, 2),
        "wo": kmaj(w_out, 8),
        "wg": kmaj(w_ffn_gate, 8),
        "wu": kmaj(w_ffn_up, 8),
        "wd": kmaj(w_ffn_down, NHC),
        "ln": np.ascontiguousarray(np.stack([np.asarray(a, np.float32)[0] for a in (ln1_g, ln1_b, ln2_g, ln2_b)], 0)),
        "cst": _consts(),
    }
    return d


_NC_CACHE = {}


def run(x, positions, weights, S, ktop):
    B = x.shape[0]
    key = (S, ktop)
    if key not in _NC_CACHE:
        _NC_CACHE[key] = build(S, ktop)
    nc = _NC_CACHE[key]
    wd = _prep_weights(**weights)
    in_maps = []
    for b in range(B):
        m = dict(wd)
        m["x"] = np.ascontiguousarray(np.asarray(x[b], np.float32))
        m["pos"] = np.ascontiguousarray(np.asarray(positions[b], np.int32).reshape(1, S))
        in_maps.append(m)
    res = run_bass_kernel_spmd(nc, in_maps, core_ids=list(range(B)))
    return np.stack([np.asarray(r["out"], np.float32) for r in res.results], 0)


def kernel(x, positions, w_in, b_gate, w_branch_a, w_branch_b, w_out, ln1_g, ln1_b,
           w_ffn_gate, w_ffn_up, w_ffn_down, ln2_g, ln2_b):
    x = np.asarray(x)
    S = x.shape[1]
    weights = dict(w_in=w_in, b_gate=b_gate, w_branch_a=w_branch_a, w_branch_b=w_branch_b, w_out=w_out,
                   ln1_g=ln1_g, ln1_b=ln1_b, w_ffn_gate=w_ffn_gate, w_ffn_up=w_ffn_up, w_ffn_down=w_ffn_down,
                   ln2_g=ln2_g, ln2_b=ln2_b)
    return run(x, np.asarray(positions), weights, S, min(256, S // 4))
```

```python
import math
from contextlib import ExitStack

import numpy as np
import concourse.bass as bass
import concourse.mybir as mybir
from concourse.bass_utils import run_bass_kernel_spmd

F32 = mybir.dt.float32
BF16 = mybir.dt.bfloat16
I32 = mybir.dt.int32
AF = mybir.ActivationFunctionType
ALU = mybir.AluOpType
AX = mybir.AxisListType

D = 1024
HD = 64
NQA = 768
FFN = 2816
NHC = FFN // 128
N_IN = 7240
ROPE_THETA = 500000.0
ALPHA = 2.0 ** 0.25
LN_EPS = 1e-5
NIT = 14
BIG = 1.0e30
MASK_NEG = 30000.0
SAME_ENGINE_SYNC = True
import os
KSTOP = int(os.environ.get('KSTOP', '99'))
KMAX = int(os.environ.get('KMAX', '100000000'))

C_ID, C_PERM, C_TRIM, C_TRIP, C_MD, C_MP, C_ROPE, C_POW = 0, 128, 256, 384, 512, 640, 768, 772
NCONST = C_POW + NIT + 1


class Buf:
    __slots__ = ("name", "W", "Wf", "R", "Rprev", "excl", "last")

    def __init__(self, name, excl=False):
        self.name = name
        self.excl = excl
        self.last = {}
        self.W = []
        self.Wf = []
        self.R = []
        self.Rprev = []


class Op:
    __slots__ = ("eng", "fn", "deps", "dma", "ms", "msidx", "dsem", "dval", "prev_dval", "seq", "cdeps")

    def __init__(self, eng, fn, dma):
        self.eng = eng
        self.fn = fn
        self.deps = []
        self.dma = dma
        self.ms = False
        self.msidx = 0
        self.dsem = None
        self.dval = 0
        self.prev_dval = 0


class Prog:
    ENGS = ("pe", "act", "dve", "pool", "sp")

    def __init__(self, n_dma_sems=24):
        self.ops = {e: [] for e in self.ENGS}
        self.n_dma_sems = n_dma_sems
        self.n_dma = 0
        self.final_dmas = []
        self.phase = Buf("phase")
        self.stopped = False

    def op(self, eng, fn, reads=(), writes=(), pw=(), dma=False, final=False, _barrier=False):
        if self.stopped:
            return None
        self.nops = getattr(self, "nops", 0) + 1
        if self.nops > KMAX:
            self.stopped = True
            return None
        o = Op(eng, fn, dma)
        deps = []
        for b in list(reads) + list(writes) + list(pw):
            if b.excl:
                deps.extend(op_ for e2, op_ in b.last.items() if e2 != eng)
                b.last[eng] = o
        reads = [b for b in reads if not b.excl]
        writes = [b for b in writes if not b.excl]
        pw = [b for b in pw if not b.excl]
        wset = set(id(b) for b in writes) | set(id(b) for b in pw)
        rlist = [b for b in reads if id(b) not in wset]
        if not _barrier:
            rlist = rlist + [self.phase]
        for b in rlist:
            deps.extend(b.W)
        for b in list(writes) + list(pw):
            if b.R:
                b.Rprev = b.R
                b.R = []
                oldW = b.W
                b.W = []
                b.Wf = []
                if any(id(b) == id(x) for x in reads):
                    deps.extend(oldW)
            deps.extend(b.Rprev)
            if any(id(b) == id(x) for x in writes):
                deps.extend(b.W)
            else:
                deps.extend(b.Wf)
        seen = set()
        for d in deps:
            if id(d) in seen or d is o:
                continue
            seen.add(id(d))
            if (not d.dma) and (not dma) and d.eng == eng and (eng == "pe" or not SAME_ENGINE_SYNC):
                continue
            o.deps.append(d)
        for b in rlist:
            b.R.append(o)
        for b in list(writes) + list(pw):
            b.W.append(o)
        for b in writes:
            b.Wf.append(o)
        if dma:
            k = self.n_dma % self.n_dma_sems
            o.dsem = k
            o.dval = 16 * (self.n_dma // self.n_dma_sems + 1)
            o.prev_dval = o.dval - 16
            self.n_dma += 1
            if final:
                self.final_dmas.append(o)
        o.seq = len(self.ops[eng])
        self.ops[eng].append(o)
        return o

    def barrier(self, fn):
        return self.op("pool", fn, writes=[self.phase], _barrier=True)

    def emit(self, nc, es, block):
        for e in self.ENGS:
            for o in self.ops[e]:
                best = {}
                for d in o.deps:
                    if not d.dma:
                        if d.eng not in best or best[d.eng].seq < d.seq:
                            best[d.eng] = d
                for d in best.values():
                    d.ms = True
                o.cdeps = list(best.values()) + [d for d in o.deps if d.dma]
        for e in self.ENGS:
            c = 0
            for o in self.ops[e]:
                if o.ms and not o.dma:
                    c += 1
                    o.msidx = c
        esem = {e: es.enter_context(nc.semaphore("sem_" + e)) for e in self.ENGS}
        dsems = [es.enter_context(nc.semaphore("dsem%d" % i)) for i in range(self.n_dma_sems)]
        prog = self

        def run(eng_name):
            def body(e):
                waited = {}

                def wait(key, sem, val):
                    if val <= 0:
                        return
                    if waited.get(key, 0) >= val:
                        return
                    waited[key] = val
                    e.wait_ge(sem, val)

                for o in prog.ops[eng_name]:
                    need = {}
                    for d in o.cdeps:
                        if d.dma:
                            k_, s_, v_ = ("d", d.dsem), dsems[d.dsem], d.dval
                        else:
                            k_, s_, v_ = ("e", d.eng), esem[d.eng], d.msidx
                        if k_ not in need or need[k_][1] < v_:
                            need[k_] = (s_, v_)
                    if o.dma:
                        k_ = ("d", o.dsem)
                        if k_ not in need or need[k_][1] < o.prev_dval:
                            need[k_] = (dsems[o.dsem], o.prev_dval)
                    for k_, (s_, v_) in need.items():
                        wait(k_, s_, v_)
                    inst = o.fn(e)
                    if o.dma:
                        inst.then_inc(dsems[o.dsem], 16)
                    elif o.ms:
                        inst.then_inc(esem[eng_name], 1)
                if eng_name == "sp":
                    for o in prog.final_dmas:
                        wait(("d", o.dsem), dsems[o.dsem], o.dval)
            return body

        block.tensor(run("pe"))
        block.scalar(run("act"))
        block.vector(run("dve"))
        block.gpsimd(run("pool"))
        block.sync(run("sp"))


def build(S, KTOP):
    NT = S // 128
    NCH = S // 512
    nc = bass.Bass("TRN2", target_bir_lowering=False)
    P = Prog()

    def din(name, shape, dt=F32):
        return nc.dram_tensor(name, list(shape), dt, kind="ExternalInput").ap()

    def dscr(name, shape, dt):
        return nc.dram_tensor(name, list(shape), dt, kind="Internal").ap()

    x_d = din("x", [S, D])
    pos_d = din("pos", [1, S], I32)
    wfm_d = din("w_fm", [45, 128, 1024])
    wtm_d = din("w_tm", [128, 8 * 1544])
    bg_d = din("bg", [128, 16])
    wa_d = din("wa", [128, 6 * 1024])
    wb_d = din("wb", [128, 2 * 1024])
    wo_d = din("wo", [128, 8 * 1024])
    wg_d = din("wg", [128, 8 * FFN])
    wu_d = din("wu", [128, 8 * FFN])
    wd_d = din("wd", [128, NHC * 1024])
    ln_d = din("ln", [4, D])
    cst_d = din("cst", [128, NCONST])
    out_d = nc.dram_tensor("out", [S, D], F32, kind="ExternalOutput").ap()

    QA_T = dscr("s_qat", [6, 128, S], BF16)
    KA_T = dscr("s_kat", [6, 128, S], BF16)
    QB_T = dscr("s_qbt", [6, 128, S], BF16)
    KB_T = dscr("s_kbt", [6, 128, S], BF16)
    QI_T = dscr("s_qit", [4, 128, S], BF16)
    KI_T = dscr("s_kit", [128, S], BF16)
    G_T = dscr("s_gt", [16, 128, S], BF16)
    VA_S = dscr("s_va", [S, 6 * 192], BF16)
    VB_S = dscr("s_vb", [S, 6 * 192], BF16)
    OA_T = dscr("s_oat", [6, 128, S], BF16)
    OB_T = dscr("s_obt", [2, 128, S], BF16)
    H1 = dscr("s_h1", [S, D], F32)
    H1T = dscr("s_h1t", [8, 128, S], BF16)
    b_QA, b_KA, b_QB, b_KB, b_QI, b_KI, b_G = (Buf(n) for n in ("QA", "KA", "QB", "KB", "QI", "KI", "G"))
    b_VA, b_VB, b_OA, b_OB, b_H1, b_H1T = (Buf(n) for n in ("VA", "VB", "OA", "OB", "H1", "H1T"))

    with ExitStack() as es:
        def sb(name, shape, dt, st=None):
            return (st or es).enter_context(nc.sbuf_tensor("sb_" + name, list(shape), dt))

        banks = [es.enter_context(nc.psum_tensor("bank%d" % i, [128, 512], F32)) for i in range(8)]
        bbuf = [Buf("bank%d" % i, excl=True) for i in range(8)]

        cst = sb("cst", [128, NCONST], F32)
        b_cst = Buf("cst")
        identb = sb("identb", [128, 128], BF16)
        permb = sb("permb", [128, 128], BF16)
        mb4 = sb("mb4", [128, 512], BF16)
        b_cb = Buf("constb")
        P.op("sp", lambda e: e.dma_start(out=cst[:], in_=cst_d[:, :]), writes=[b_cst], dma=True)
        P.op("dve", lambda e: e.tensor_copy(out=identb[:], in_=cst[:, C_ID:C_ID + 128]), reads=[b_cst], writes=[b_cb])
        P.op("dve", lambda e: e.tensor_copy(out=permb[:], in_=cst[:, C_PERM:C_PERM + 128]), reads=[b_cst], writes=[b_cb])
        for j in range(4):
            src = C_MD if j % 2 == 0 else C_MP
            P.op("dve", lambda e, j=j, src=src: e.tensor_copy(out=mb4[:, j * 128:(j + 1) * 128], in_=cst[:, src:src + 128]),
                 reads=[b_cst], writes=[b_cb])
        ident_f = cst[:, C_ID:C_ID + 128]
        negb = sb("negb", [128, 1], F32)
        P.op("dve", lambda e: e.memset(negb[:], -MASK_NEG), writes=[b_cb])

        WI = sb("WI", [128, NT, 8], F32)
        b_WI = Buf("WI")
        bar = sb("bar", [128, 8], F32)

        def barrier(k=0):
            if os.environ.get("KDBG"):
                print("barrier", k, getattr(P, "nops", 0))
            if k > KSTOP:
                P.stopped = True
            P.barrier(lambda e: e.memset(bar[:], 0.0))

        with ExitStack() as s2:
            XT = sb("XT", [128, 8, S], BF16, s2)
            b_XT = [Buf("XT%d" % t) for t in range(NT)]
            xs = [sb("xs%d" % i, [128, D], F32, s2) for i in range(2)]
            b_xs = [Buf("xs%d" % i) for i in range(2)]
            for tt in range(NT):
                sl = tt % 2
                P.op("sp", lambda e, tt=tt, sl=sl: e.dma_start(out=xs[sl][:], in_=x_d[tt * 128:(tt + 1) * 128, :]),
                     writes=[b_xs[sl]], dma=True)
                for half in range(2):
                    bk = (tt * 2 + half) % 2
                    for j in range(4):
                        kc = half * 4 + j
                        P.op("pe", lambda e, bk=bk, j=j, kc=kc, sl=sl: e.transpose(
                            out=banks[bk][:, j * 128:(j + 1) * 128], in_=xs[sl][:, kc * 128:(kc + 1) * 128], identity=ident_f),
                            reads=[b_xs[sl], b_cst], writes=[bbuf[bk]])
                    eng = "act" if half == 0 else "dve"
                    if eng == "act":
                        P.op("act", lambda e, bk=bk, half=half, tt=tt: e.activation(
                            out=XT[:, half * 4:half * 4 + 4, tt * 128:(tt + 1) * 128],
                            in_=banks[bk][:].rearrange("p (a b) -> p a b", a=4), func=AF.Copy),
                            reads=[bbuf[bk]], pw=[b_XT[tt]])
                    else:
                        P.op("dve", lambda e, bk=bk, half=half, tt=tt: e.tensor_copy(
                            out=XT[:, half * 4:half * 4 + 4, tt * 128:(tt + 1) * 128],
                            in_=banks[bk][:].rearrange("p (a b) -> p a b", a=4)),
                            reads=[bbuf[bk]], pw=[b_XT[tt]])

            CT = sb("CT", [128, S], F32, s2)
            ST = sb("ST", [128, S], F32, s2)
            b_CT, b_ST = Buf("CT"), Buf("ST")
            with ExitStack() as sr:
                posi = sb("posi", [128, S], I32, sr)
                posf = sb("posf", [128, S], F32, sr)
                tk = sb("ropek", [128, S], F32, sr)
                b_posi, b_posf, b_tk = Buf("posi"), Buf("posf"), Buf("tk")
                P.op("sp", lambda e: e.dma_start(out=posi[:], in_=pos_d[0:1, :].partition_broadcast(128)),
                     writes=[b_posi], dma=True)
                P.op("dve", lambda e: e.tensor_copy(out=posf[:], in_=posi[:]), reads=[b_posi], writes=[b_posf])
                MAGIC = 12582912.0
                for (T, bT, ca, cb) in ((ST, b_ST, C_ROPE + 0, None), (CT, b_CT, C_ROPE + 1, C_ROPE + 2)):
                    if cb is None:
                        P.op("dve", lambda e, T=T, ca=ca: e.tensor_scalar(
                            out=T[:], in0=posf[:], scalar1=cst[:, ca:ca + 1], scalar2=None, op0=ALU.mult),
                            reads=[b_posf, b_cst], writes=[bT])
                    else:
                        P.op("dve", lambda e, T=T, ca=ca, cb=cb: e.tensor_scalar(
                            out=T[:], in0=posf[:], scalar1=cst[:, ca:ca + 1], scalar2=cst[:, cb:cb + 1],
                            op0=ALU.mult, op1=ALU.add), reads=[b_posf, b_cst], writes=[bT])
                    P.op("dve", lambda e, T=T: e.tensor_scalar(out=tk[:], in0=T[:], scalar1=MAGIC, scalar2=None, op0=ALU.add),
                         reads=[bT], writes=[b_tk])
                    P.op("dve", lambda e, T=T: e.tensor_scalar(out=tk[:], in0=tk[:], scalar1=MAGIC, scalar2=None, op0=ALU.subtract),
                         reads=[b_tk], writes=[b_tk])
                    P.op("dve", lambda e, T=T: e.tensor_tensor(out=T[:], in0=T[:], in1=tk[:], op=ALU.subtract),
                         reads=[bT, b_tk], writes=[bT])
                    P.op("dve", lambda e, T=T: e.tensor_scalar(out=tk[:], in0=T[:], scalar1=0.5, scalar2=None, op0=ALU.is_gt),
                         reads=[bT], writes=[b_tk])
                    P.op("dve", lambda e, T=T: e.tensor_tensor(out=T[:], in0=T[:], in1=tk[:], op=ALU.subtract),
                         reads=[bT, b_tk], writes=[bT])
                    P.op("dve", lambda e, T=T: e.tensor_scalar(out=tk[:], in0=T[:], scalar1=-0.5, scalar2=None, op0=ALU.is_lt),
                         reads=[bT], writes=[b_tk])
                    P.op("dve", lambda e, T=T: e.tensor_tensor(out=T[:], in0=T[:], in1=tk[:], op=ALU.add),
                         reads=[bT, b_tk], writes=[bT])
                    P.op("act", lambda e, T=T: e.activation(out=T[:], in_=T[:], func=AF.Sin, scale=2.0 * math.pi * (1.0 - 1e-6)),
                         reads=[bT], writes=[bT])

            barrier(2)
            wst = [sb("wst%d" % i, [128, 1024], F32, s2) for i in range(2)]
            wbf = [sb("wbf%d" % i, [128, 8, 128], BF16, s2) for i in range(2)]
            b_wst = [Buf("wst%d" % i) for i in range(2)]
            b_wbf = [Buf("wbf%d" % i) for i in range(2)]
            bgt = sb("bgt", [128, 16], F32, s2)
            b_bgt = Buf("bgt")
            P.op("sp", lambda e: e.dma_start(out=bgt[:], in_=bg_d[:, :]), writes=[b_bgt], dma=True)
            qtmp = [sb("qtmp%d" % i, [128, 512], BF16, s2) for i in range(2)]
            b_qtmp = [Buf("qtmp%d" % i) for i in range(2)]
            rt1 = [sb("rt1_%d" % i, [128, 512], F32, s2) for i in range(2)]
            rt2 = [sb("rt2_%d" % i, [128, 512], F32, s2) for i in range(2)]
            b_rt1 = [Buf("rt1_%d" % i) for i in range(2)]
            b_rt2 = [Buf("rt2_%d" % i) for i in range(2)]
            ost = [sb("ost%d" % i, [128, 512], BF16, s2) for i in range(3)]
            b_ost = [Buf("ost%d" % i) for i in range(3)]
            tiles = []
            for c in range(6):
                tiles.append((QA_T[c], b_QA, "rope"))
            for c in range(6):
                tiles.append((KA_T[c], b_KA, "rope"))
            for c in range(6):
                tiles.append((QB_T[c], b_QB, "rope"))
            for c in range(6):
                tiles.append((KB_T[c], b_KB, "rope"))
            for c in range(4):
                tiles.append((QI_T[c], b_QI, "rope"))
            tiles.append((KI_T, b_KI, "rope"))
            for c in range(16):
                tiles.append((G_T[c], b_G, "gate"))
            def fm_main(idx):
                ci, n = idx // NCH, idx % NCH
                dst, bdst, kind = tiles[ci]
                ws = ci % 2
                if n == 0:
                    for cj in ([0, 1] if ci == 0 else [ci + 1]):
                        if cj < len(tiles):
                            wj = cj % 2
                            P.op("act", lambda e, cj=cj, wj=wj: e.dma_start(out=wst[wj][:], in_=wfm_d[cj]), writes=[b_wst[wj]], dma=True)
                            P.op("pool", lambda e, wj=wj: e.tensor_copy(out=wbf[wj][:].rearrange("p a b -> p (a b)"), in_=wst[wj][:]),
                                 reads=[b_wst[wj]], writes=[b_wbf[wj]])
                bk = idx % 2
                tsl = slice(n * 512, (n + 1) * 512)
                xdeps = [b_XT[t] for t in range(n * 4, n * 4 + 4)]
                for kc in range(8):
                    P.op("pe", lambda e, bk=bk, kc=kc, ws=ws, tsl=tsl: e.matmul(
                        banks[bk][:], lhsT=wbf[ws][:, kc, :], rhs=XT[:, kc, tsl], start=(kc == 0), stop=(kc == 7)),
                        reads=[b_wbf[ws]] + xdeps, writes=[bbuf[bk]])
                o3 = idx % 3
                if kind == "gate":
                    gc = ci - 29
                    P.op("act", lambda e, bk=bk, o3=o3, gc=gc: e.activation(
                        out=ost[o3][:], in_=banks[bk][:], func=AF.Sigmoid, bias=bgt[:, gc:gc + 1]),
                        reads=[bbuf[bk], b_bgt], writes=[b_ost[o3]])
                    P.op("sp", lambda e, o3=o3, dst=dst, tsl=tsl: e.dma_start(out=dst[:, tsl], in_=ost[o3][:]),
                         reads=[b_ost[o3]], pw=[bdst], dma=True)
                else:
                    P.op("act", lambda e, bk=bk: e.activation(out=qtmp[bk][:], in_=banks[bk][:], func=AF.Copy),
                         reads=[bbuf[bk]], writes=[b_qtmp[bk]])

            def fm_rope(idx):
                ci, n = idx // NCH, idx % NCH
                dst, bdst, kind = tiles[ci]
                if kind == "gate":
                    return
                bk = idx % 2
                q2 = bk
                o3 = idx % 3
                tsl = slice(n * 512, (n + 1) * 512)
                P.op("pe", lambda e, bk=bk, q2=q2: e.matmul(banks[2 + bk][:], lhsT=permb[:], rhs=qtmp[q2][:], start=True, stop=True),
                     reads=[b_qtmp[q2], b_cb], writes=[bbuf[2 + bk]])
                P.op("pool", lambda e, q2=q2, tsl=tsl: e.tensor_tensor(out=rt1[q2][:], in0=qtmp[q2][:], in1=CT[:, tsl], op=ALU.mult),
                     reads=[b_qtmp[q2], b_CT], writes=[b_rt1[q2]])
                P.op("dve", lambda e, bk=bk, q2=q2, tsl=tsl: e.tensor_tensor(out=rt2[q2][:], in0=banks[2 + bk][:], in1=ST[:, tsl], op=ALU.mult),
                     reads=[bbuf[2 + bk], b_ST], writes=[b_rt2[q2]])
                P.op("dve", lambda e, q2=q2, o3=o3: e.tensor_tensor(out=ost[o3][:], in0=rt1[q2][:], in1=rt2[q2][:], op=ALU.add),
                     reads=[b_rt1[q2], b_rt2[q2]], writes=[b_ost[o3]])
                P.op("sp", lambda e, o3=o3, dst=dst, tsl=tsl: e.dma_start(out=dst[:, tsl], in_=ost[o3][:]),
                     reads=[b_ost[o3]], pw=[bdst], dma=True)

            nitems = len(tiles) * NCH
            for idx in range(nitems + 1):
                if idx < nitems:
                    fm_main(idx)
                if idx >= 1:
                    fm_rope(idx - 1)

            if os.environ.get("KDBG"):
                print("tm start", getattr(P, "nops", 0))
            wtm = sb("wtm", [128, 8, 1544], BF16, s2)
            b_wtm = Buf("wtm")
            for kc in range(8):
                ws = kc % 2
                P.op("sp", lambda e, kc=kc, ws=ws: e.dma_start(out=wst[ws][:, 0:1544 - 1024], in_=wtm_d[:, kc * 1544 + 1024:(kc + 1) * 1544]),
                     writes=[b_wst[ws]], dma=True)
                P.op("pool", lambda e, kc=kc, ws=ws: e.tensor_copy(out=wtm[:, kc, 1024:1544], in_=wst[ws][:, 0:520]),
                     reads=[b_wst[ws]], pw=[b_wtm])
                ws = (kc + 1) % 2
                P.op("sp", lambda e, kc=kc, ws=ws: e.dma_start(out=wst[ws][:], in_=wtm_d[:, kc * 1544:kc * 1544 + 1024]),
                     writes=[b_wst[ws]], dma=True)
                P.op("pool", lambda e, kc=kc, ws=ws: e.tensor_copy(out=wtm[:, kc, 0:1024], in_=wst[ws][:]),
                     reads=[b_wst[ws]], pw=[b_wtm])
            vst = [sb("vst%d" % i, [128, 2, 6, 192], BF16, s2) for i in range(2)]
            b_vst = [Buf("vst%d" % i) for i in range(2)]
            for i in range(2):
                P.op("pool", lambda e, i=i: e.memset(vst[i][:].rearrange("p a b c -> p (a b c)"), 1.0), writes=[b_vst[i]])
            colofs = [0, 384, 776, 1160]
            colw = [384, 392, 384, 384]
            for tt in range(NT):
                vs = tt % 2
                for part in range(4):
                    bk = 4 + (tt * 4 + part) % 4
                    for kc in range(8):
                        P.op("pe", lambda e, bk=bk, kc=kc, tt=tt, part=part: e.matmul(
                            banks[bk][:, 0:colw[part]], lhsT=XT[:, kc, tt * 128:(tt + 1) * 128],
                            rhs=wtm[:, kc, colofs[part]:colofs[part] + colw[part]], start=(kc == 0), stop=(kc == 7)),
                            reads=[b_wtm, b_XT[tt]], writes=[bbuf[bk]])
                    ab = part // 2
                    pr0 = (part % 2) * 3
                    src = banks[bk][:, 0:384].rearrange("p (a b) -> p a b", a=3)
                    P.op("act", lambda e, vs=vs, ab=ab, pr0=pr0, src=src: e.activation(
                        out=vst[vs][:, ab, pr0:pr0 + 3, 0:64], in_=src[:, :, 0:64], func=AF.Copy),
                        reads=[bbuf[bk]], pw=[b_vst[vs]])
                    P.op("dve", lambda e, vs=vs, ab=ab, pr0=pr0, src=src: e.tensor_copy(
                        out=vst[vs][:, ab, pr0:pr0 + 3, 128:192], in_=src[:, :, 64:128]),
                        reads=[bbuf[bk]], pw=[b_vst[vs]])
                    if part == 1:
                        P.op("dve", lambda e, bk=bk, tt=tt: e.tensor_copy(out=WI[:, tt, :], in_=banks[bk][:, 384:392]),
                             reads=[bbuf[bk]], writes=[b_WI])
                P.op("sp", lambda e, vs=vs, tt=tt: e.dma_start(out=VA_S[tt * 128:(tt + 1) * 128, :],
                                                              in_=vst[vs][:, 0].rearrange("p b c -> p (b c)")),
                     reads=[b_vst[vs]], pw=[b_VA], dma=True)
                P.op("sp", lambda e, vs=vs, tt=tt: e.dma_start(out=VB_S[tt * 128:(tt + 1) * 128, :],
                                                              in_=vst[vs][:, 1].rearrange("p b c -> p (b c)")),
                     reads=[b_vst[vs]], pw=[b_VB], dma=True)

        barrier(3)
        with ExitStack() as s3:
            KA = sb("KA", [128, 6, S], BF16, s3)
            VA = sb("VA", [128, NT, 6 * 192], BF16, s3)
            KI = sb("KI", [128, S], BF16, s3)
            b_KAs, b_VAs, b_KIs = Buf("KAs"), Buf("VAs"), Buf("KIs")
            for c in range(6):
                P.op("sp", lambda e, c=c: e.dma_start(out=KA[:, c, :], in_=KA_T[c]), reads=[b_KA], pw=[b_KAs], dma=True)
            for t0 in range(0, NT, 8):
                t1 = min(NT, t0 + 8)
                P.op("sp", lambda e, t0=t0, t1=t1: e.dma_start(
                    out=VA[:, t0:t1, :], in_=VA_S[t0 * 128:t1 * 128, :].rearrange("(t p) c -> p t c", p=128)),
                    reads=[b_VA], pw=[b_VAs], dma=True)
            P.op("sp", lambda e: e.dma_start(out=KI[:], in_=KI_T), reads=[b_KI], writes=[b_KIs], dma=True)

            qch = sb("qch", [128, 6, 2, 256], BF16, s3)
            qich = sb("qich", [128, 4, 2, 256], BF16, s3)
            b_qch, b_qich = Buf("qch"), Buf("qich")
            P.op("pool", lambda e: e.memset(qch[:].rearrange("p a b c -> p (a b c)"), 0.0), writes=[b_qch])
            P.op("pool", lambda e: e.memset(qich[:].rearrange("p a b c -> p (a b c)"), 0.0), writes=[b_qich])
            scs = [sb("sc%d" % i, [128, S], F32, s3) for i in range(2)]
            b_scs = [Buf("sc%d" % i) for i in range(2)]
            junk = sb("junk", [128, S], BF16, s3)
            b_junk = Buf("junk")
            MT = [sb("MT%d" % i, [128, NT, 128], BF16, s3) for i in range(1)]
            b_MT = [Buf("MT%d" % i) for i in range(1)]
            rsb = [sb("rsb%d" % i, [128, 512], BF16, s3) for i in range(2)]
            b_rsb = [Buf("rsb%d" % i) for i in range(2)]
            esb = [sb("esb%d" % i, [128, 512], BF16, s3) for i in range(3)]
            b_esb = [Buf("esb%d" % i) for i in range(3)]
            otile = [sb("otile%d" % i, [128, 6, 128], BF16, s3) for i in range(2)]
            b_otile = [Buf("otile%d" % i) for i in range(2)]
            dsg = [sb("dsg%d" % i, [128, 8, 128], BF16, s3) for i in range(2)]
            b_dsg = [Buf("dsg%d" % i) for i in range(2)]
            wab = [sb("wab%d" % i, [128, 8], F32, s3) for i in range(2)]
            wsg = [sb("wsg%d" % i, [128, 8], F32, s3) for i in range(2)]
            b_wab = [Buf("wab%d" % i) for i in range(2)]
            b_wsg = [Buf("wsg%d" % i) for i in range(2)]
            dmin = [sb("dmin%d" % i, [128, 1], F32, s3) for i in range(2)]
            b_dmin = [Buf("dmin%d" % i) for i in range(2)]
            small = sb("small", [128, 16], F32, s3)
            b_small = Buf("small")
            hw = sb("hw", [128, NIT + 1], F32, s3)
            b_hw = Buf("hw")
            dtmp = sb("dtmp", [128, 128], F32, s3)
            b_dtmp = Buf("dtmp")
            rden = [sb("rden%d" % i, [128, 128], F32, s3) for i in range(2)]
            b_rden = [Buf("rden%d" % i) for i in range(2)]
            cnts = {"z": 0, "e": 0, "o": 0}

            def stageB1(qt):
                qc, qo = qt // 2, (qt % 2) * 128
                if qt % 2 == 0:
                    for hh_ in range(2):
                        P.op("sp", lambda e, qc=qc, hh_=hh_: e.dma_start(
                            out=qich[hh_ * 64:(hh_ + 1) * 64, :, hh_, :],
                            in_=QI_T[:, hh_ * 64:(hh_ + 1) * 64, qc * 256:(qc + 1) * 256].rearrange("c p t -> p c t")),
                            reads=[b_QI], writes=[b_qich], dma=True)
                n_s = qt + 1
                sc, b_sc = scs[qt % 2], b_scs[qt % 2]
                n = 128 * n_s
                nchk = (n_s + 3) // 4
                ds = qt % 2
                P.op("dve", lambda e, ds=ds, qt=qt: e.tensor_scalar(out=wsg[ds][:], in0=WI[:, qt, :], scalar1=0.0, scalar2=-0.5,
                                                                   op0=ALU.is_gt, op1=ALU.add),
                     reads=[b_WI], writes=[b_wsg[ds]])
                P.op("dve", lambda e, ds=ds, qt=qt: e.scalar_tensor_tensor(out=wsg[ds][:], in0=WI[:, qt, :], scalar=0.0, in1=wsg[ds][:],
                                                                          op0=ALU.is_lt, op1=ALU.subtract),
                     reads=[b_WI, b_wsg[ds]], writes=[b_wsg[ds]])
                P.op("dve", lambda e, ds=ds: e.tensor_scalar(out=wsg[ds][:], in0=wsg[ds][:], scalar1=-1.0, scalar2=0.5,
                                                            op0=ALU.mult, op1=ALU.add),
                     reads=[b_wsg[ds]], writes=[b_wsg[ds]])
                P.op("dve", lambda e, ds=ds, qt=qt: e.tensor_tensor(out=wab[ds][:], in0=WI[:, qt, :], in1=wsg[ds][:], op=ALU.mult),
                     reads=[b_WI, b_wsg[ds]], writes=[b_wab[ds]])
                for h in range(8):
                    P.op("dve", lambda e, ds=ds, h=h: e.tensor_scalar(out=dsg[ds][:, h, :], in0=cst[:, C_ID:C_ID + 128],
                                                                     scalar1=wsg[ds][:, h:h + 1], scalar2=None, op0=ALU.mult),
                         reads=[b_cst, b_wsg[ds]], writes=[b_dsg[ds]])
                yield
                items = [(c, h) for c in range(nchk) for h in range(8)]

                def z_part(c, h):
                    ncols = min(512, n - 512 * c)
                    hp, hh = h // 2, h % 2
                    zb = (c * 8 + h) % 2
                    rows = slice(hh * 64, hh * 64 + 64)
                    P.op("pe", lambda e, zb=zb, hh=hh, hp=hp, c=c, ncols=ncols: e.matmul(
                        banks[zb][:, 0:ncols], lhsT=qich[:, hp, hh, qo:qo + 128], rhs=KI[:, c * 512:c * 512 + ncols],
                        start=True, stop=True), reads=[b_qich, b_KIs], writes=[bbuf[zb]])
                    P.op("act", lambda e, zb=zb, ncols=ncols, h=h: e.activation(
                        out=rsb[zb][:, 0:ncols], in_=banks[zb][:, 0:ncols], func=AF.Relu, scale=wab[ds][:, h:h + 1]),
                        reads=[bbuf[zb], b_wab[ds]], writes=[b_rsb[zb]])

                def d_part(c, h):
                    ncols = min(512, n - 512 * c)
                    zb = (c * 8 + h) % 2
                    ab = 2 if c % 2 == 0 else 7
                    P.op("pe", lambda e, ab=ab, zb=zb, ncols=ncols, h=h: e.matmul(
                        banks[ab][:, 0:ncols], lhsT=dsg[ds][:, h, :], rhs=rsb[zb][:, 0:ncols], start=(h == 0), stop=(h == 7)),
                        reads=[b_dsg[ds], b_rsb[zb]], writes=[bbuf[ab]])
                    if h != 7:
                        return
                    last = (c == nchk - 1)
                    nfull = ncols - 128 if last else ncols
                    if nfull > 0:
                        P.op("dve", lambda e, ab=ab, c=c, nfull=nfull: e.tensor_copy(out=sc[:, c * 512:c * 512 + nfull], in_=banks[ab][:, 0:nfull]),
                             reads=[bbuf[ab]], writes=[b_sc])
                    if last:
                        dsl = slice(n - 128, n)
                        P.op("dve", lambda e, ab=ab, nfull=nfull: e.tensor_tensor(out=dtmp[:], in0=banks[ab][:, nfull:nfull + 128],
                                                                                 in1=cst[:, C_TRIP:C_TRIP + 128], op=ALU.add),
                             reads=[bbuf[ab], b_cst], writes=[b_dtmp])
                        P.op("dve", lambda e: e.tensor_reduce(out=dmin[qt % 2][:, 0:1], in_=dtmp[:], axis=AX.X, op=ALU.min),
                             reads=[b_dtmp], writes=[b_dmin[qt % 2]])
                        P.op("dve", lambda e, ab=ab, nfull=nfull, dsl=dsl: e.tensor_tensor(out=sc[:, dsl], in0=banks[ab][:, nfull:nfull + 128],
                                                                                          in1=cst[:, C_TRIM:C_TRIM + 128], op=ALU.add),
                             reads=[bbuf[ab], b_cst], writes=[b_sc])

                for i_ in range(len(items) + 1):
                    if i_ < len(items):
                        z_part(*items[i_])
                    if i_ >= 1:
                        d_part(*items[i_ - 1])
                    if i_ % 2 == 1:
                        yield

            def stageB2(qt):
                n_s = qt + 1
                n = 128 * n_s
                nchk = (n_s + 3) // 4
                sc, b_sc = scs[qt % 2], b_scs[qt % 2]
                if n > KTOP:
                    P.op("dve", lambda e, n=n: e.tensor_reduce(out=small[:, 0:1], in_=sc[:, 0:n], axis=AX.X, op=ALU.max),
                         reads=[b_sc], writes=[b_small])
                    P.op("dve", lambda e, n=n: e.tensor_reduce(out=small[:, 1:2], in_=sc[:, 0:n - 128], axis=AX.X, op=ALU.min),
                         reads=[b_sc], writes=[b_small])
                    P.op("dve", lambda e: e.tensor_tensor(out=small[:, 1:2], in0=small[:, 1:2], in1=dmin[qt % 2][:, 0:1], op=ALU.min),
                         reads=[b_small, b_dmin[qt % 2]], writes=[b_small])
                    P.op("dve", lambda e: e.tensor_tensor(out=small[:, 3:4], in0=small[:, 0:1], in1=small[:, 1:2], op=ALU.subtract),
                         reads=[b_small], writes=[b_small])
                    P.op("dve", lambda e: e.tensor_scalar(out=small[:, 3:4], in0=small[:, 3:4], scalar1=1.0001, scalar2=1e-6,
                                                          op0=ALU.mult, op1=ALU.add), reads=[b_small], writes=[b_small])
                    P.op("dve", lambda e: e.tensor_scalar(out=hw[:], in0=cst[:, C_POW:C_POW + NIT + 1], scalar1=small[:, 3:4], scalar2=None,
                                                          op0=ALU.mult), reads=[b_small, b_cst], writes=[b_hw])
                    P.op("dve", lambda e: e.tensor_tensor(out=small[:, 4:5], in0=small[:, 1:2], in1=hw[:, 0:1], op=ALU.add),
                         reads=[b_small, b_hw], writes=[b_small])
                    yield
                    for it in range(NIT):
                        P.op("dve", lambda e, n=n: e.tensor_scalar(out=junk[:, 0:n], in0=sc[:, 0:n], scalar1=small[:, 4:5], scalar2=None,
                                                                   op0=ALU.is_ge, op1=ALU.add, accum_out=small[:, 5:6]),
                             reads=[b_sc, b_small], writes=[b_junk, b_small])
                        P.op("dve", lambda e, it=it: e.tensor_scalar(out=small[:, 6:7], in0=small[:, 5:6], scalar1=float(KTOP) - 0.5,
                                                                     scalar2=hw[:, it:it + 1], op0=ALU.is_ge, op1=ALU.mult),
                             reads=[b_small, b_hw], writes=[b_small])
                        P.op("dve", lambda e, it=it: e.scalar_tensor_tensor(out=small[:, 4:5], in0=small[:, 6:7], scalar=hw[:, it + 1:it + 2],
                                                                            in1=small[:, 4:5], op0=ALU.subtract, op1=ALU.add),
                             reads=[b_small, b_hw], writes=[b_small])
                        yield
                    P.op("dve", lambda e: e.scalar_tensor_tensor(out=small[:, 7:8], in0=hw[:, NIT:NIT + 1], scalar=-2.0, in1=small[:, 4:5],
                                                                 op0=ALU.mult, op1=ALU.add),
                         reads=[b_small, b_hw], writes=[b_small])
                else:
                    P.op("dve", lambda e: e.memset(small[:, 7:8], -1.0e29), writes=[b_small])
                P.op("dve", lambda e, n=n: e.tensor_scalar(out=junk[:, 0:n], in0=sc[:, 0:n], scalar1=small[:, 7:8], scalar2=None, op0=ALU.is_ge),
                     reads=[b_sc, b_small], writes=[b_junk])
                yield

            def stageB2b(qt):
                n_s = qt + 1
                nchk = (n_s + 3) // 4
                ms = 0
                for c in range(nchk):
                    g = min(4, n_s - 4 * c)
                    tb = 7 if c % 2 == 0 else 2
                    tbank = banks[tb][:, 0:256].bitcast(BF16)
                    for j in range(g):
                        st = 4 * c + j
                        P.op("pe", lambda e, tbank=tbank, j=j, st=st: e.transpose(
                            out=tbank[:, j * 128:(j + 1) * 128], in_=junk[:, st * 128:(st + 1) * 128], identity=identb[:]),
                            reads=[b_junk, b_cb], writes=[bbuf[tb]])
                    P.op("act", lambda e, tbank=tbank, ms=ms, c=c, g=g: e.activation(
                        out=MT[ms][:, 4 * c:4 * c + g, :].rearrange("p a b -> p (a b)"), in_=tbank[:, 0:g * 128], func=AF.Identity,
                        scale=MASK_NEG, bias=negb[:, 0:1]),
                        reads=[bbuf[tb], b_cb], pw=[b_MT[ms]])
                    yield

            def stageC(qt):
                qc, qo = qt // 2, (qt % 2) * 128
                if qt % 2 == 0:
                    for hh_ in range(2):
                        P.op("sp", lambda e, qc=qc, hh_=hh_: e.dma_start(
                            out=qch[hh_ * 64:(hh_ + 1) * 64, :, hh_, :],
                            in_=QA_T[:, hh_ * 64:(hh_ + 1) * 64, qc * 256:(qc + 1) * 256].rearrange("c p t -> p c t")),
                            reads=[b_QA], writes=[b_qch], dma=True)
                n_s = qt + 1
                nchk = (n_s + 3) // 4
                ms = 0
                ot = qt % 2
                SK = 2
                items = [(h, c) for h in range(12) for c in range(nchk)]
                e0 = cnts["e"]
                cnts["e"] += len(items)
                o0 = cnts["o"]
                cnts["o"] += 12

                def qk_part(idx):
                    h, c = items[idx]
                    hp, hh = h // 2, h % 2
                    rows = slice(hh * 64, hh * 64 + 64)
                    g = min(4, n_s - 4 * c)
                    sbk = 3 + (e0 + idx) % 2
                    es_ = (e0 + idx) % 3
                    for j in range(g):
                        st = 4 * c + j
                        P.op("pe", lambda e, sbk=sbk, j=j, st=st, hh=hh, hp=hp: e.matmul(
                            banks[sbk][:, j * 128:(j + 1) * 128], lhsT=KA[:, hp, st * 128:(st + 1) * 128],
                            rhs=qch[:, hp, hh, qo:qo + 128], start=True, stop=False),
                            reads=[b_KAs, b_qch], writes=[bbuf[sbk]])
                        P.op("pe", lambda e, sbk=sbk, j=j, st=st: e.matmul(
                            banks[sbk][:, j * 128:(j + 1) * 128], lhsT=identb[:], rhs=MT[ms][:, st, :], start=False, stop=True),
                            reads=[b_cb, b_MT[ms]], writes=[bbuf[sbk]])
                    P.op("act", lambda e, sbk=sbk, es_=es_, g=g: e.activation(
                        out=esb[es_][:, 0:g * 128], in_=banks[sbk][:, 0:g * 128], func=AF.Exp, scale=0.125),
                        reads=[bbuf[sbk]], writes=[b_esb[es_]])

                def pv_part(idx):
                    h, c = items[idx]
                    hp, hh = h // 2, h % 2
                    g = min(4, n_s - 4 * c)
                    es_ = (e0 + idx) % 3
                    ob = 5 + (o0 + h) % 2
                    vofs = hp * 192 + hh * 64
                    for j in range(g):
                        st = 4 * c + j
                        P.op("pe", lambda e, ob=ob, es_=es_, j=j, st=st, vofs=vofs: e.matmul(
                            banks[ob][:, 0:128], lhsT=VA[:, st, vofs:vofs + 128], rhs=esb[es_][:, j * 128:(j + 1) * 128],
                            start=(st == 0), stop=(st == n_s - 1)),
                            reads=[b_VAs, b_esb[es_]], writes=[bbuf[ob]])
                    if c != nchk - 1:
                        return
                    num = slice(0, 64) if hh == 0 else slice(64, 128)
                    den = slice(64, 128) if hh == 0 else slice(0, 64)
                    rd = h % 2
                    P.op("act", lambda e, ob=ob, den=den, rd=rd: e.activation(out=rden[rd][den, :], in_=banks[ob][den, 0:128], func=AF.Ln),
                         reads=[bbuf[ob]], writes=[b_rden[rd]])
                    P.op("act", lambda e, den=den, rd=rd: e.activation(out=rden[rd][den, :], in_=rden[rd][den, :], func=AF.Exp, scale=-1.0),
                         reads=[b_rden[rd]], writes=[b_rden[rd]])
                    P.op("dve", lambda e, ob=ob, num=num, den=den, rd=rd, hp=hp: e.tensor_tensor(
                        out=otile[ot][num, hp, :], in0=banks[ob][num, 0:128], in1=rden[rd][den, :], op=ALU.mult),
                        reads=[bbuf[ob], b_rden[rd]], pw=[b_otile[ot]])

                for i_ in range(len(items) + SK):
                    if i_ < len(items):
                        qk_part(i_)
                    if i_ >= SK:
                        pv_part(i_ - SK)
                    yield
                P.op("sp", lambda e, ot=ot, qt=qt: e.dma_start(
                    out=OA_T[:, :, qt * 128:(qt + 1) * 128].rearrange("c p t -> p c t"), in_=otile[ot][:]),
                    reads=[b_otile[ot]], pw=[b_OA], dma=True)
                yield

            def interleave(*gens):
                live = [g for g in gens if g is not None]
                while live:
                    nxt = []
                    for g in live:
                        try:
                            next(g)
                            nxt.append(g)
                        except StopIteration:
                            pass
                    live = nxt

            interleave(stageB1(0))
            interleave(stageB2(0), stageB1(1) if NT > 1 else None)
            interleave(stageB2b(0))
            for qt in range(NT):
                interleave(stageC(qt),
                           stageB2(qt + 1) if qt + 1 < NT else None,
                           stageB1(qt + 2) if qt + 2 < NT else None)
                if qt + 1 < NT:
                    interleave(stageB2b(qt + 1))

        barrier(5)
        with ExitStack() as s5:
            acc = [sb("acc%d" % i, [128, S], F32, s5) for i in range(4)]
            b_acc = [Buf("acc%d" % i) for i in range(4)]
            QB = sb("QBg", [128, 2, 2, S], BF16, s5)
            KB = sb("KBg", [128, 2, S], BF16, s5)
            VB = sb("VBg", [128, NT, 2 * 192], BF16, s5)
            b_QBs, b_KBs, b_VBs = Buf("QBs"), Buf("KBs"), Buf("VBs")
            psb5 = [sb("p5_%d" % i, [128, 512], BF16, s5) for i in range(3)]
            b_p5 = [Buf("p5_%d" % i) for i in range(3)]
            obt = sb("obt", [128, 2, S], BF16, s5)
            b_obt = Buf("obt")
            rd5 = sb("rd5", [128, S], F32, s5)
            b_rd5 = Buf("rd5")
            mb4n = sb("mb4n", [128, 512], BF16, s5)
            P.op("dve", lambda e: e.tensor_scalar(out=mb4n[:], in0=mb4[:], scalar1=MASK_NEG, scalar2=-MASK_NEG, op0=ALU.mult, op1=ALU.add),
                 reads=[b_cb], writes=[b_cb])
            P.op("pool", lambda e: e.memset(QB[:].rearrange("p a b c -> p (a b c)"), 0.0), writes=[b_QBs])
            oc = 0
            SK5 = 2
            for g, dil in enumerate((1, 4, 16)):
                L = S // dil
                nsub = L // 128
                for c in range(2):
                    for hh_ in range(2):
                        P.op("sp", lambda e, c=c, g=g, hh_=hh_: e.dma_start(
                            out=QB[hh_ * 64:(hh_ + 1) * 64, c, hh_, :], in_=QB_T[2 * g + c][hh_ * 64:(hh_ + 1) * 64, :]),
                            reads=[b_QB], pw=[b_QBs], dma=True)
                    P.op("sp", lambda e, c=c, g=g: e.dma_start(out=KB[:, c, :], in_=KB_T[2 * g + c]), reads=[b_KB], pw=[b_KBs], dma=True)
                for r in range(dil):
                    for iu0 in range(0, nsub, 8):
                        iu1 = min(nsub, iu0 + 8)
                        base = r + dil * 128 * iu0
                        cnt_rows = (iu1 - iu0) * 128
                        srcv = VB_S[base:base + dil * (cnt_rows - 1) + 1:dil, g * 384:(g + 1) * 384]
                        P.op("sp", lambda e, r=r, iu0=iu0, iu1=iu1, srcv=srcv, nsub=nsub: e.dma_start(
                            out=VB[:, r * nsub + iu0:r * nsub + iu1, :], in_=srcv.rearrange("(t p) c -> p t c", p=128)),
                            reads=[b_VB], pw=[b_VBs], dma=True)
                items5 = []
                for hl in range(4):
                    for r in range(dil):
                        blocks = [(0, 0, "D")]
                        for iu in range(1, nsub):
                            blocks.append((iu - 1, iu, "P"))
                            blocks.append((iu, iu, "D"))
                        cur_ob = None
                        for b0 in range(0, len(blocks), 4):
                            chunk = blocks[b0:b0 + 4]
                            obs = []
                            for (st, iu, kind) in chunk:
                                if iu % 4 == 0 and kind == ("D" if iu == 0 else "P"):
                                    cur_ob = 4 + oc % 4
                                    oc += 1
                                obs.append(cur_ob)
                            items5.append((hl, r, chunk, obs))

                def tok5(r, iu, dil=dil):
                    b0 = r + dil * 128 * iu
                    return slice(b0, b0 + dil * 127 + 1, dil)

                def qk5(idx, g=g, nsub=nsub):
                    hl, r, chunk, obs = items5[idx]
                    hp, hh = hl // 2, hl % 2
                    gsz = len(chunk)
                    sbk = idx % 3
                    for j, (st, iu, kind) in enumerate(chunk):
                        ksl, qsl = tok5(r, st), tok5(r, iu)
                        P.op("pe", lambda e, sbk=sbk, j=j, hp=hp, hh=hh, ksl=ksl, qsl=qsl: e.matmul(
                            banks[sbk][:, j * 128:(j + 1) * 128], lhsT=KB[:, hp, ksl], rhs=QB[:, hp, hh, qsl],
                            start=True, stop=False), reads=[b_KBs, b_QBs], writes=[bbuf[sbk]])
                        P.op("pe", lambda e, sbk=sbk, j=j: e.matmul(
                            banks[sbk][:, j * 128:(j + 1) * 128], lhsT=identb[:], rhs=mb4n[:, j * 128:(j + 1) * 128],
                            start=False, stop=True), reads=[b_cb], writes=[bbuf[sbk]])
                    P.op("act", lambda e, sbk=sbk, gsz=gsz: e.activation(
                        out=psb5[sbk][:, 0:gsz * 128], in_=banks[sbk][:, 0:gsz * 128], func=AF.Exp, scale=0.125),
                        reads=[bbuf[sbk]], writes=[b_p5[sbk]])

                def pv5(idx, g=g, nsub=nsub, dil=dil):
                    hl, r, chunk, obs = items5[idx]
                    hp, hh = hl // 2, hl % 2
                    vofs = hp * 192 + hh * 64
                    es_ = idx % 3
                    for j, (st, iu, kind) in enumerate(chunk):
                        ob = obs[j]
                        jo = iu % 4
                        P.op("pe", lambda e, ob=ob, jo=jo, es_=es_, j=j, st=st, r=r, vofs=vofs, kind=kind, iu=iu: e.matmul(
                            banks[ob][:, jo * 128:(jo + 1) * 128], lhsT=VB[:, r * nsub + st, vofs:vofs + 128],
                            rhs=psb5[es_][:, j * 128:(j + 1) * 128],
                            start=(kind == "P" or iu == 0), stop=(kind == "D")),
                            reads=[b_VBs, b_p5[es_]], writes=[bbuf[ob]])
                        if kind == "D" and (iu % 4 == 3 or iu == nsub - 1):
                            iu_lo = iu - (iu % 4)
                            nt_ = iu - iu_lo + 1
                            b00 = r + dil * 128 * iu_lo
                            dst = acc[hl][:, b00:b00 + dil * (nt_ * 128 - 1) + 1:dil]
                            if g == 0:
                                P.op("dve", lambda e, ob=ob, nt_=nt_, dst=dst: e.tensor_copy(out=dst, in_=banks[ob][:, 0:nt_ * 128]),
                                     reads=[bbuf[ob]], writes=[b_acc[hl]])
                            else:
                                P.op("dve", lambda e, ob=ob, nt_=nt_, dst=dst: e.tensor_tensor(
                                    out=dst, in0=banks[ob][:, 0:nt_ * 128], in1=dst, op=ALU.add),
                                    reads=[bbuf[ob], b_acc[hl]], writes=[b_acc[hl]])

                for i_ in range(len(items5) + SK5):
                    if i_ < len(items5):
                        qk5(i_)
                    if i_ >= SK5:
                        pv5(i_ - SK5)
            for hl in range(4):
                hp, hh = hl // 2, hl % 2
                num = slice(0, 64) if hh == 0 else slice(64, 128)
                den = slice(64, 128) if hh == 0 else slice(0, 64)
                P.op("dve", lambda e, hl=hl, den=den, num=num: e.reciprocal(out=rd5[num, :], in_=acc[hl][den, :]),
                     reads=[b_acc[hl]], writes=[b_rd5])
                P.op("dve", lambda e, hl=hl, num=num, den=den, hp=hp: e.tensor_tensor(
                    out=obt[num, hp, :], in0=acc[hl][num, :], in1=rd5[num, :], op=ALU.mult),
                    reads=[b_acc[hl], b_rd5], writes=[b_obt])
            for c in range(2):
                P.op("sp", lambda e, c=c: e.dma_start(out=OB_T[c], in_=obt[:, c, :]), reads=[b_obt], pw=[b_OB], dma=True)

        def load_w(dst2d, src2d, ncols, stg, b_stg, b_dst, cw=2048, engs=("pool", "dve", "act")):
            for i, c0 in enumerate(range(0, ncols, cw)):
                c1 = min(ncols, c0 + cw)
                sl = i % len(stg)
                P.op("sp", lambda e, c0=c0, c1=c1, sl=sl: e.dma_start(out=stg[sl][:, 0:c1 - c0], in_=src2d[:, c0:c1]),
                     writes=[b_stg[sl]], dma=True)
                ceng = engs[i % len(engs)]
                if ceng == "act":
                    P.op("act", lambda e, c0=c0, c1=c1, sl=sl: e.activation(out=dst2d[:, c0:c1], in_=stg[sl][:, 0:c1 - c0], func=AF.Copy),
                         reads=[b_stg[sl]], pw=[b_dst])
                else:
                    P.op(ceng, lambda e, c0=c0, c1=c1, sl=sl: e.tensor_copy(out=dst2d[:, c0:c1], in_=stg[sl][:, 0:c1 - c0]),
                         reads=[b_stg[sl]], pw=[b_dst])

        def layer_norm(src, b_src, dst, b_dst, gbc, bbc, b_ln, stats, mv, b_st):
            for hf in range(2):
                P.op("dve", lambda e, hf=hf: e.bn_stats(out=stats[:, hf * 6:(hf + 1) * 6], in_=src[:, hf * 512:(hf + 1) * 512]),
                     reads=[b_src], writes=[b_st])
            P.op("dve", lambda e: e.bn_aggr(out=mv[:, 0:2], in_=stats[:, 0:12]), reads=[b_st], writes=[b_st])
            P.op("dve", lambda e: e.tensor_scalar(out=mv[:, 2:3], in0=mv[:, 1:2], scalar1=LN_EPS, scalar2=None, op0=ALU.add),
                 reads=[b_st], writes=[b_st])
            P.op("act", lambda e: e.activation(out=mv[:, 2:3], in_=mv[:, 2:3], func=AF.Sqrt), reads=[b_st], writes=[b_st])
            P.op("dve", lambda e: e.reciprocal(out=mv[:, 3:4], in_=mv[:, 2:3]), reads=[b_st], writes=[b_st])
            P.op("dve", lambda e: e.tensor_scalar(out=dst[:], in0=src[:], scalar1=mv[:, 0:1], scalar2=mv[:, 3:4],
                                                  op0=ALU.subtract, op1=ALU.mult), reads=[b_src, b_st], writes=[b_dst])
            P.op("pool", lambda e: e.tensor_tensor(out=dst[:], in0=dst[:], in1=gbc[:], op=ALU.mult), reads=[b_dst, b_ln], writes=[b_dst])
            P.op("pool", lambda e: e.tensor_tensor(out=dst[:], in0=dst[:], in1=bbc[:], op=ALU.add), reads=[b_dst, b_ln], writes=[b_dst])

        barrier(6)
        with ExitStack() as s6:
            stg = [sb("stg%d" % i, [128, 2048], F32, s6) for i in range(3)]
            b_stg = [Buf("stg%d" % i) for i in range(3)]
            WA = sb("WA", [128, 6 * 1024], BF16, s6)
            WB = sb("WB", [128, 2 * 1024], BF16, s6)
            WO = sb("WO", [128, 8 * 1024], BF16, s6)
            b_WA, b_WB, b_WO = Buf("WA"), Buf("WB"), Buf("WO")
            load_w(WA, wa_d, 6 * 1024, stg, b_stg, b_WA)
            load_w(WB, wb_d, 2 * 1024, stg, b_stg, b_WB)
            load_w(WO, wo_d, 8 * 1024, stg, b_stg, b_WO)
            gbc = sb("gbc1", [128, D], F32, s6)
            bbc = sb("bbc1", [128, D], F32, s6)
            b_ln1 = Buf("ln1")
            P.op("sp", lambda e, gbc=gbc: e.dma_start(out=gbc[:], in_=ln_d[0:1, :].partition_broadcast(128)), writes=[b_ln1], dma=True)
            P.op("sp", lambda e, bbc=bbc: e.dma_start(out=bbc[:], in_=ln_d[1:2, :].partition_broadcast(128)), writes=[b_ln1], dma=True)
            oac = [sb("oac%d" % i, [128, 6, 512], BF16, s6) for i in range(2)]
            obc = [sb("obc%d" % i, [128, 2, 512], BF16, s6) for i in range(2)]
            gch = [sb("gch%d" % i, [128, 16, 512], BF16, s6) for i in range(2)]
            b_oac = [Buf("oac%d" % i) for i in range(2)]
            b_obc = [Buf("obc%d" % i) for i in range(2)]
            b_gch = [Buf("gch%d" % i) for i in range(2)]
            mg = [sb("mg%d" % i, [128, 8, 512], BF16, s6) for i in range(2)]
            b_mg = [Buf("mg%d" % i) for i in range(2)]
            mt1 = [sb("mt1_%d" % i, [128, 512], F32, s6) for i in range(2)]
            b_mt1 = [Buf("mt1_%d" % i) for i in range(2)]
            xt6 = [sb("xt6_%d" % i, [128, D], F32, s6) for i in range(2)]
            b_xt6 = [Buf("xt6_%d" % i) for i in range(2)]
            hp6 = [sb("hp6_%d" % i, [128, D], F32, s6) for i in range(2)]
            b_hp6 = [Buf("hp6_%d" % i) for i in range(2)]
            h1s = [sb("h1s_%d" % i, [128, D], F32, s6) for i in range(2)]
            b_h1s = [Buf("h1s_%d" % i) for i in range(2)]
            h1t = [sb("h1t_%d" % i, [128, 8, 128], BF16, s6) for i in range(2)]
            b_h1t = [Buf("h1t_%d" % i) for i in range(2)]
            stats = [sb("stats6_%d" % i, [128, 12], F32, s6) for i in range(2)]
            mv = [sb("mv6_%d" % i, [128, 4], F32, s6) for i in range(2)]
            b_st = [Buf("st6_%d" % i) for i in range(2)]
            mc = 0
            tcnt = 0
            pend6 = None

            def p6a_back(tt, ts_):
                for hf in range(2):
                    bk = 6 + hf
                    for j in range(4):
                        kc = hf * 4 + j
                        P.op("pe", lambda e, bk=bk, j=j, kc=kc: e.transpose(
                            out=banks[bk][:, j * 128:(j + 1) * 128], in_=h1s[ts_][:, kc * 128:(kc + 1) * 128], identity=ident_f),
                            reads=[b_h1s[ts_], b_cst], writes=[bbuf[bk]])
                    P.op("act", lambda e, bk=bk, hf=hf: e.activation(
                        out=h1t[ts_][:, hf * 4:hf * 4 + 4, :], in_=banks[bk][:].rearrange("p (a b) -> p a b", a=4), func=AF.Copy),
                        reads=[bbuf[bk]], writes=[b_h1t[ts_]])
                P.op("sp", lambda e: e.dma_start(
                    out=H1T[:, :, tt * 128:(tt + 1) * 128].rearrange("c p t -> p c t"), in_=h1t[ts_][:]),
                    reads=[b_h1t[ts_]], pw=[b_H1T], dma=True)

            for n in range(NCH):
                cs = n % 2
                tsl = slice(n * 512, (n + 1) * 512)
                P.op("sp", lambda e, cs=cs, tsl=tsl: e.dma_start(out=oac[cs][:], in_=OA_T[:, :, tsl].rearrange("c p t -> p c t")),
                     reads=[b_OA], writes=[b_oac[cs]], dma=True)
                P.op("sp", lambda e, cs=cs, tsl=tsl: e.dma_start(out=obc[cs][:], in_=OB_T[:, :, tsl].rearrange("c p t -> p c t")),
                     reads=[b_OB], writes=[b_obc[cs]], dma=True)
                P.op("sp", lambda e, cs=cs, tsl=tsl: e.dma_start(out=gch[cs][:], in_=G_T[:, :, tsl].rearrange("c p t -> p c t")),
                     reads=[b_G], writes=[b_gch[cs]], dma=True)
                for oc_ in range(8):
                    ba = (mc % 2) * 2
                    bb = ba + 1
                    m1 = mc % 2
                    mc += 1
                    for kc in range(6):
                        P.op("pe", lambda e, ba=ba, kc=kc, oc_=oc_, cs=cs: e.matmul(
                            banks[ba][:], lhsT=WA[:, kc * 1024 + oc_ * 128:kc * 1024 + oc_ * 128 + 128], rhs=oac[cs][:, kc, :],
                            start=(kc == 0), stop=(kc == 5)), reads=[b_WA, b_oac[cs]], writes=[bbuf[ba]])
                    for kc in range(2):
                        P.op("pe", lambda e, bb=bb, kc=kc, oc_=oc_, cs=cs: e.matmul(
                            banks[bb][:], lhsT=WB[:, kc * 1024 + oc_ * 128:kc * 1024 + oc_ * 128 + 128], rhs=obc[cs][:, kc, :],
                            start=(kc == 0), stop=(kc == 1)), reads=[b_WB, b_obc[cs]], writes=[bbuf[bb]])
                    P.op("dve", lambda e, ba=ba, m1=m1, oc_=oc_, cs=cs: e.tensor_tensor(
                        out=mt1[m1][:], in0=banks[ba][:], in1=gch[cs][:, oc_, :], op=ALU.mult),
                        reads=[bbuf[ba], b_gch[cs]], writes=[b_mt1[m1]])
                    P.op("dve", lambda e, bb=bb, m1=m1, oc_=oc_, cs=cs: e.tensor_tensor(
                        out=mg[cs][:, oc_, :], in0=banks[bb][:], in1=gch[cs][:, 8 + oc_, :], op=ALU.mult),
                        reads=[bbuf[bb], b_gch[cs]], writes=[b_mg[cs]])
                    P.op("pool", lambda e, m1=m1, oc_=oc_, cs=cs: e.tensor_tensor(
                        out=mg[cs][:, oc_, :], in0=mg[cs][:, oc_, :], in1=mt1[m1][:], op=ALU.add),
                        reads=[b_mt1[m1], b_mg[cs]], writes=[b_mg[cs]])
                for t4 in range(4):
                    tt = n * 4 + t4
                    ts_ = tcnt % 2
                    tcnt += 1
                    P.op("sp", lambda e, ts_=ts_, tt=tt: e.dma_start(out=xt6[ts_][:], in_=x_d[tt * 128:(tt + 1) * 128, :]),
                         writes=[b_xt6[ts_]], dma=True)
                    for hf in range(2):
                        bk = (4 if ts_ == 0 else 2) + hf
                        for kc in range(8):
                            P.op("pe", lambda e, bk=bk, kc=kc, t4=t4, hf=hf, cs=cs: e.matmul(
                                banks[bk][:], lhsT=mg[cs][:, kc, t4 * 128:(t4 + 1) * 128],
                                rhs=WO[:, kc * 1024 + hf * 512:kc * 1024 + hf * 512 + 512], start=(kc == 0), stop=(kc == 7)),
                                reads=[b_mg[cs], b_WO], writes=[bbuf[bk]])
                        P.op("dve", lambda e, bk=bk, hf=hf, ts_=ts_: e.scalar_tensor_tensor(
                            out=hp6[ts_][:, hf * 512:(hf + 1) * 512], in0=xt6[ts_][:, hf * 512:(hf + 1) * 512], scalar=ALPHA,
                            in1=banks[bk][:], op0=ALU.mult, op1=ALU.add),
                            reads=[b_xt6[ts_], bbuf[bk]], writes=[b_hp6[ts_]])
                    if pend6 is not None:
                        p6a_back(*pend6)
                    layer_norm(hp6[ts_], b_hp6[ts_], h1s[ts_], b_h1s[ts_], gbc, bbc, b_ln1, stats[ts_], mv[ts_], b_st[ts_])
                    P.op("sp", lambda e, ts_=ts_, tt=tt: e.dma_start(out=H1[tt * 128:(tt + 1) * 128, :], in_=h1s[ts_][:]),
                         reads=[b_h1s[ts_]], pw=[b_H1], dma=True)
                    pend6 = (tt, ts_)
            if pend6 is not None:
                p6a_back(*pend6)

        barrier(7)
        with ExitStack() as s7:
            stg = [sb("stgb%d" % i, [128, 1024], F32, s7) for i in range(2)]
            b_stg = [Buf("stgb%d" % i) for i in range(2)]
            WG = sb("WG", [128, 8 * FFN], BF16, s7)
            WU = sb("WU", [128, 8 * FFN], BF16, s7)
            WD = sb("WD", [128, NHC * 1024], BF16, s7)
            b_WG, b_WU, b_WD = Buf("WG"), Buf("WU"), Buf("WD")
            load_w(WG, wg_d, 8 * FFN, stg, b_stg, b_WG, 1024)
            load_w(WU, wu_d, 8 * FFN, stg, b_stg, b_WU, 1024)
            load_w(WD, wd_d, NHC * 1024, stg, b_stg, b_WD, 1024, engs=("pool",))
            gbc = sb("gbc2", [128, D], F32, s7)
            bbc = sb("bbc2", [128, D], F32, s7)
            b_ln2 = Buf("ln2")
            P.op("sp", lambda e, gbc=gbc: e.dma_start(out=gbc[:], in_=ln_d[2:3, :].partition_broadcast(128)), writes=[b_ln2], dma=True)
            P.op("sp", lambda e, bbc=bbc: e.dma_start(out=bbc[:], in_=ln_d[3:4, :].partition_broadcast(128)), writes=[b_ln2], dma=True)
            hch = [sb("hch%d" % i, [128, 8, 512], BF16, s7) for i in range(1)]
            b_hch = [Buf("hch%d" % i) for i in range(1)]
            hid = [sb("hid%d" % i, [128, NHC, 512], BF16, s7) for i in range(1)]
            b_hid = [Buf("hid%d" % i) for i in range(1)]
            sg = [sb("sg%d" % i, [128, 512], F32, s7) for i in range(2)]
            b_sg = [Buf("sg%d" % i) for i in range(2)]
            h1r = [sb("h1r%d" % i, [128, D], F32, s7) for i in range(2)]
            b_h1r = [Buf("h1r%d" % i) for i in range(2)]
            hp7 = [sb("hp7_%d" % i, [128, D], F32, s7) for i in range(2)]
            b_hp7 = [Buf("hp7_%d" % i) for i in range(2)]
            o7, b_o7 = hp7, b_hp7
            stats = [sb("stats7_%d" % i, [128, 12], F32, s7) for i in range(2)]
            mv = [sb("mv7_%d" % i, [128, 4], F32, s7) for i in range(2)]
            b_st = [Buf("st7_%d" % i) for i in range(2)]
            gc_ = 0
            tcnt = 0
            for n in range(NCH):
                cs = 0
                tsl = slice(n * 512, (n + 1) * 512)
                P.op("sp", lambda e, cs=cs, tsl=tsl: e.dma_start(out=hch[cs][:], in_=H1T[:, :, tsl].rearrange("c p t -> p c t")),
                     reads=[b_H1T], writes=[b_hch[cs]], dma=True)
                for hc in range(NHC):
                    bg_ = (gc_ % 2) * 2
                    bu_ = bg_ + 1
                    s1 = gc_ % 2
                    gc_ += 1
                    for kc in range(8):
                        P.op("pe", lambda e, bg_=bg_, kc=kc, hc=hc, cs=cs: e.matmul(
                            banks[bg_][:], lhsT=WG[:, kc * FFN + hc * 128:kc * FFN + hc * 128 + 128], rhs=hch[cs][:, kc, :],
                            start=(kc == 0), stop=(kc == 7)), reads=[b_WG, b_hch[cs]], writes=[bbuf[bg_]])
                    for kc in range(8):
                        P.op("pe", lambda e, bu_=bu_, kc=kc, hc=hc, cs=cs: e.matmul(
                            banks[bu_][:], lhsT=WU[:, kc * FFN + hc * 128:kc * FFN + hc * 128 + 128], rhs=hch[cs][:, kc, :],
                            start=(kc == 0), stop=(kc == 7)), reads=[b_WU, b_hch[cs]], writes=[bbuf[bu_]])
                    P.op("act", lambda e, bg_=bg_, s1=s1: e.activation(out=sg[s1][:], in_=banks[bg_][:], func=AF.Silu),
                         reads=[bbuf[bg_]], writes=[b_sg[s1]])
                    P.op("dve", lambda e, bu_=bu_, s1=s1, hc=hc, cs=cs: e.tensor_tensor(
                        out=hid[cs][:, hc, :], in0=banks[bu_][:], in1=sg[s1][:], op=ALU.mult),
                        reads=[bbuf[bu_], b_sg[s1]], writes=[b_hid[cs]])
                for t4 in range(4):
                    tt = n * 4 + t4
                    ts_ = tcnt % 2
                    tcnt += 1
                    P.op("sp", lambda e, ts_=ts_, tt=tt: e.dma_start(out=h1r[ts_][:], in_=H1[tt * 128:(tt + 1) * 128, :]),
                         reads=[b_H1], writes=[b_h1r[ts_]], dma=True)
                    for hf in range(2):
                        bk = 4 + (tcnt % 2) * 2 + hf
                        for hc in range(NHC):
                            P.op("pe", lambda e, bk=bk, hc=hc, t4=t4, hf=hf, cs=cs: e.matmul(
                                banks[bk][:], lhsT=hid[cs][:, hc, t4 * 128:(t4 + 1) * 128],
                                rhs=WD[:, hc * 1024 + hf * 512:hc * 1024 + hf * 512 + 512], start=(hc == 0), stop=(hc == NHC - 1)),
                                reads=[b_hid[cs], b_WD], writes=[bbuf[bk]])
                        P.op("dve", lambda e, bk=bk, hf=hf, ts_=ts_: e.scalar_tensor_tensor(
                            out=hp7[ts_][:, hf * 512:(hf + 1) * 512], in0=h1r[ts_][:, hf * 512:(hf + 1) * 512], scalar=ALPHA,
                            in1=banks[bk][:], op0=ALU.mult, op1=ALU.add),
                            reads=[b_h1r[ts_], bbuf[bk]], writes=[b_hp7[ts_]])
                    layer_norm(hp7[ts_], b_hp7[ts_], o7[ts_], b_o7[ts_], gbc, bbc, b_ln2, stats[ts_], mv[ts_], b_st[ts_])
                    P.op("sp", lambda e, ts_=ts_, tt=tt: e.dma_start(out=out_d[tt * 128:(tt + 1) * 128, :], in_=o7[ts_][:]),
                         reads=[b_o7[ts_]], writes=[Buf("outd")], dma=True, final=True)

        with ExitStack() as ee:
            block = ee.enter_context(nc.Block())
            P.emit(nc, ee, block)
    return nc


def _consts():
    c = np.zeros((128, NCONST), np.float32)
    idx = np.arange(128)
    c[:, C_ID:C_ID + 128] = np.eye(128, dtype=np.float32)
    perm = np.zeros((128, 128), np.float32)
    for m in range(128):
        mm = m % 64
        if mm < 8:
            perm[m + 8, m] = 1.0
        elif mm < 16:
            perm[m - 8, m] = 1.0
    c[:, C_PERM:C_PERM + 128] = perm
    qq, ss = idx[:, None], idx[None, :]
    c[:, C_TRIM:C_TRIM + 128] = np.where(ss <= qq, 0.0, -BIG)
    c[:, C_TRIP:C_TRIP + 128] = np.where(ss <= qq, 0.0, BIG)
    s_, u_ = idx[:, None], idx[None, :]
    c[:, C_MD:C_MD + 128] = (s_ <= u_).astype(np.float32)
    c[:, C_MP:C_MP + 128] = (s_ >= u_).astype(np.float32)
    inv_freq = (ROPE_THETA ** (-np.arange(0, 16, 2, dtype=np.float32) / 16.0)).astype(np.float32)
    for m in range(128):
        mm = m % 64
        if mm < 8:
            c[m, C_ROPE + 0] = -inv_freq[mm] / (2 * np.pi)
            c[m, C_ROPE + 1] = inv_freq[mm] / (2 * np.pi)
        elif mm < 16:
            c[m, C_ROPE + 0] = inv_freq[mm - 8] / (2 * np.pi)
            c[m, C_ROPE + 1] = inv_freq[mm - 8] / (2 * np.pi)
    c[:, C_ROPE + 2] = 0.25
    for i in range(NIT + 1):
        c[:, C_POW + i] = 2.0 ** (-(i + 1))
    return c


def _prep_weights(w_in, b_gate, w_branch_a, w_branch_b, w_out, ln1_g, ln1_b, w_ffn_gate, w_ffn_up, w_ffn_down, ln2_g, ln2_b):
    w_in = np.asarray(w_in, np.float32)[0]
    cols = []
    for base in (0, 768, 2304, 3072):
        for c in range(6):
            cols.append(np.arange(base + c * 128, base + (c + 1) * 128))
    for c in range(4):
        cols.append(np.arange(4608 + c * 128, 4608 + (c + 1) * 128))
    ki = np.arange(5120, 5184)
    cols.append(np.concatenate([ki, ki]))
    for c in range(16):
        cols.append(np.arange(5192 + c * 128, 5192 + (c + 1) * 128))
    w_fm = np.stack([w_in[:, cc].reshape(8, 128, 128).transpose(1, 0, 2).reshape(128, 1024) for cc in cols], 0)
    tmc = np.concatenate([np.arange(1536, 1920), np.arange(1920, 2304), np.arange(5184, 5192),
                          np.arange(3840, 4224), np.arange(4224, 4608)])
    w_tm = w_in[:, tmc].reshape(8, 128, 1544).transpose(1, 0, 2).reshape(128, 8 * 1544)

    def kmaj(w, nk):
        w = np.asarray(w, np.float32)[0]
        return np.ascontiguousarray(w.reshape(nk, 128, w.shape[1]).transpose(1, 0, 2).reshape(128, nk * w.shape[1]))

    d = {
        "w_fm": np.ascontiguousarray(w_fm),
        "w_tm": np.ascontiguousarray(w_tm),
        "bg": np.ascontiguousarray(np.asarray(b_gate, np.float32)[0].reshape(16, 128).T),
        "wa": kmaj(w_branch_a, 6),
        "wb": kmaj(w_branch_b, 2),
        "wo": kmaj(w_out, 8),
        "wg": kmaj(w_ffn_gate, 8),
        "wu": kmaj(w_ffn_up, 8),
        "wd": kmaj(w_ffn_down, NHC),
        "ln": np.ascontiguousarray(np.stack([np.asarray(a, np.float32)[0] for a in (ln1_g, ln1_b, ln2_g, ln2_b)], 0)),
        "cst": _consts(),
    }
    return d


_NC_CACHE = {}


def run(x, positions, weights, S, ktop):
    B = x.shape[0]
    key = (S, ktop)
    if key not in _NC_CACHE:
        _NC_CACHE[key] = build(S, ktop)
    nc = _NC_CACHE[key]
    wd = _prep_weights(**weights)
    in_maps = []
    for b in range(B):
        m = dict(wd)
        m["x"] = np.ascontiguousarray(np.asarray(x[b], np.float32))
        m["pos"] = np.ascontiguousarray(np.asarray(positions[b], np.int32).reshape(1, S))
        in_maps.append(m)
    res = run_bass_kernel_spmd(nc, in_maps, core_ids=list(range(B)))
    return np.stack([np.asarray(r["out"], np.float32) for r in res.results], 0)


def kernel(x, positions, w_in, b_gate, w_branch_a, w_branch_b, w_out, ln1_g, ln1_b,
           w_ffn_gate, w_ffn_up, w_ffn_down, ln2_g, ln2_b):
    x = np.asarray(x)
    S = x.shape[1]
    weights = dict(w_in=w_in, b_gate=b_gate, w_branch_a=w_branch_a, w_branch_b=w_branch_b, w_out=w_out,
                   ln1_g=ln1_g, ln1_b=ln1_b, w_ffn_gate=w_ffn_gate, w_ffn_up=w_ffn_up, w_ffn_down=w_ffn_down,
                   ln2_g=ln2_g, ln2_b=ln2_b)
    return run(x, np.asarray(positions), weights, S, min(256, S // 4))
```

```python
import math
from contextlib import ExitStack

import numpy as np
import concourse.bass as bass
import concourse.mybir as mybir
from concourse.bass_utils import run_bass_kernel_spmd

F32 = mybir.dt.float32
BF16 = mybir.dt.bfloat16
I32 = mybir.dt.int32
AF = mybir.ActivationFunctionType
ALU = mybir.AluOpType
AX = mybir.AxisListType

D = 1024
HD = 64
NQA = 768
FFN = 2816
NHC = FFN // 128
N_IN = 7240
ROPE_THETA = 500000.0
ALPHA = 2.0 ** 0.25
LN_EPS = 1e-5
NIT = 14
BIG = 1.0e30
MASK_NEG = 30000.0
SAME_ENGINE_SYNC = True
import os
KSTOP = int(os.environ.get('KSTOP', '99'))
KMAX = int(os.environ.get('KMAX', '100000000'))

C_ID, C_PERM, C_TRIM, C_TRIP, C_MD, C_MP, C_ROPE, C_POW = 0, 128, 256, 384, 512, 640, 768, 772
NCONST = C_POW + NIT + 1


class Buf:
    __slots__ = ("name", "W", "Wf", "R", "Rprev", "excl", "last")

    def __init__(self, name, excl=False):
        self.name = name
        self.excl = excl
        self.last = {}
        self.W = []
        self.Wf = []
        self.R = []
        self.Rprev = []


class Op:
    __slots__ = ("eng", "fn", "deps", "dma", "ms", "msidx", "dsem", "dval", "prev_dval", "seq", "cdeps")

    def __init__(self, eng, fn, dma):
        self.eng = eng
        self.fn = fn
        self.deps = []
        self.dma = dma
        self.ms = False
        self.msidx = 0
        self.dsem = None
        self.dval = 0
        self.prev_dval = 0


class Prog:
    ENGS = ("pe", "act", "dve", "pool", "sp")

    def __init__(self, n_dma_sems=24):
        self.ops = {e: [] for e in self.ENGS}
        self.n_dma_sems = n_dma_sems
        self.n_dma = 0
        self.final_dmas = []
        self.phase = Buf("phase")
        self.stopped = False

    def op(self, eng, fn, reads=(), writes=(), pw=(), dma=False, final=False, _barrier=False):
        if self.stopped:
            return None
        self.nops = getattr(self, "nops", 0) + 1
        if self.nops > KMAX:
            self.stopped = True
            return None
        o = Op(eng, fn, dma)
        deps = []
        for b in list(reads) + list(writes) + list(pw):
            if b.excl:
                deps.extend(op_ for e2, op_ in b.last.items() if e2 != eng)
                b.last[eng] = o
        reads = [b for b in reads if not b.excl]
        writes = [b for b in writes if not b.excl]
        pw = [b for b in pw if not b.excl]
        wset = set(id(b) for b in writes) | set(id(b) for b in pw)
        rlist = [b for b in reads if id(b) not in wset]
        if not _barrier:
            rlist = rlist + [self.phase]
        for b in rlist:
            deps.extend(b.W)
        for b in list(writes) + list(pw):
            if b.R:
                b.Rprev = b.R
                b.R = []
                oldW = b.W
                b.W = []
                b.Wf = []
                if any(id(b) == id(x) for x in reads):
                    deps.extend(oldW)
            deps.extend(b.Rprev)
            if any(id(b) == id(x) for x in writes):
                deps.extend(b.W)
            else:
                deps.extend(b.Wf)
        seen = set()
        for d in deps:
            if id(d) in seen or d is o:
                continue
            seen.add(id(d))
            if (not d.dma) and (not dma) and d.eng == eng and (eng == "pe" or not SAME_ENGINE_SYNC):
                continue
            o.deps.append(d)
        for b in rlist:
            b.R.append(o)
        for b in list(writes) + list(pw):
            b.W.append(o)
        for b in writes:
            b.Wf.append(o)
        if dma:
            k = self.n_dma % self.n_dma_sems
            o.dsem = k
            o.dval = 16 * (self.n_dma // self.n_dma_sems + 1)
            o.prev_dval = o.dval - 16
            self.n_dma += 1
            if final:
                self.final_dmas.append(o)
        o.seq = len(self.ops[eng])
        self.ops[eng].append(o)
        return o

    def barrier(self, fn):
        return self.op("pool", fn, writes=[self.phase], _barrier=True)

    def emit(self, nc, es, block):
        for e in self.ENGS:
            for o in self.ops[e]:
                best = {}
                for d in o.deps:
                    if not d.dma:
                        if d.eng not in best or best[d.eng].seq < d.seq:
                            best[d.eng] = d
                for d in best.values():
                    d.ms = True
                o.cdeps = list(best.values()) + [d for d in o.deps if d.dma]
        for e in self.ENGS:
            c = 0
            for o in self.ops[e]:
                if o.ms and not o.dma:
                    c += 1
                    o.msidx = c
        esem = {e: es.enter_context(nc.semaphore("sem_" + e)) for e in self.ENGS}
        dsems = [es.enter_context(nc.semaphore("dsem%d" % i)) for i in range(self.n_dma_sems)]
        prog = self

        def run(eng_name):
            def body(e):
                waited = {}

                def wait(key, sem, val):
                    if val <= 0:
                        return
                    if waited.get(key, 0) >= val:
                        return
                    waited[key] = val
                    e.wait_ge(sem, val)

                for o in prog.ops[eng_name]:
                    need = {}
                    for d in o.cdeps:
                        if d.dma:
                            k_, s_, v_ = ("d", d.dsem), dsems[d.dsem], d.dval
                        else:
                            k_, s_, v_ = ("e", d.eng), esem[d.eng], d.msidx
                        if k_ not in need or need[k_][1] < v_:
                            need[k_] = (s_, v_)
                    if o.dma:
                        k_ = ("d", o.dsem)
                        if k_ not in need or need[k_][1] < o.prev_dval:
                            need[k_] = (dsems[o.dsem], o.prev_dval)
                    for k_, (s_, v_) in need.items():
                        wait(k_, s_, v_)
                    inst = o.fn(e)
                    if o.dma:
                        inst.then_inc(dsems[o.dsem], 16)
                    elif o.ms:
                        inst.then_inc(esem[eng_name], 1)
                if eng_name == "sp":
                    for o in prog.final_dmas:
                        wait(("d", o.dsem), dsems[o.dsem], o.dval)
            return body

        block.tensor(run("pe"))
        block.scalar(run("act"))
        block.vector(run("dve"))
        block.gpsimd(run("pool"))
        block.sync(run("sp"))


def build(S, KTOP):
    NT = S // 128
    NCH = S // 512
    nc = bass.Bass("TRN2", target_bir_lowering=False)
    P = Prog()

    def din(name, shape, dt=F32):
        return nc.dram_tensor(name, list(shape), dt, kind="ExternalInput").ap()

    def dscr(name, shape, dt):
        return nc.dram_tensor(name, list(shape), dt, kind="Internal").ap()

    x_d = din("x", [S, D])
    pos_d = din("pos", [1, S], I32)
    wfm_d = din("w_fm", [45, 128, 1024])
    wtm_d = din("w_tm", [128, 8 * 1544])
    bg_d = din("bg", [128, 16])
    wa_d = din("wa", [128, 6 * 1024])
    wb_d = din("wb", [128, 2 * 1024])
    wo_d = din("wo", [128, 8 * 1024])
    wg_d = din("wg", [128, 8 * FFN])
    wu_d = din("wu", [128, 8 * FFN])
    wd_d = din("wd", [128, NHC * 1024])
    ln_d = din("ln", [4, D])
    cst_d = din("cst", [128, NCONST])
    out_d = nc.dram_tensor("out", [S, D], F32, kind="ExternalOutput").ap()

    QA_T = dscr("s_qat", [6, 128, S], BF16)
    KA_T = dscr("s_kat", [6, 128, S], BF16)
    QB_T = dscr("s_qbt", [6, 128, S], BF16)
    KB_T = dscr("s_kbt", [6, 128, S], BF16)
    QI_T = dscr("s_qit", [4, 128, S], BF16)
    KI_T = dscr("s_kit", [128, S], BF16)
    G_T = dscr("s_gt", [16, 128, S], BF16)
    VA_S = dscr("s_va", [S, 6 * 192], BF16)
    VB_S = dscr("s_vb", [S, 6 * 192], BF16)
    OA_T = dscr("s_oat", [6, 128, S], BF16)
    OB_T = dscr("s_obt", [2, 128, S], BF16)
    H1 = dscr("s_h1", [S, D], F32)
    H1T = dscr("s_h1t", [8, 128, S], BF16)
    b_QA, b_KA, b_QB, b_KB, b_QI, b_KI, b_G = (Buf(n) for n in ("QA", "KA", "QB", "KB", "QI", "KI", "G"))
    b_VA, b_VB, b_OA, b_OB, b_H1, b_H1T = (Buf(n) for n in ("VA", "VB", "OA", "OB", "H1", "H1T"))

    with ExitStack() as es:
        def sb(name, shape, dt, st=None):
            return (st or es).enter_context(nc.sbuf_tensor("sb_" + name, list(shape), dt))

        banks = [es.enter_context(nc.psum_tensor("bank%d" % i, [128, 512], F32)) for i in range(8)]
        bbuf = [Buf("bank%d" % i, excl=True) for i in range(8)]

        cst = sb("cst", [128, NCONST], F32)
        b_cst = Buf("cst")
        identb = sb("identb", [128, 128], BF16)
        permb = sb("permb", [128, 128], BF16)
        mb4 = sb("mb4", [128, 512], BF16)
        b_cb = Buf("constb")
        P.op("sp", lambda e: e.dma_start(out=cst[:], in_=cst_d[:, :]), writes=[b_cst], dma=True)
        P.op("dve", lambda e: e.tensor_copy(out=identb[:], in_=cst[:, C_ID:C_ID + 128]), reads=[b_cst], writes=[b_cb])
        P.op("dve", lambda e: e.tensor_copy(out=permb[:], in_=cst[:, C_PERM:C_PERM + 128]), reads=[b_cst], writes=[b_cb])
        for j in range(4):
            src = C_MD if j % 2 == 0 else C_MP
            P.op("dve", lambda e, j=j, src=src: e.tensor_copy(out=mb4[:, j * 128:(j + 1) * 128], in_=cst[:, src:src + 128]),
                 reads=[b_cst], writes=[b_cb])
        ident_f = cst[:, C_ID:C_ID + 128]
        negb = sb("negb", [128, 1], F32)
        P.op("dve", lambda e: e.memset(negb[:], -MASK_NEG), writes=[b_cb])

        WI = sb("WI", [128, NT, 8], F32)
        b_WI = Buf("WI")
        bar = sb("bar", [128, 8], F32)

        def barrier(k=0):
            if os.environ.get("KDBG"):
                print("barrier", k, getattr(P, "nops", 0))
            if k > KSTOP:
                P.stopped = True
            P.barrier(lambda e: e.memset(bar[:], 0.0))

        with ExitStack() as s2:
            XT = sb("XT", [128, 8, S], BF16, s2)
            b_XT = [Buf("XT%d" % t) for t in range(NT)]
            xs = [sb("xs%d" % i, [128, D], F32, s2) for i in range(2)]
            b_xs = [Buf("xs%d" % i) for i in range(2)]
            for tt in range(NT):
                sl = tt % 2
                P.op("sp", lambda e, tt=tt, sl=sl: e.dma_start(out=xs[sl][:], in_=x_d[tt * 128:(tt + 1) * 128, :]),
                     writes=[b_xs[sl]], dma=True)
                for half in range(2):
                    bk = (tt * 2 + half) % 2
                    for j in range(4):
                        kc = half * 4 + j
                        P.op("pe", lambda e, bk=bk, j=j, kc=kc, sl=sl: e.transpose(
                            out=banks[bk][:, j * 128:(j + 1) * 128], in_=xs[sl][:, kc * 128:(kc + 1) * 128], identity=ident_f),
                            reads=[b_xs[sl], b_cst], writes=[bbuf[bk]])
                    eng = "act" if half == 0 else "dve"
                    if eng == "act":
                        P.op("act", lambda e, bk=bk, half=half, tt=tt: e.activation(
                            out=XT[:, half * 4:half * 4 + 4, tt * 128:(tt + 1) * 128],
                            in_=banks[bk][:].rearrange("p (a b) -> p a b", a=4), func=AF.Copy),
                            reads=[bbuf[bk]], pw=[b_XT[tt]])
                    else:
                        P.op("dve", lambda e, bk=bk, half=half, tt=tt: e.tensor_copy(
                            out=XT[:, half * 4:half * 4 + 4, tt * 128:(tt + 1) * 128],
                            in_=banks[bk][:].rearrange("p (a b) -> p a b", a=4)),
                            reads=[bbuf[bk]], pw=[b_XT[tt]])

            CT = sb("CT", [128, S], F32, s2)
            ST = sb("ST", [128, S], F32, s2)
            b_CT, b_ST = Buf("CT"), Buf("ST")
            with ExitStack() as sr:
                posi = sb("posi", [128, S], I32, sr)
                posf = sb("posf", [128, S], F32, sr)
                tk = sb("ropek", [128, S], F32, sr)
                b_posi, b_posf, b_tk = Buf("posi"), Buf("posf"), Buf("tk")
                P.op("sp", lambda e: e.dma_start(out=posi[:], in_=pos_d[0:1, :].partition_broadcast(128)),
                     writes=[b_posi], dma=True)
                P.op("dve", lambda e: e.tensor_copy(out=posf[:], in_=posi[:]), reads=[b_posi], writes=[b_posf])
                MAGIC = 12582912.0
                for (T, bT, ca, cb) in ((ST, b_ST, C_ROPE + 0, None), (CT, b_CT, C_ROPE + 1, C_ROPE + 2)):
                    if cb is None:
                        P.op("dve", lambda e, T=T, ca=ca: e.tensor_scalar(
                            out=T[:], in0=posf[:], scalar1=cst[:, ca:ca + 1], scalar2=None, op0=ALU.mult),
                            reads=[b_posf, b_cst], writes=[bT])
                    else:
                        P.op("dve", lambda e, T=T, ca=ca, cb=cb: e.tensor_scalar(
                            out=T[:], in0=posf[:], scalar1=cst[:, ca:ca + 1], scalar2=cst[:, cb:cb + 1],
                            op0=ALU.mult, op1=ALU.add), reads=[b_posf, b_cst], writes=[bT])
                    P.op("dve", lambda e, T=T: e.tensor_scalar(out=tk[:], in0=T[:], scalar1=MAGIC, scalar2=None, op0=ALU.add),
                         reads=[bT], writes=[b_tk])
                    P.op("dve", lambda e, T=T: e.tensor_scalar(out=tk[:], in0=tk[:], scalar1=MAGIC, scalar2=None, op0=ALU.subtract),
                         reads=[b_tk], writes=[b_tk])
                    P.op("dve", lambda e, T=T: e.tensor_tensor(out=T[:], in0=T[:], in1=tk[:], op=ALU.subtract),
                         reads=[bT, b_tk], writes=[bT])
                    P.op("dve", lambda e, T=T: e.tensor_scalar(out=tk[:], in0=T[:], scalar1=0.5, scalar2=None, op0=ALU.is_gt),
                         reads=[bT], writes=[b_tk])
                    P.op("dve", lambda e, T=T: e.tensor_tensor(out=T[:], in0=T[:], in1=tk[:], op=ALU.subtract),
                         reads=[bT, b_tk], writes=[bT])
                    P.op("dve", lambda e, T=T: e.tensor_scalar(out=tk[:], in0=T[:], scalar1=-0.5, scalar2=None, op0=ALU.is_lt),
                         reads=[bT], writes=[b_tk])
                    P.op("dve", lambda e, T=T: e.tensor_tensor(out=T[:], in0=T[:], in1=tk[:], op=ALU.add),
                         reads=[bT, b_tk], writes=[bT])
                    P.op("act", lambda e, T=T: e.activation(out=T[:], in_=T[:], func=AF.Sin, scale=2.0 * math.pi * (1.0 - 1e-6)),
                         reads=[bT], writes=[bT])

            barrier(2)
            wst = [sb("wst%d" % i, [128, 1024], F32, s2) for i in range(2)]
            wbf = [sb("wbf%d" % i, [128, 8, 128], BF16, s2) for i in range(2)]
            b_wst = [Buf("wst%d" % i) for i in range(2)]
            b_wbf = [Buf("wbf%d" % i) for i in range(2)]
            bgt = sb("bgt", [128, 16], F32, s2)
            b_bgt = Buf("bgt")
            P.op("sp", lambda e: e.dma_start(out=bgt[:], in_=bg_d[:, :]), writes=[b_bgt], dma=True)
            qtmp = [sb("qtmp%d" % i, [128, 512], BF16, s2) for i in range(2)]
            b_qtmp = [Buf("qtmp%d" % i) for i in range(2)]
            rt1 = [sb("rt1_%d" % i, [128, 512], F32, s2) for i in range(2)]
            rt2 = [sb("rt2_%d" % i, [128, 512], F32, s2) for i in range(2)]
            b_rt1 = [Buf("rt1_%d" % i) for i in range(2)]
            b_rt2 = [Buf("rt2_%d" % i) for i in range(2)]
            ost = [sb("ost%d" % i, [128, 512], BF16, s2) for i in range(3)]
            b_ost = [Buf("ost%d" % i) for i in range(3)]
            tiles = []
            for c in range(6):
                tiles.append((QA_T[c], b_QA, "rope"))
            for c in range(6):
                tiles.append((KA_T[c], b_KA, "rope"))
            for c in range(6):
                tiles.append((QB_T[c], b_QB, "rope"))
            for c in range(6):
                tiles.append((KB_T[c], b_KB, "rope"))
            for c in range(4):
                tiles.append((QI_T[c], b_QI, "rope"))
            tiles.append((KI_T, b_KI, "rope"))
            for c in range(16):
                tiles.append((G_T[c], b_G, "gate"))
            def fm_main(idx):
                ci, n = idx // NCH, idx % NCH
                dst, bdst, kind = tiles[ci]
                ws = ci % 2
                if n == 0:
                    for cj in ([0, 1] if ci == 0 else [ci + 1]):
                        if cj < len(tiles):
                            wj = cj % 2
                            P.op("act", lambda e, cj=cj, wj=wj: e.dma_start(out=wst[wj][:], in_=wfm_d[cj]), writes=[b_wst[wj]], dma=True)
                            P.op("pool", lambda e, wj=wj: e.tensor_copy(out=wbf[wj][:].rearrange("p a b -> p (a b)"), in_=wst[wj][:]),
                                 reads=[b_wst[wj]], writes=[b_wbf[wj]])
                bk = idx % 2
                tsl = slice(n * 512, (n + 1) * 512)
                xdeps = [b_XT[t] for t in range(n * 4, n * 4 + 4)]
                for kc in range(8):
                    P.op("pe", lambda e, bk=bk, kc=kc, ws=ws, tsl=tsl: e.matmul(
                        banks[bk][:], lhsT=wbf[ws][:, kc, :], rhs=XT[:, kc, tsl], start=(kc == 0), stop=(kc == 7)),
                        reads=[b_wbf[ws]] + xdeps, writes=[bbuf[bk]])
                o3 = idx % 3
                if kind == "gate":
                    gc = ci - 29
                    P.op("act", lambda e, bk=bk, o3=o3, gc=gc: e.activation(
                        out=ost[o3][:], in_=banks[bk][:], func=AF.Sigmoid, bias=bgt[:, gc:gc + 1]),
                        reads=[bbuf[bk], b_bgt], writes=[b_ost[o3]])
                    P.op("sp", lambda e, o3=o3, dst=dst, tsl=tsl: e.dma_start(out=dst[:, tsl], in_=ost[o3][:]),
                         reads=[b_ost[o3]], pw=[bdst], dma=True)
                else:
                    P.op("act", lambda e, bk=bk: e.activation(out=qtmp[bk][:], in_=banks[bk][:], func=AF.Copy),
                         reads=[bbuf[bk]], writes=[b_qtmp[bk]])

            def fm_rope(idx):
                ci, n = idx // NCH, idx % NCH
                dst, bdst, kind = tiles[ci]
                if kind == "gate":
                    return
                bk = idx % 2
                q2 = bk
                o3 = idx % 3
                tsl = slice(n * 512, (n + 1) * 512)
                P.op("pe", lambda e, bk=bk, q2=q2: e.matmul(banks[2 + bk][:], lhsT=permb[:], rhs=qtmp[q2][:], start=True, stop=True),
                     reads=[b_qtmp[q2], b_cb], writes=[bbuf[2 + bk]])
                P.op("pool", lambda e, q2=q2, tsl=tsl: e.tensor_tensor(out=rt1[q2][:], in0=qtmp[q2][:], in1=CT[:, tsl], op=ALU.mult),
                     reads=[b_qtmp[q2], b_CT], writes=[b_rt1[q2]])
                P.op("dve", lambda e, bk=bk, q2=q2, tsl=tsl: e.tensor_tensor(out=rt2[q2][:], in0=banks[2 + bk][:], in1=ST[:, tsl], op=ALU.mult),
                     reads=[bbuf[2 + bk], b_ST], writes=[b_rt2[q2]])
                P.op("dve", lambda e, q2=q2, o3=o3: e.tensor_tensor(out=ost[o3][:], in0=rt1[q2][:], in1=rt2[q2][:], op=ALU.add),
                     reads=[b_rt1[q2], b_rt2[q2]], writes=[b_ost[o3]])
                P.op("sp", lambda e, o3=o3, dst=dst, tsl=tsl: e.dma_start(out=dst[:, tsl], in_=ost[o3][:]),
                     reads=[b_ost[o3]], pw=[bdst], dma=True)

            nitems = len(tiles) * NCH
            for idx in range(nitems + 1):
                if idx < nitems:
                    fm_main(idx)
                if idx >= 1:
                    fm_rope(idx - 1)

            if os.environ.get("KDBG"):
                print("tm start", getattr(P, "nops", 0))
            wtm = sb("wtm", [128, 8, 1544], BF16, s2)
            b_wtm = Buf("wtm")
            for kc in range(8):
                ws = kc % 2
                P.op("sp", lambda e, kc=kc, ws=ws: e.dma_start(out=wst[ws][:, 0:1544 - 1024], in_=wtm_d[:, kc * 1544 + 1024:(kc + 1) * 1544]),
                     writes=[b_wst[ws]], dma=True)
                P.op("pool", lambda e, kc=kc, ws=ws: e.tensor_copy(out=wtm[:, kc, 1024:1544], in_=wst[ws][:, 0:520]),
                     reads=[b_wst[ws]], pw=[b_wtm])
                ws = (kc + 1) % 2
                P.op("sp", lambda e, kc=kc, ws=ws: e.dma_start(out=wst[ws][:], in_=wtm_d[:, kc * 1544:kc * 1544 + 1024]),
                     writes=[b_wst[ws]], dma=True)
                P.op("pool", lambda e, kc=kc, ws=ws: e.tensor_copy(out=wtm[:, kc, 0:1024], in_=wst[ws][:]),
                     reads=[b_wst[ws]], pw=[b_wtm])
            vst = [sb("vst%d" % i, [128, 2, 6, 192], BF16, s2) for i in range(2)]
            b_vst = [Buf("vst%d" % i) for i in range(2)]
            for i in range(2):
                P.op("pool", lambda e, i=i: e.memset(vst[i][:].rearrange("p a b c -> p (a b c)"), 1.0), writes=[b_vst[i]])
            colofs = [0, 384, 776, 1160]
            colw = [384, 392, 384, 384]
            for tt in range(NT):
                vs = tt % 2
                for part in range(4):
                    bk = 4 + (tt * 4 + part) % 4
                    for kc in range(8):
                        P.op("pe", lambda e, bk=bk, kc=kc, tt=tt, part=part: e.matmul(
                            banks[bk][:, 0:colw[part]], lhsT=XT[:, kc, tt * 128:(tt + 1) * 128],
                            rhs=wtm[:, kc, colofs[part]:colofs[part] + colw[part]], start=(kc == 0), stop=(kc == 7)),
                            reads=[b_wtm, b_XT[tt]], writes=[bbuf[bk]])
                    ab = part // 2
                    pr0 = (part % 2) * 3
                    src = banks[bk][:, 0:384].rearrange("p (a b) -> p a b", a=3)
                    P.op("act", lambda e, vs=vs, ab=ab, pr0=pr0, src=src: e.activation(
                        out=vst[vs][:, ab, pr0:pr0 + 3, 0:64], in_=src[:, :, 0:64], func=AF.Copy),
                        reads=[bbuf[bk]], pw=[b_vst[vs]])
                    P.op("dve", lambda e, vs=vs, ab=ab, pr0=pr0, src=src: e.tensor_copy(
                        out=vst[vs][:, ab, pr0:pr0 + 3, 128:192], in_=src[:, :, 64:128]),
                        reads=[bbuf[bk]], pw=[b_vst[vs]])
                    if part == 1:
                        P.op("dve", lambda e, bk=bk, tt=tt: e.tensor_copy(out=WI[:, tt, :], in_=banks[bk][:, 384:392]),
                             reads=[bbuf[bk]], writes=[b_WI])
                P.op("sp", lambda e, vs=vs, tt=tt: e.dma_start(out=VA_S[tt * 128:(tt + 1) * 128, :],
                                                              in_=vst[vs][:, 0].rearrange("p b c -> p (b c)")),
                     reads=[b_vst[vs]], pw=[b_VA], dma=True)
                P.op("sp", lambda e, vs=vs, tt=tt: e.dma_start(out=VB_S[tt * 128:(tt + 1) * 128, :],
                                                              in_=vst[vs][:, 1].rearrange("p b c -> p (b c)")),
                     reads=[b_vst[vs]], pw=[b_VB], dma=True)

        barrier(3)
        with ExitStack() as s3:
            KA = sb("KA", [128, 6, S], BF16, s3)
            VA = sb("VA", [128, NT, 6 * 192], BF16, s3)
            KI = sb("KI", [128, S], BF16, s3)
            b_KAs, b_VAs, b_KIs = Buf("KAs"), Buf("VAs"), Buf("KIs")
            for c in range(6):
                P.op("sp", lambda e, c=c: e.dma_start(out=KA[:, c, :], in_=KA_T[c]), reads=[b_KA], pw=[b_KAs], dma=True)
            for t0 in range(0, NT, 8):
                t1 = min(NT, t0 + 8)
                P.op("sp", lambda e, t0=t0, t1=t1: e.dma_start(
                    out=VA[:, t0:t1, :], in_=VA_S[t0 * 128:t1 * 128, :].rearrange("(t p) c -> p t c", p=128)),
                    reads=[b_VA], pw=[b_VAs], dma=True)
            P.op("sp", lambda e: e.dma_start(out=KI[:], in_=KI_T), reads=[b_KI], writes=[b_KIs], dma=True)

            qch = sb("qch", [128, 6, 2, 256], BF16, s3)
            qich = sb("qich", [128, 4, 2, 256], BF16, s3)
            b_qch, b_qich = Buf("qch"), Buf("qich")
            P.op("pool", lambda e: e.memset(qch[:].rearrange("p a b c -> p (a b c)"), 0.0), writes=[b_qch])
            P.op("pool", lambda e: e.memset(qich[:].rearrange("p a b c -> p (a b c)"), 0.0), writes=[b_qich])
            scs = [sb("sc%d" % i, [128, S], F32, s3) for i in range(2)]
            b_scs = [Buf("sc%d" % i) for i in range(2)]
            junk = sb("junk", [128, S], BF16, s3)
            b_junk = Buf("junk")
            MT = [sb("MT%d" % i, [128, NT, 128], BF16, s3) for i in range(1)]
            b_MT = [Buf("MT%d" % i) for i in range(1)]
            rsb = [sb("rsb%d" % i, [128, 512], BF16, s3) for i in range(2)]
            b_rsb = [Buf("rsb%d" % i) for i in range(2)]
            esb = [sb("esb%d" % i, [128, 512], BF16, s3) for i in range(3)]
            b_esb = [Buf("esb%d" % i) for i in range(3)]
            otile = [sb("otile%d" % i, [128, 6, 128], BF16, s3) for i in range(2)]
            b_otile = [Buf("otile%d" % i) for i in range(2)]
            dsg = [sb("dsg%d" % i, [128, 8, 128], BF16, s3) for i in range(2)]
            b_dsg = [Buf("dsg%d" % i) for i in range(2)]
            wab = [sb("wab%d" % i, [128, 8], F32, s3) for i in range(2)]
            wsg = [sb("wsg%d" % i, [128, 8], F32, s3) for i in range(2)]
            b_wab = [Buf("wab%d" % i) for i in range(2)]
            b_wsg = [Buf("wsg%d" % i) for i in range(2)]
            dmin = [sb("dmin%d" % i, [128, 1], F32, s3) for i in range(2)]
            b_dmin = [Buf("dmin%d" % i) for i in range(2)]
            small = sb("small", [128, 16], F32, s3)
            b_small = Buf("small")
            hw = sb("hw", [128, NIT + 1], F32, s3)
            b_hw = Buf("hw")
            dtmp = sb("dtmp", [128, 128], F32, s3)
            b_dtmp = Buf("dtmp")
            rden = [sb("rden%d" % i, [128, 128], F32, s3) for i in range(2)]
            b_rden = [Buf("rden%d" % i) for i in range(2)]
            cnts = {"z": 0, "e": 0, "o": 0}

            def stageB1(qt):
                qc, qo = qt // 2, (qt % 2) * 128
                if qt % 2 == 0:
                    for hh_ in range(2):
                        P.op("sp", lambda e, qc=qc, hh_=hh_: e.dma_start(
                            out=qich[hh_ * 64:(hh_ + 1) * 64, :, hh_, :],
                            in_=QI_T[:, hh_ * 64:(hh_ + 1) * 64, qc * 256:(qc + 1) * 256].rearrange("c p t -> p c t")),
                            reads=[b_QI], writes=[b_qich], dma=True)
                n_s = qt + 1
                sc, b_sc = scs[qt % 2], b_scs[qt % 2]
                n = 128 * n_s
                nchk = (n_s + 3) // 4
                ds = qt % 2
                P.op("dve", lambda e, ds=ds, qt=qt: e.tensor_scalar(out=wsg[ds][:], in0=WI[:, qt, :], scalar1=0.0, scalar2=-0.5,
                                                                   op0=ALU.is_gt, op1=ALU.add),
                     reads=[b_WI], writes=[b_wsg[ds]])
                P.op("dve", lambda e, ds=ds, qt=qt: e.scalar_tensor_tensor(out=wsg[ds][:], in0=WI[:, qt, :], scalar=0.0, in1=wsg[ds][:],
                                                                          op0=ALU.is_lt, op1=ALU.subtract),
                     reads=[b_WI, b_wsg[ds]], writes=[b_wsg[ds]])
                P.op("dve", lambda e, ds=ds: e.tensor_scalar(out=wsg[ds][:], in0=wsg[ds][:], scalar1=-1.0, scalar2=0.5,
                                                            op0=ALU.mult, op1=ALU.add),
                     reads=[b_wsg[ds]], writes=[b_wsg[ds]])
                P.op("dve", lambda e, ds=ds, qt=qt: e.tensor_tensor(out=wab[ds][:], in0=WI[:, qt, :], in1=wsg[ds][:], op=ALU.mult),
                     reads=[b_WI, b_wsg[ds]], writes=[b_wab[ds]])
                for h in range(8):
                    P.op("dve", lambda e, ds=ds, h=h: e.tensor_scalar(out=dsg[ds][:, h, :], in0=cst[:, C_ID:C_ID + 128],
                                                                     scalar1=wsg[ds][:, h:h + 1], scalar2=None, op0=ALU.mult),
                         reads=[b_cst, b_wsg[ds]], writes=[b_dsg[ds]])
                yield
                items = [(c, h) for c in range(nchk) for h in range(8)]

                def z_part(c, h):
                    ncols = min(512, n - 512 * c)
                    hp, hh = h // 2, h % 2
                    zb = (c * 8 + h) % 2
                    rows = slice(hh * 64, hh * 64 + 64)
                    P.op("pe", lambda e, zb=zb, hh=hh, hp=hp, c=c, ncols=ncols: e.matmul(
                        banks[zb][:, 0:ncols], lhsT=qich[:, hp, hh, qo:qo + 128], rhs=KI[:, c * 512:c * 512 + ncols],
                        start=True, stop=True), reads=[b_qich, b_KIs], writes=[bbuf[zb]])
                    P.op("act", lambda e, zb=zb, ncols=ncols, h=h: e.activation(
                        out=rsb[zb][:, 0:ncols], in_=banks[zb][:, 0:ncols], func=AF.Relu, scale=wab[ds][:, h:h + 1]),
                        reads=[bbuf[zb], b_wab[ds]], writes=[b_rsb[zb]])

                def d_part(c, h):
                    ncols = min(512, n - 512 * c)
                    zb = (c * 8 + h) % 2
                    ab = 2 if c % 2 == 0 else 7
                    P.op("pe", lambda e, ab=ab, zb=zb, ncols=ncols, h=h: e.matmul(
                        banks[ab][:, 0:ncols], lhsT=dsg[ds][:, h, :], rhs=rsb[zb][:, 0:ncols], start=(h == 0), stop=(h == 7)),
                        reads=[b_dsg[ds], b_rsb[zb]], writes=[bbuf[ab]])
                    if h != 7:
                        return
                    last = (c == nchk - 1)
                    nfull = ncols - 128 if last else ncols
                    if nfull > 0:
                        P.op("dve", lambda e, ab=ab, c=c, nfull=nfull: e.tensor_copy(out=sc[:, c * 512:c * 512 + nfull], in_=banks[ab][:, 0:nfull]),
                             reads=[bbuf[ab]], writes=[b_sc])
                    if last:
                        dsl = slice(n - 128, n)
                        P.op("dve", lambda e, ab=ab, nfull=nfull: e.tensor_tensor(out=dtmp[:], in0=banks[ab][:, nfull:nfull + 128],
                                                                                 in1=cst[:, C_TRIP:C_TRIP + 128], op=ALU.add),
                             reads=[bbuf[ab], b_cst], writes=[b_dtmp])
                        P.op("dve", lambda e: e.tensor_reduce(out=dmin[qt % 2][:, 0:1], in_=dtmp[:], axis=AX.X, op=ALU.min),
                             reads=[b_dtmp], writes=[b_dmin[qt % 2]])
                        P.op("dve", lambda e, ab=ab, nfull=nfull, dsl=dsl: e.tensor_tensor(out=sc[:, dsl], in0=banks[ab][:, nfull:nfull + 128],
                                                                                          in1=cst[:, C_TRIM:C_TRIM + 128], op=ALU.add),
                             reads=[bbuf[ab], b_cst], writes=[b_sc])

                for i_ in range(len(items) + 1):
                    if i_ < len(items):
                        z_part(*items[i_])
                    if i_ >= 1:
                        d_part(*items[i_ - 1])
                    if i_ % 2 == 1:
                        yield

            def stageB2(qt):
                n_s = qt + 1
                n = 128 * n_s
                nchk = (n_s + 3) // 4
                sc, b_sc = scs[qt % 2], b_scs[qt % 2]
                if n > KTOP:
                    P.op("dve", lambda e, n=n: e.tensor_reduce(out=small[:, 0:1], in_=sc[:, 0:n], axis=AX.X, op=ALU.max),
                         reads=[b_sc], writes=[b_small])
                    P.op("dve", lambda e, n=n: e.tensor_reduce(out=small[:, 1:2], in_=sc[:, 0:n - 128], axis=AX.X, op=ALU.min),
                         reads=[b_sc], writes=[b_small])
                    P.op("dve", lambda e: e.tensor_tensor(out=small[:, 1:2], in0=small[:, 1:2], in1=dmin[qt % 2][:, 0:1], op=ALU.min),
                         reads=[b_small, b_dmin[qt % 2]], writes=[b_small])
                    P.op("dve", lambda e: e.tensor_tensor(out=small[:, 3:4], in0=small[:, 0:1], in1=small[:, 1:2], op=ALU.subtract),
                         reads=[b_small], writes=[b_small])
                    P.op("dve", lambda e: e.tensor_scalar(out=small[:, 3:4], in0=small[:, 3:4], scalar1=1.0001, scalar2=1e-6,
                                                          op0=ALU.mult, op1=ALU.add), reads=[b_small], writes=[b_small])
                    P.op("dve", lambda e: e.tensor_scalar(out=hw[:], in0=cst[:, C_POW:C_POW + NIT + 1], scalar1=small[:, 3:4], scalar2=None,
                                                          op0=ALU.mult), reads=[b_small, b_cst], writes=[b_hw])
                    P.op("dve", lambda e: e.tensor_tensor(out=small[:, 4:5], in0=small[:, 1:2], in1=hw[:, 0:1], op=ALU.add),
                         reads=[b_small, b_hw], writes=[b_small])
                    yield
                    for it in range(NIT):
                        P.op("dve", lambda e, n=n: e.tensor_scalar(out=junk[:, 0:n], in0=sc[:, 0:n], scalar1=small[:, 4:5], scalar2=None,
                                                                   op0=ALU.is_ge, op1=ALU.add, accum_out=small[:, 5:6]),
                             reads=[b_sc, b_small], writes=[b_junk, b_small])
                        P.op("dve", lambda e, it=it: e.tensor_scalar(out=small[:, 6:7], in0=small[:, 5:6], scalar1=float(KTOP) - 0.5,
                                                                     scalar2=hw[:, it:it + 1], op0=ALU.is_ge, op1=ALU.mult),
                             reads=[b_small, b_hw], writes=[b_small])
                        P.op("dve", lambda e, it=it: e.scalar_tensor_tensor(out=small[:, 4:5], in0=small[:, 6:7], scalar=hw[:, it + 1:it + 2],
                                                                            in1=small[:, 4:5], op0=ALU.subtract, op1=ALU.add),
                             reads=[b_small, b_hw], writes=[b_small])
                        yield
                    P.op("dve", lambda e: e.scalar_tensor_tensor(out=small[:, 7:8], in0=hw[:, NIT:NIT + 1], scalar=-2.0, in1=small[:, 4:5],
                                                                 op0=ALU.mult, op1=ALU.add),
                         reads=[b_small, b_hw], writes=[b_small])
                else:
                    P.op("dve", lambda e: e.memset(small[:, 7:8], -1.0e29), writes=[b_small])
                P.op("dve", lambda e, n=n: e.tensor_scalar(out=junk[:, 0:n], in0=sc[:, 0:n], scalar1=small[:, 7:8], scalar2=None, op0=ALU.is_ge),
                     reads=[b_sc, b_small], writes=[b_junk])
                yield

            def stageB2b(qt):
                n_s = qt + 1
                nchk = (n_s + 3) // 4
                ms = 0
                for c in range(nchk):
                    g = min(4, n_s - 4 * c)
                    tb = 7 if c % 2 == 0 else 2
                    tbank = banks[tb][:, 0:256].bitcast(BF16)
                    for j in range(g):
                        st = 4 * c + j
                        P.op("pe", lambda e, tbank=tbank, j=j, st=st: e.transpose(
                            out=tbank[:, j * 128:(j + 1) * 128], in_=junk[:, st * 128:(st + 1) * 128], identity=identb[:]),
                            reads=[b_junk, b_cb], writes=[bbuf[tb]])
                    P.op("act", lambda e, tbank=tbank, ms=ms, c=c, g=g: e.activation(
                        out=MT[ms][:, 4 * c:4 * c + g, :].rearrange("p a b -> p (a b)"), in_=tbank[:, 0:g * 128], func=AF.Identity,
                        scale=MASK_NEG, bias=negb[:, 0:1]),
                        reads=[bbuf[tb], b_cb], pw=[b_MT[ms]])
                    yield

            def stageC(qt):
                qc, qo = qt // 2, (qt % 2) * 128
                if qt % 2 == 0:
                    for hh_ in range(2):
                        P.op("sp", lambda e, qc=qc, hh_=hh_: e.dma_start(
                            out=qch[hh_ * 64:(hh_ + 1) * 64, :, hh_, :],
                            in_=QA_T[:, hh_ * 64:(hh_ + 1) * 64, qc * 256:(qc + 1) * 256].rearrange("c p t -> p c t")),
                            reads=[b_QA], writes=[b_qch], dma=True)
                n_s = qt + 1
                nchk = (n_s + 3) // 4
                ms = 0
                ot = qt % 2
                SK = 2
                items = [(h, c) for h in range(12) for c in range(nchk)]
                e0 = cnts["e"]
                cnts["e"] += len(items)
                o0 = cnts["o"]
                cnts["o"] += 12

                def qk_part(idx):
                    h, c = items[idx]
                    hp, hh = h // 2, h % 2
                    rows = slice(hh * 64, hh * 64 + 64)
                    g = min(4, n_s - 4 * c)
                    sbk = 3 + (e0 + idx) % 2
                    es_ = (e0 + idx) % 3
                    for j in range(g):
                        st = 4 * c + j
                        P.op("pe", lambda e, sbk=sbk, j=j, st=st, hh=hh, hp=hp: e.matmul(
                            banks[sbk][:, j * 128:(j + 1) * 128], lhsT=KA[:, hp, st * 128:(st + 1) * 128],
                            rhs=qch[:, hp, hh, qo:qo + 128], start=True, stop=False),
                            reads=[b_KAs, b_qch], writes=[bbuf[sbk]])
                        P.op("pe", lambda e, sbk=sbk, j=j, st=st: e.matmul(
                            banks[sbk][:, j * 128:(j + 1) * 128], lhsT=identb[:], rhs=MT[ms][:, st, :], start=False, stop=True),
                            reads=[b_cb, b_MT[ms]], writes=[bbuf[sbk]])
                    P.op("act", lambda e, sbk=sbk, es_=es_, g=g: e.activation(
                        out=esb[es_][:, 0:g * 128], in_=banks[sbk][:, 0:g * 128], func=AF.Exp, scale=0.125),
                        reads=[bbuf[sbk]], writes=[b_esb[es_]])

                def pv_part(idx):
                    h, c = items[idx]
                    hp, hh = h // 2, h % 2
                    g = min(4, n_s - 4 * c)
                    es_ = (e0 + idx) % 3
                    ob = 5 + (o0 + h) % 2
                    vofs = hp * 192 + hh * 64
                    for j in range(g):
                        st = 4 * c + j
                        P.op("pe", lambda e, ob=ob, es_=es_, j=j, st=st, vofs=vofs: e.matmul(
                            banks[ob][:, 0:128], lhsT=VA[:, st, vofs:vofs + 128], rhs=esb[es_][:, j * 128:(j + 1) * 128],
                            start=(st == 0), stop=(st == n_s - 1)),
                            reads=[b_VAs, b_esb[es_]], writes=[bbuf[ob]])
                    if c != nchk - 1:
                        return
                    num = slice(0, 64) if hh == 0 else slice(64, 128)
                    den = slice(64, 128) if hh == 0 else slice(0, 64)
                    rd = h % 2
                    P.op("act", lambda e, ob=ob, den=den, rd=rd: e.activation(out=rden[rd][den, :], in_=banks[ob][den, 0:128], func=AF.Ln),
                         reads=[bbuf[ob]], writes=[b_rden[rd]])
                    P.op("act", lambda e, den=den, rd=rd: e.activation(out=rden[rd][den, :], in_=rden[rd][den, :], func=AF.Exp, scale=-1.0),
                         reads=[b_rden[rd]], writes=[b_rden[rd]])
                    P.op("dve", lambda e, ob=ob, num=num, den=den, rd=rd, hp=hp: e.tensor_tensor(
                        out=otile[ot][num, hp, :], in0=banks[ob][num, 0:128], in1=rden[rd][den, :], op=ALU.mult),
                        reads=[bbuf[ob], b_rden[rd]], pw=[b_otile[ot]])

                for i_ in range(len(items) + SK):
                    if i_ < len(items):
                        qk_part(i_)
                    if i_ >= SK:
                        pv_part(i_ - SK)
                    yield
                P.op("sp", lambda e, ot=ot, qt=qt: e.dma_start(
                    out=OA_T[:, :, qt * 128:(qt + 1) * 128].rearrange("c p t -> p c t"), in_=otile[ot][:]),
                    reads=[b_otile[ot]], pw=[b_OA], dma=True)
                yield

            def interleave(*gens):
                live = [g for g in gens if g is not None]
                while live:
                    nxt = []
                    for g in live:
                        try:
                            next(g)
                            nxt.append(g)
                        except StopIteration:
                            pass
                    live = nxt

            interleave(stageB1(0))
            interleave(stageB2(0), stageB1(1) if NT > 1 else None)
            interleave(stageB2b(0))
            for qt in range(NT):
                interleave(stageC(qt),
                           stageB2(qt + 1) if qt + 1 < NT else None,
                           stageB1(qt + 2) if qt + 2 < NT else None)
                if qt + 1 < NT:
                    interleave(stageB2b(qt + 1))

        barrier(5)
        with ExitStack() as s5:
            acc = [sb("acc%d" % i, [128, S], F32, s5) for i in range(4)]
            b_acc = [Buf("acc%d" % i) for i in range(4)]
            QB = sb("QBg", [128, 2, 2, S], BF16, s5)
            KB = sb("KBg", [128, 2, S], BF16, s5)
            VB = sb("VBg", [128, NT, 2 * 192], BF16, s5)
            b_QBs, b_KBs, b_VBs = Buf("QBs"), Buf("KBs"), Buf("VBs")
            psb5 = [sb("p5_%d" % i, [128, 512], BF16, s5) for i in range(3)]
            b_p5 = [Buf("p5_%d" % i) for i in range(3)]
            obt = sb("obt", [128, 2, S], BF16, s5)
            b_obt = Buf("obt")
            rd5 = sb("rd5", [128, S], F32, s5)
            b_rd5 = Buf("rd5")
            mb4n = sb("mb4n", [128, 512], BF16, s5)
            P.op("dve", lambda e: e.tensor_scalar(out=mb4n[:], in0=mb4[:], scalar1=MASK_NEG, scalar2=-MASK_NEG, op0=ALU.mult, op1=ALU.add),
                 reads=[b_cb], writes=[b_cb])
            P.op("pool", lambda e: e.memset(QB[:].rearrange("p a b c -> p (a b c)"), 0.0), writes=[b_QBs])
            oc = 0
            SK5 = 2
            for g, dil in enumerate((1, 4, 16)):
                L = S // dil
                nsub = L // 128
                for c in range(2):
                    for hh_ in range(2):
                        P.op("sp", lambda e, c=c, g=g, hh_=hh_: e.dma_start(
                            out=QB[hh_ * 64:(hh_ + 1) * 64, c, hh_, :], in_=QB_T[2 * g + c][hh_ * 64:(hh_ + 1) * 64, :]),
                            reads=[b_QB], pw=[b_QBs], dma=True)
                    P.op("sp", lambda e, c=c, g=g: e.dma_start(out=KB[:, c, :], in_=KB_T[2 * g + c]), reads=[b_KB], pw=[b_KBs], dma=True)
                for r in range(dil):
                    for iu0 in range(0, nsub, 8):
                        iu1 = min(nsub, iu0 + 8)
                        base = r + dil * 128 * iu0
                        cnt_rows = (iu1 - iu0) * 128
                        srcv = VB_S[base:base + dil * (cnt_rows - 1) + 1:dil, g * 384:(g + 1) * 384]
                        P.op("sp", lambda e, r=r, iu0=iu0, iu1=iu1, srcv=srcv, nsub=nsub: e.dma_start(
                            out=VB[:, r * nsub + iu0:r * nsub + iu1, :], in_=srcv.rearrange("(t p) c -> p t c", p=128)),
                            reads=[b_VB], pw=[b_VBs], dma=True)
                items5 = []
                for hl in range(4):
                    for r in range(dil):
                        blocks = [(0, 0, "D")]
                        for iu in range(1, nsub):
                            blocks.append((iu - 1, iu, "P"))
                            blocks.append((iu, iu, "D"))
                        cur_ob = None
                        for b0 in range(0, len(blocks), 4):
                            chunk = blocks[b0:b0 + 4]
                            obs = []
                            for (st, iu, kind) in chunk:
                                if iu % 4 == 0 and kind == ("D" if iu == 0 else "P"):
                                    cur_ob = 4 + oc % 4
                                    oc += 1
                                obs.append(cur_ob)
                            items5.append((hl, r, chunk, obs))

                def tok5(r, iu, dil=dil):
                    b0 = r + dil * 128 * iu
                    return slice(b0, b0 + dil * 127 + 1, dil)

                def qk5(idx, g=g, nsub=nsub):
                    hl, r, chunk, obs = items5[idx]
                    hp, hh = hl // 2, hl % 2
                    gsz = len(chunk)
                    sbk = idx % 3
                    for j, (st, iu, kind) in enumerate(chunk):
                        ksl, qsl = tok5(r, st), tok5(r, iu)
                        P.op("pe", lambda e, sbk=sbk, j=j, hp=hp, hh=hh, ksl=ksl, qsl=qsl: e.matmul(
                            banks[sbk][:, j * 128:(j + 1) * 128], lhsT=KB[:, hp, ksl], rhs=QB[:, hp, hh, qsl],
                            start=True, stop=False), reads=[b_KBs, b_QBs], writes=[bbuf[sbk]])
                        P.op("pe", lambda e, sbk=sbk, j=j: e.matmul(
                            banks[sbk][:, j * 128:(j + 1) * 128], lhsT=identb[:], rhs=mb4n[:, j * 128:(j + 1) * 128],
                            start=False, stop=True), reads=[b_cb], writes=[bbuf[sbk]])
                    P.op("act", lambda e, sbk=sbk, gsz=gsz: e.activation(
                        out=psb5[sbk][:, 0:gsz * 128], in_=banks[sbk][:, 0:gsz * 128], func=AF.Exp, scale=0.125),
                        reads=[bbuf[sbk]], writes=[b_p5[sbk]])

                def pv5(idx, g=g, nsub=nsub, dil=dil):
                    hl, r, chunk, obs = items5[idx]
                    hp, hh = hl // 2, hl % 2
                    vofs = hp * 192 + hh * 64
                    es_ = idx % 3
                    for j, (st, iu, kind) in enumerate(chunk):
                        ob = obs[j]
                        jo = iu % 4
                        P.op("pe", lambda e, ob=ob, jo=jo, es_=es_, j=j, st=st, r=r, vofs=vofs, kind=kind, iu=iu: e.matmul(
                            banks[ob][:, jo * 128:(jo + 1) * 128], lhsT=VB[:, r * nsub + st, vofs:vofs + 128],
                            rhs=psb5[es_][:, j * 128:(j + 1) * 128],
                            start=(kind == "P" or iu == 0), stop=(kind == "D")),
                            reads=[b_VBs, b_p5[es_]], writes=[bbuf[ob]])
                        if kind == "D" and (iu % 4 == 3 or iu == nsub - 1):
                            iu_lo = iu - (iu % 4)
                            nt_ = iu - iu_lo + 1
                            b00 = r + dil * 128 * iu_lo
                            dst = acc[hl][:, b00:b00 + dil * (nt_ * 128 - 1) + 1:dil]
                            if g == 0:
                                P.op("dve", lambda e, ob=ob, nt_=nt_, dst=dst: e.tensor_copy(out=dst, in_=banks[ob][:, 0:nt_ * 128]),
                                     reads=[bbuf[ob]], writes=[b_acc[hl]])
                            else:
                                P.op("dve", lambda e, ob=ob, nt_=nt_, dst=dst: e.tensor_tensor(
                                    out=dst, in0=banks[ob][:, 0:nt_ * 128], in1=dst, op=ALU.add),
                                    reads=[bbuf[ob], b_acc[hl]], writes=[b_acc[hl]])

                for i_ in range(len(items5) + SK5):
                    if i_ < len(items5):
                        qk5(i_)
                    if i_ >= SK5:
                        pv5(i_ - SK5)
            for hl in range(4):
                hp, hh = hl // 2, hl % 2
                num = slice(0, 64) if hh == 0 else slice(64, 128)
                den = slice(64, 128) if hh == 0 else slice(0, 64)
                P.op("dve", lambda e, hl=hl, den=den, num=num: e.reciprocal(out=rd5[num, :], in_=acc[hl][den, :]),
                     reads=[b_acc[hl]], writes=[b_rd5])
                P.op("dve", lambda e, hl=hl, num=num, den=den, hp=hp: e.tensor_tensor(
                    out=obt[num, hp, :], in0=acc[hl][num, :], in1=rd5[num, :], op=ALU.mult),
                    reads=[b_acc[hl], b_rd5], writes=[b_obt])
            for c in range(2):
                P.op("sp", lambda e, c=c: e.dma_start(out=OB_T[c], in_=obt[:, c, :]), reads=[b_obt], pw=[b_OB], dma=True)

        def load_w(dst2d, src2d, ncols, stg, b_stg, b_dst, cw=2048, engs=("pool", "dve", "act")):
            for i, c0 in enumerate(range(0, ncols, cw)):
                c1 = min(ncols, c0 + cw)
                sl = i % len(stg)
                P.op("sp", lambda e, c0=c0, c1=c1, sl=sl: e.dma_start(out=stg[sl][:, 0:c1 - c0], in_=src2d[:, c0:c1]),
                     writes=[b_stg[sl]], dma=True)
                ceng = engs[i % len(engs)]
                if ceng == "act":
                    P.op("act", lambda e, c0=c0, c1=c1, sl=sl: e.activation(out=dst2d[:, c0:c1], in_=stg[sl][:, 0:c1 - c0], func=AF.Copy),
                         reads=[b_stg[sl]], pw=[b_dst])
                else:
                    P.op(ceng, lambda e, c0=c0, c1=c1, sl=sl: e.tensor_copy(out=dst2d[:, c0:c1], in_=stg[sl][:, 0:c1 - c0]),
                         reads=[b_stg[sl]], pw=[b_dst])

        def layer_norm(src, b_src, dst, b_dst, gbc, bbc, b_ln, stats, mv, b_st):
            for hf in range(2):
                P.op("dve", lambda e, hf=hf: e.bn_stats(out=stats[:, hf * 6:(hf + 1) * 6], in_=src[:, hf * 512:(hf + 1) * 512]),
                     reads=[b_src], writes=[b_st])
            P.op("dve", lambda e: e.bn_aggr(out=mv[:, 0:2], in_=stats[:, 0:12]), reads=[b_st], writes=[b_st])
            P.op("dve", lambda e: e.tensor_scalar(out=mv[:, 2:3], in0=mv[:, 1:2], scalar1=LN_EPS, scalar2=None, op0=ALU.add),
                 reads=[b_st], writes=[b_st])
            P.op("act", lambda e: e.activation(out=mv[:, 2:3], in_=mv[:, 2:3], func=AF.Sqrt), reads=[b_st], writes=[b_st])
            P.op("dve", lambda e: e.reciprocal(out=mv[:, 3:4], in_=mv[:, 2:3]), reads=[b_st], writes=[b_st])
            P.op("dve", lambda e: e.tensor_scalar(out=dst[:], in0=src[:], scalar1=mv[:, 0:1], scalar2=mv[:, 3:4],
                                                  op0=ALU.subtract, op1=ALU.mult), reads=[b_src, b_st], writes=[b_dst])
            for k, eng_ in enumerate(("pool", "dve")):
                bh = Buf("lnhalf")
                csl = slice(k * 512, (k + 1) * 512)
                P.op(eng_, lambda e, csl=csl: e.tensor_tensor(out=dst[:, csl], in0=dst[:, csl], in1=gbc[:, csl], op=ALU.mult),
                     reads=[b_ln], writes=[bh], pw=[b_dst])
                P.op(eng_, lambda e, csl=csl: e.tensor_tensor(out=dst[:, csl], in0=dst[:, csl], in1=bbc[:, csl], op=ALU.add),
                     reads=[b_ln], writes=[bh], pw=[b_dst])

        barrier(6)
        with ExitStack() as s6:
            stg = [sb("stg%d" % i, [128, 2048], F32, s6) for i in range(3)]
            b_stg = [Buf("stg%d" % i) for i in range(3)]
            WA = sb("WA", [128, 6 * 1024], BF16, s6)
            WB = sb("WB", [128, 2 * 1024], BF16, s6)
            WO = sb("WO", [128, 8 * 1024], BF16, s6)
            b_WA, b_WB, b_WO = Buf("WA"), Buf("WB"), Buf("WO")
            load_w(WA, wa_d, 6 * 1024, stg, b_stg, b_WA)
            load_w(WB, wb_d, 2 * 1024, stg, b_stg, b_WB)
            load_w(WO, wo_d, 8 * 1024, stg, b_stg, b_WO)
            gbc = sb("gbc1", [128, D], F32, s6)
            bbc = sb("bbc1", [128, D], F32, s6)
            b_ln1 = Buf("ln1")
            P.op("sp", lambda e, gbc=gbc: e.dma_start(out=gbc[:], in_=ln_d[0:1, :].partition_broadcast(128)), writes=[b_ln1], dma=True)
            P.op("sp", lambda e, bbc=bbc: e.dma_start(out=bbc[:], in_=ln_d[1:2, :].partition_broadcast(128)), writes=[b_ln1], dma=True)
            oac = [sb("oac%d" % i, [128, 6, 512], BF16, s6) for i in range(2)]
            obc = [sb("obc%d" % i, [128, 2, 512], BF16, s6) for i in range(2)]
            gch = [sb("gch%d" % i, [128, 16, 512], BF16, s6) for i in range(2)]
            b_oac = [Buf("oac%d" % i) for i in range(2)]
            b_obc = [Buf("obc%d" % i) for i in range(2)]
            b_gch = [Buf("gch%d" % i) for i in range(2)]
            mg = [sb("mg%d" % i, [128, 8, 512], BF16, s6) for i in range(2)]
            b_mg = [Buf("mg%d" % i) for i in range(2)]
            mt1 = [sb("mt1_%d" % i, [128, 512], F32, s6) for i in range(2)]
            b_mt1 = [Buf("mt1_%d" % i) for i in range(2)]
            xt6 = [sb("xt6_%d" % i, [128, D], F32, s6) for i in range(2)]
            b_xt6 = [Buf("xt6_%d" % i) for i in range(2)]
            hp6 = [sb("hp6_%d" % i, [128, D], F32, s6) for i in range(2)]
            b_hp6 = [Buf("hp6_%d" % i) for i in range(2)]
            h1s = [sb("h1s_%d" % i, [128, D], F32, s6) for i in range(2)]
            b_h1s = [Buf("h1s_%d" % i) for i in range(2)]
            h1t = [sb("h1t_%d" % i, [128, 8, 128], BF16, s6) for i in range(2)]
            b_h1t = [Buf("h1t_%d" % i) for i in range(2)]
            stats = [sb("stats6_%d" % i, [128, 12], F32, s6) for i in range(2)]
            mv = [sb("mv6_%d" % i, [128, 4], F32, s6) for i in range(2)]
            b_st = [Buf("st6_%d" % i) for i in range(2)]
            mc = 0
            tcnt = 0
            pend6 = None

            def p6a_back(tt, ts_):
                for hf in range(2):
                    bk = 6 + hf
                    for j in range(4):
                        kc = hf * 4 + j
                        P.op("pe", lambda e, bk=bk, j=j, kc=kc: e.transpose(
                            out=banks[bk][:, j * 128:(j + 1) * 128], in_=h1s[ts_][:, kc * 128:(kc + 1) * 128], identity=ident_f),
                            reads=[b_h1s[ts_], b_cst], writes=[bbuf[bk]])
                    P.op("act", lambda e, bk=bk, hf=hf: e.activation(
                        out=h1t[ts_][:, hf * 4:hf * 4 + 4, :], in_=banks[bk][:].rearrange("p (a b) -> p a b", a=4), func=AF.Copy),
                        reads=[bbuf[bk]], writes=[b_h1t[ts_]])
                P.op("sp", lambda e: e.dma_start(
                    out=H1T[:, :, tt * 128:(tt + 1) * 128].rearrange("c p t -> p c t"), in_=h1t[ts_][:]),
                    reads=[b_h1t[ts_]], pw=[b_H1T], dma=True)

            for n in range(NCH):
                cs = n % 2
                tsl = slice(n * 512, (n + 1) * 512)
                P.op("sp", lambda e, cs=cs, tsl=tsl: e.dma_start(out=oac[cs][:], in_=OA_T[:, :, tsl].rearrange("c p t -> p c t")),
                     reads=[b_OA], writes=[b_oac[cs]], dma=True)
                P.op("sp", lambda e, cs=cs, tsl=tsl: e.dma_start(out=obc[cs][:], in_=OB_T[:, :, tsl].rearrange("c p t -> p c t")),
                     reads=[b_OB], writes=[b_obc[cs]], dma=True)
                P.op("sp", lambda e, cs=cs, tsl=tsl: e.dma_start(out=gch[cs][:], in_=G_T[:, :, tsl].rearrange("c p t -> p c t")),
                     reads=[b_G], writes=[b_gch[cs]], dma=True)
                for oc_ in range(8):
                    ba = (mc % 2) * 2
                    bb = ba + 1
                    m1 = mc % 2
                    mc += 1
                    for kc in range(6):
                        P.op("pe", lambda e, ba=ba, kc=kc, oc_=oc_, cs=cs: e.matmul(
                            banks[ba][:], lhsT=WA[:, kc * 1024 + oc_ * 128:kc * 1024 + oc_ * 128 + 128], rhs=oac[cs][:, kc, :],
                            start=(kc == 0), stop=(kc == 5)), reads=[b_WA, b_oac[cs]], writes=[bbuf[ba]])
                    for kc in range(2):
                        P.op("pe", lambda e, bb=bb, kc=kc, oc_=oc_, cs=cs: e.matmul(
                            banks[bb][:], lhsT=WB[:, kc * 1024 + oc_ * 128:kc * 1024 + oc_ * 128 + 128], rhs=obc[cs][:, kc, :],
                            start=(kc == 0), stop=(kc == 1)), reads=[b_WB, b_obc[cs]], writes=[bbuf[bb]])
                    P.op("dve", lambda e, ba=ba, m1=m1, oc_=oc_, cs=cs: e.tensor_tensor(
                        out=mt1[m1][:], in0=banks[ba][:], in1=gch[cs][:, oc_, :], op=ALU.mult),
                        reads=[bbuf[ba], b_gch[cs]], writes=[b_mt1[m1]])
                    P.op("dve", lambda e, bb=bb, m1=m1, oc_=oc_, cs=cs: e.tensor_tensor(
                        out=mg[cs][:, oc_, :], in0=banks[bb][:], in1=gch[cs][:, 8 + oc_, :], op=ALU.mult),
                        reads=[bbuf[bb], b_gch[cs]], writes=[b_mg[cs]])
                    P.op("pool", lambda e, m1=m1, oc_=oc_, cs=cs: e.tensor_tensor(
                        out=mg[cs][:, oc_, :], in0=mg[cs][:, oc_, :], in1=mt1[m1][:], op=ALU.add),
                        reads=[b_mt1[m1], b_mg[cs]], writes=[b_mg[cs]])
                for t4 in range(4):
                    tt = n * 4 + t4
                    ts_ = tcnt % 2
                    tcnt += 1
                    P.op("sp", lambda e, ts_=ts_, tt=tt: e.dma_start(out=xt6[ts_][:], in_=x_d[tt * 128:(tt + 1) * 128, :]),
                         writes=[b_xt6[ts_]], dma=True)
                    for hf in range(2):
                        bk = (4 if ts_ == 0 else 2) + hf
                        for kc in range(8):
                            P.op("pe", lambda e, bk=bk, kc=kc, t4=t4, hf=hf, cs=cs: e.matmul(
                                banks[bk][:], lhsT=mg[cs][:, kc, t4 * 128:(t4 + 1) * 128],
                                rhs=WO[:, kc * 1024 + hf * 512:kc * 1024 + hf * 512 + 512], start=(kc == 0), stop=(kc == 7)),
                                reads=[b_mg[cs], b_WO], writes=[bbuf[bk]])
                        P.op("dve", lambda e, bk=bk, hf=hf, ts_=ts_: e.scalar_tensor_tensor(
                            out=hp6[ts_][:, hf * 512:(hf + 1) * 512], in0=xt6[ts_][:, hf * 512:(hf + 1) * 512], scalar=ALPHA,
                            in1=banks[bk][:], op0=ALU.mult, op1=ALU.add),
                            reads=[b_xt6[ts_], bbuf[bk]], writes=[b_hp6[ts_]])
                    if pend6 is not None:
                        p6a_back(*pend6)
                    layer_norm(hp6[ts_], b_hp6[ts_], h1s[ts_], b_h1s[ts_], gbc, bbc, b_ln1, stats[ts_], mv[ts_], b_st[ts_])
                    P.op("sp", lambda e, ts_=ts_, tt=tt: e.dma_start(out=H1[tt * 128:(tt + 1) * 128, :], in_=h1s[ts_][:]),
                         reads=[b_h1s[ts_]], pw=[b_H1], dma=True)
                    pend6 = (tt, ts_)
            if pend6 is not None:
                p6a_back(*pend6)

        barrier(7)
        with ExitStack() as s7:
            stg = [sb("stgb%d" % i, [128, 1024], F32, s7) for i in range(2)]
            b_stg = [Buf("stgb%d" % i) for i in range(2)]
            WG = sb("WG", [128, 8 * FFN], BF16, s7)
            WU = sb("WU", [128, 8 * FFN], BF16, s7)
            WD = sb("WD", [128, NHC * 1024], BF16, s7)
            b_WG, b_WU, b_WD = Buf("WG"), Buf("WU"), Buf("WD")
            load_w(WG, wg_d, 8 * FFN, stg, b_stg, b_WG, 1024)
            load_w(WU, wu_d, 8 * FFN, stg, b_stg, b_WU, 1024)
            load_w(WD, wd_d, NHC * 1024, stg, b_stg, b_WD, 1024, engs=("pool",))
            gbc = sb("gbc2", [128, D], F32, s7)
            bbc = sb("bbc2", [128, D], F32, s7)
            b_ln2 = Buf("ln2")
            P.op("sp", lambda e, gbc=gbc: e.dma_start(out=gbc[:], in_=ln_d[2:3, :].partition_broadcast(128)), writes=[b_ln2], dma=True)
            P.op("sp", lambda e, bbc=bbc: e.dma_start(out=bbc[:], in_=ln_d[3:4, :].partition_broadcast(128)), writes=[b_ln2], dma=True)
            hch = [sb("hch%d" % i, [128, 8, 512], BF16, s7) for i in range(1)]
            b_hch = [Buf("hch%d" % i) for i in range(1)]
            hid = [sb("hid%d" % i, [128, NHC, 512], BF16, s7) for i in range(1)]
            b_hid = [Buf("hid%d" % i) for i in range(1)]
            sg = [sb("sg%d" % i, [128, 512], F32, s7) for i in range(2)]
            b_sg = [Buf("sg%d" % i) for i in range(2)]
            h1r = [sb("h1r%d" % i, [128, D], F32, s7) for i in range(2)]
            b_h1r = [Buf("h1r%d" % i) for i in range(2)]
            hp7 = [sb("hp7_%d" % i, [128, D], F32, s7) for i in range(2)]
            b_hp7 = [Buf("hp7_%d" % i) for i in range(2)]
            o7, b_o7 = hp7, b_hp7
            stats = [sb("stats7_%d" % i, [128, 12], F32, s7) for i in range(2)]
            mv = [sb("mv7_%d" % i, [128, 4], F32, s7) for i in range(2)]
            b_st = [Buf("st7_%d" % i) for i in range(2)]
            gc_ = 0
            tcnt = 0
            for n in range(NCH):
                cs = 0
                tsl = slice(n * 512, (n + 1) * 512)
                P.op("sp", lambda e, cs=cs, tsl=tsl: e.dma_start(out=hch[cs][:], in_=H1T[:, :, tsl].rearrange("c p t -> p c t")),
                     reads=[b_H1T], writes=[b_hch[cs]], dma=True)
                for hc in range(NHC):
                    bg_ = (gc_ % 2) * 2
                    bu_ = bg_ + 1
                    s1 = gc_ % 2
                    gc_ += 1
                    for kc in range(8):
                        P.op("pe", lambda e, bg_=bg_, kc=kc, hc=hc, cs=cs: e.matmul(
                            banks[bg_][:], lhsT=WG[:, kc * FFN + hc * 128:kc * FFN + hc * 128 + 128], rhs=hch[cs][:, kc, :],
                            start=(kc == 0), stop=(kc == 7)), reads=[b_WG, b_hch[cs]], writes=[bbuf[bg_]])
                    for kc in range(8):
                        P.op("pe", lambda e, bu_=bu_, kc=kc, hc=hc, cs=cs: e.matmul(
                            banks[bu_][:], lhsT=WU[:, kc * FFN + hc * 128:kc * FFN + hc * 128 + 128], rhs=hch[cs][:, kc, :],
                            start=(kc == 0), stop=(kc == 7)), reads=[b_WU, b_hch[cs]], writes=[bbuf[bu_]])
                    P.op("act", lambda e, bg_=bg_, s1=s1: e.activation(out=sg[s1][:], in_=banks[bg_][:], func=AF.Silu),
                         reads=[bbuf[bg_]], writes=[b_sg[s1]])
                    P.op("dve", lambda e, bu_=bu_, s1=s1, hc=hc, cs=cs: e.tensor_tensor(
                        out=hid[cs][:, hc, :], in0=banks[bu_][:], in1=sg[s1][:], op=ALU.mult),
                        reads=[bbuf[bu_], b_sg[s1]], writes=[b_hid[cs]])
                for t4 in range(4):
                    tt = n * 4 + t4
                    ts_ = tcnt % 2
                    tcnt += 1
                    P.op("sp", lambda e, ts_=ts_, tt=tt: e.dma_start(out=h1r[ts_][:], in_=H1[tt * 128:(tt + 1) * 128, :]),
                         reads=[b_H1], writes=[b_h1r[ts_]], dma=True)
                    for hf in range(2):
                        bk = 4 + (tcnt % 2) * 2 + hf
                        for hc in range(NHC):
                            P.op("pe", lambda e, bk=bk, hc=hc, t4=t4, hf=hf, cs=cs: e.matmul(
                                banks[bk][:], lhsT=hid[cs][:, hc, t4 * 128:(t4 + 1) * 128],
                                rhs=WD[:, hc * 1024 + hf * 512:hc * 1024 + hf * 512 + 512], start=(hc == 0), stop=(hc == NHC - 1)),
                                reads=[b_hid[cs], b_WD], writes=[bbuf[bk]])
                        P.op("dve", lambda e, bk=bk, hf=hf, ts_=ts_: e.scalar_tensor_tensor(
                            out=hp7[ts_][:, hf * 512:(hf + 1) * 512], in0=h1r[ts_][:, hf * 512:(hf + 1) * 512], scalar=ALPHA,
                            in1=banks[bk][:], op0=ALU.mult, op1=ALU.add),
                            reads=[b_h1r[ts_], bbuf[bk]], writes=[b_hp7[ts_]])
                    layer_norm(hp7[ts_], b_hp7[ts_], o7[ts_], b_o7[ts_], gbc, bbc, b_ln2, stats[ts_], mv[ts_], b_st[ts_])
                    P.op("sp", lambda e, ts_=ts_, tt=tt: e.dma_start(out=out_d[tt * 128:(tt + 1) * 128, :], in_=o7[ts_][:]),
                         reads=[b_o7[ts_]], writes=[Buf("outd")], dma=True, final=True)

        with ExitStack() as ee:
            block = ee.enter_context(nc.Block())
            P.emit(nc, ee, block)
    return nc


def _consts():
    c = np.zeros((128, NCONST), np.float32)
    idx = np.arange(128)
    c[:, C_ID:C_ID + 128] = np.eye(128, dtype=np.float32)
    perm = np.zeros((128, 128), np.float32)
    for m in range(128):
        mm = m % 64
        if mm < 8:
            perm[m + 8, m] = 1.0
        elif mm < 16:
            perm[m - 8, m] = 1.0
    c[:, C_PERM:C_PERM + 128] = perm
    qq, ss = idx[:, None], idx[None, :]
    c[:, C_TRIM:C_TRIM + 128] = np.where(ss <= qq, 0.0, -BIG)
    c[:, C_TRIP:C_TRIP + 128] = np.where(ss <= qq, 0.0, BIG)
    s_, u_ = idx[:, None], idx[None, :]
    c[:, C_MD:C_MD + 128] = (s_ <= u_).astype(np.float32)
    c[:, C_MP:C_MP + 128] = (s_ >= u_).astype(np.float32)
    inv_freq = (ROPE_THETA ** (-np.arange(0, 16, 2, dtype=np.float32) / 16.0)).astype(np.float32)
    for m in range(128):
        mm = m % 64
        if mm < 8:
            c[m, C_ROPE + 0] = -inv_freq[mm] / (2 * np.pi)
            c[m, C_ROPE + 1] = inv_freq[mm] / (2 * np.pi)
        elif mm < 16:
            c[m, C_ROPE + 0] = inv_freq[mm - 8] / (2 * np.pi)
            c[m, C_ROPE + 1] = inv_freq[mm - 8] / (2 * np.pi)
    c[:, C_ROPE + 2] = 0.25
    for i in range(NIT + 1):
        c[:, C_POW + i] = 2.0 ** (-(i + 1))
    return c


def _prep_weights(w_in, b_gate, w_branch_a, w_branch_b, w_out, ln1_g, ln1_b, w_ffn_gate, w_ffn_up, w_ffn_down, ln2_g, ln2_b):
    w_in = np.asarray(w_in, np.float32)[0]
    cols = []
    for base in (0, 768, 2304, 3072):
        for c in range(6):
            cols.append(np.arange(base + c * 128, base + (c + 1) * 128))
    for c in range(4):
        cols.append(np.arange(4608 + c * 128, 4608 + (c + 1) * 128))
    ki = np.arange(5120, 5184)
    cols.append(np.concatenate([ki, ki]))
    for c in range(16):
        cols.append(np.arange(5192 + c * 128, 5192 + (c + 1) * 128))
    w_fm = np.stack([w_in[:, cc].reshape(8, 128, 128).transpose(1, 0, 2).reshape(128, 1024) for cc in cols], 0)
    tmc = np.concatenate([np.arange(1536, 1920), np.arange(1920, 2304), np.arange(5184, 5192),
                          np.arange(3840, 4224), np.arange(4224, 4608)])
    w_tm = w_in[:, tmc].reshape(8, 128, 1544).transpose(1, 0, 2).reshape(128, 8 * 1544)

    def kmaj(w, nk):
        w = np.asarray(w, np.float32)[0]
        return np.ascontiguousarray(w.reshape(nk, 128, w.shape[1]).transpose(1, 0, 2).reshape(128, nk * w.shape[1]))

    d = {
        "w_fm": np.ascontiguousarray(w_fm),
        "w_tm": np.ascontiguousarray(w_tm),
        "bg": np.ascontiguousarray(np.asarray(b_gate, np.float32)[0].reshape(16, 128).T),
        "wa": kmaj(w_branch_a, 6),
        "wb": kmaj(w_branch_b, 2),
        "wo": kmaj(w_out, 8),
        "wg": kmaj(w_ffn_gate, 8),
        "wu": kmaj(w_ffn_up, 8),
        "wd": kmaj(w_ffn_down, NHC),
        "ln": np.ascontiguousarray(np.stack([np.asarray(a, np.float32)[0] for a in (ln1_g, ln1_b, ln2_g, ln2_b)], 0)),
        "cst": _consts(),
    }
    return d


_NC_CACHE = {}


def run(x, positions, weights, S, ktop):
    B = x.shape[0]
    key = (S, ktop)
    if key not in _NC_CACHE:
        _NC_CACHE[key] = build(S, ktop)
    nc = _NC_CACHE[key]
    wd = _prep_weights(**weights)
    in_maps = []
    for b in range(B):
        m = dict(wd)
        m["x"] = np.ascontiguousarray(np.asarray(x[b], np.float32))
        m["pos"] = np.ascontiguousarray(np.asarray(positions[b], np.int32).reshape(1, S))
        in_maps.append(m)
    res = run_bass_kernel_spmd(nc, in_maps, core_ids=list(range(B)))
    return np.stack([np.asarray(r["out"], np.float32) for r in res.results], 0)


def kernel(x, positions, w_in, b_gate, w_branch_a, w_branch_b, w_out, ln1_g, ln1_b,
           w_ffn_gate, w_ffn_up, w_ffn_down, ln2_g, ln2_b):
    x = np.asarray(x)
    S = x.shape[1]
    weights = dict(w_in=w_in, b_gate=b_gate, w_branch_a=w_branch_a, w_branch_b=w_branch_b, w_out=w_out,
                   ln1_g=ln1_g, ln1_b=ln1_b, w_ffn_gate=w_ffn_gate, w_ffn_up=w_ffn_up, w_ffn_down=w_ffn_down,
                   ln2_g=ln2_g, ln2_b=ln2_b)
    return run(x, np.asarray(positions), weights, S, min(256, S // 4))
```
